# Optimizing a Trainium2 kernel written in Bass

```python
import math
import jax
import jax.numpy as jnp
from jax import lax

D_MODEL = 1024
BATCH = 32
SEQ = 2048
DEPTH = 4

N_MIXERS = 4
HEAD_DIM = 64
BLOCK = 128
RMS_EPS = 1e-6
NUM_BUCKETS = 32
MAX_DISTANCE = 2048
N_BIAS_HEADS = 16
DIL_GROUPS = ((128, 1), (512, 4), (2048, 16))
N_DIL = 3
A_HEADS = 8
MLA_HEADS = 16
MLA_NOPE = 64
MLA_ROPE = 32
MLA_QK = 96
MLA_V = 64
MLA_Q_RANK = 384
MLA_KV_RANK = 256
ROPE_THETA = 10000.0
DIFF_HEADS = 8
SWA_Q_HEADS = 16
SWA_KV_HEADS = 2
SWA_WINDOW = 128
D_FF = 2816
CONV_WIDTH = 3
N_A = (DEPTH + 3) // 4
N_B = (DEPTH + 2) // 4
N_C = (DEPTH + 1) // 4
N_D = DEPTH // 4

kernel_name = "hybrid_interleaved_dilated_mla_diff_swa_convffn"


def rms_norm(x, g):
    xf = x.astype(jnp.float32)
    y = xf * lax.rsqrt(jnp.mean(xf * xf, axis=-1, keepdims=True) + RMS_EPS)
    return (y * g.astype(jnp.float32)).astype(x.dtype)


def t5_bucket(dist):
    max_exact = NUM_BUCKETS // 2
    d_f = jnp.maximum(dist, 1).astype(jnp.float32)
    large = max_exact + (jnp.log(d_f / max_exact) / math.log(MAX_DISTANCE / max_exact)
                         * (NUM_BUCKETS - max_exact)).astype(jnp.int32)
    return jnp.where(dist < max_exact, dist, jnp.minimum(large, NUM_BUCKETS - 1))


def band_bias(table_cols, dilation):
    offset = jnp.arange(BLOCK)[:, None] + BLOCK - jnp.arange(2 * BLOCK)[None, :]
    bucket = t5_bucket(jnp.maximum(offset, 0) * dilation)
    return table_cols.T[:, bucket].astype(jnp.float32)


def banded_attention(q, k, v, bias, window, sinks=None):
    n, L, hq, dh = q.shape
    hk = k.shape[2]
    g = hq // hk
    nb = -(-L // BLOCK)
    pad = nb * BLOCK - L
    padl = lambda a: jnp.pad(a, ((0, 0), (0, pad), (0, 0), (0, 0)))
    q, k, v = padl(q), padl(k), padl(v)
    qb = q.reshape(n, nb, BLOCK, hk, g, dh)

    def pairs(a):
        cur = a.reshape(n, nb, BLOCK, hk, dh)
        prev = jnp.pad(cur, ((0, 0), (1, 0), (0, 0), (0, 0), (0, 0)))[:, :-1]
        return jnp.concatenate([prev, cur], axis=2)

    kb, vb = pairs(k), pairs(v)
    kj = jnp.arange(2 * BLOCK)
    offset = jnp.arange(BLOCK)[:, None] + BLOCK - kj[None, :]
    key_pos = jnp.arange(nb)[:, None, None] * BLOCK + kj[None, None, :] - BLOCK
    mask = (offset >= 0) & (offset <= window) & (key_pos >= 0)
    logits = jnp.einsum('nbqhgd,nbkhd->nbhgqk', qb, kb, preferred_element_type=jnp.float32) * (dh ** -0.5)
    logits = logits + bias.reshape(hk, g, BLOCK, 2 * BLOCK)
    logits = jnp.where(mask[None, :, None, None], logits, -jnp.inf)
    m = jnp.max(logits, axis=-1)
    if sinks is not None:
        sk = sinks.astype(jnp.float32).reshape(hk, g, 1)
        m = jnp.maximum(m, sk)
    p = jnp.exp(logits - m[..., None])
    s = jnp.sum(p, axis=-1)
    if sinks is not None:
        s = s + jnp.exp(sk - m)
    o = jnp.einsum('nbhgqk,nbkhd->nbqhgd', p, vb.astype(jnp.float32))
    o = o / s.transpose(0, 1, 4, 2, 3)[..., None]
    o = o.reshape(n, nb * BLOCK, hq, dh)[:, :L]
    lse = (m + jnp.log(s)).transpose(0, 1, 4, 2, 3).reshape(n, nb * BLOCK, hq)[:, :L]
    return o, lse


def dilated_attention(h, w_in, q_norm, k_norm, w_out, table):
    b, s, _ = h.shape
    qkv = (h @ w_in).reshape(b, s, N_DIL, 3, A_HEADS, HEAD_DIM)
    outs, lses = [], []
    for gi, (window, dil) in enumerate(DIL_GROUPS):
        q = rms_norm(qkv[:, :, gi, 0], q_norm[gi])
        k = rms_norm(qkv[:, :, gi, 1], k_norm[gi])
        v = qkv[:, :, gi, 2]
        to_res = lambda a: a.reshape(b, s // dil, dil, A_HEADS, HEAD_DIM).transpose(0, 2, 1, 3, 4).reshape(
            b * dil, s // dil, A_HEADS, HEAD_DIM)
        o, lse = banded_attention(to_res(q), to_res(k), to_res(v), band_bias(table[:, :A_HEADS], dil), window // dil)
        outs.append(o.reshape(b, dil, s // dil, A_HEADS, HEAD_DIM).transpose(0, 2, 1, 3, 4).reshape(
            b, s, A_HEADS, HEAD_DIM))
        lses.append(lse.reshape(b, dil, s // dil, A_HEADS).transpose(0, 2, 1, 3).reshape(b, s, A_HEADS))
    alpha = jax.nn.softmax(jnp.stack(lses, axis=0), axis=0)
    o = jnp.sum(alpha[..., None] * jnp.stack(outs, axis=0), axis=0)
    return o.reshape(b, s, A_HEADS * HEAD_DIM).astype(h.dtype) @ w_out


def apply_rope(x, s):
    inv_freq = ROPE_THETA ** (-jnp.arange(0, MLA_ROPE, 2, dtype=jnp.float32) / MLA_ROPE)
    ang = jnp.arange(s, dtype=jnp.float32)[:, None] * inv_freq[None, :]
    cos, sin = jnp.cos(ang)[:, None, :], jnp.sin(ang)[:, None, :]
    xf = x.astype(jnp.float32)
    x1, x2 = xf[..., :MLA_ROPE // 2], xf[..., MLA_ROPE // 2:]
    return jnp.concatenate([x1 * cos - x2 * sin, x2 * cos + x1 * sin], axis=-1).astype(x.dtype)


def mla_attention(h, w_in, q_a_norm, kv_a_norm, w_q_up, w_kv_up, q_norm, k_norm, w_out):
    b, s, _ = h.shape
    lat = h @ w_in
    c_q = lat[..., :MLA_Q_RANK]
    c_kv = lat[..., MLA_Q_RANK:MLA_Q_RANK + MLA_KV_RANK]
    k_pe = lat[..., MLA_Q_RANK + MLA_KV_RANK:]
    q = (rms_norm(c_q, q_a_norm) @ w_q_up).reshape(b, s, MLA_HEADS, MLA_QK)
    kv = (rms_norm(c_kv, kv_a_norm) @ w_kv_up).reshape(b, s, MLA_HEADS, MLA_NOPE + MLA_V)
    v = kv[..., MLA_NOPE:]
    k = jnp.concatenate([kv[..., :MLA_NOPE], jnp.broadcast_to(k_pe[:, :, None, :], (b, s, MLA_HEADS, MLA_ROPE))], axis=-1)
    q = rms_norm(q, q_norm)
    k = rms_norm(k, k_norm)
    q = jnp.concatenate([q[..., :MLA_NOPE], apply_rope(q[..., MLA_NOPE:], s)], axis=-1)
    k = jnp.concatenate([k[..., :MLA_NOPE], apply_rope(k[..., MLA_NOPE:], s)], axis=-1)
    nb = s // BLOCK
    qb = q.reshape(b, nb, BLOCK, MLA_HEADS, MLA_QK).transpose(1, 0, 2, 3, 4)
    kpos = jnp.arange(s)
    vf = v.astype(jnp.float32)

    def block(args):
        q_blk, i = args
        qpos = i * BLOCK + jnp.arange(BLOCK)
        logits = jnp.einsum('bqhd,bkhd->bhqk', q_blk, k, preferred_element_type=jnp.float32) * (MLA_QK ** -0.5)
        logits = jnp.where(kpos[None, :] <= qpos[:, None], logits, -jnp.inf)
        p = jax.nn.softmax(logits, axis=-1)
        return jnp.einsum('bhqk,bkhd->bqhd', p, vf)

    o = lax.map(block, (qb, jnp.arange(nb)))
    o = o.transpose(1, 0, 2, 3, 4).reshape(b, s, MLA_HEADS * MLA_V)
    return o.astype(h.dtype) @ w_out


def diff_attention(h, w_in, q_norm, k_norm, lq1, lk1, lq2, lk2, subln, w_out, table, layer_idx):
    b, s, _ = h.shape
    qk_w = DIFF_HEADS * 2 * HEAD_DIM
    proj = h @ w_in
    q = rms_norm(proj[..., :qk_w].reshape(b, s, DIFF_HEADS, 2, HEAD_DIM), q_norm)
    k = rms_norm(proj[..., qk_w:2 * qk_w].reshape(b, s, DIFF_HEADS, 2, HEAD_DIM), k_norm)
    vf = proj[..., 2 * qk_w:].reshape(b, s, DIFF_HEADS, 2 * HEAD_DIM).astype(jnp.float32)
    lam_init = 0.8 - 0.6 * math.exp(-0.3 * layer_idx)
    f32 = lambda a: a.astype(jnp.float32)
    lam = jnp.exp(jnp.sum(f32(lq1) * f32(lk1))) - jnp.exp(jnp.sum(f32(lq2) * f32(lk2))) + lam_init
    tab = table[:, :2 * DIFF_HEADS].T.reshape(2, DIFF_HEADS, NUM_BUCKETS).transpose(1, 0, 2).astype(jnp.float32)
    nb = s // BLOCK
    qb = q.reshape(b, nb, BLOCK, DIFF_HEADS, 2, HEAD_DIM).transpose(1, 0, 2, 3, 4, 5)
    kpos = jnp.arange(s)

    def block(args):
        q_blk, i = args
        qpos = i * BLOCK + jnp.arange(BLOCK)
        dist = qpos[:, None] - kpos[None, :]
        bias = tab[:, :, t5_bucket(jnp.maximum(dist, 0))]
        logits = jnp.einsum('bqhmd,bkhmd->bhmqk', q_blk, k, preferred_element_type=jnp.float32) * (HEAD_DIM ** -0.5)
        logits = jnp.where(dist >= 0, logits + bias, -jnp.inf)
        p = jax.nn.softmax(logits, axis=-1)
        a = p[:, :, 0] - lam * p[:, :, 1]
        return jnp.einsum('bhqk,bkhe->bqhe', a, vf)

    o = lax.map(block, (qb, jnp.arange(nb)))
    o = o.transpose(1, 0, 2, 3, 4).reshape(b, s, DIFF_HEADS, 2 * HEAD_DIM)
    o = rms_norm(o, subln) * (1.0 - lam_init)
    return o.reshape(b, s, DIFF_HEADS * 2 * HEAD_DIM).astype(h.dtype) @ w_out


def swa_sink_attention(h, w_in, q_norm, k_norm, sinks, w_out, table):
    b, s, _ = h.shape
    qw, kw = SWA_Q_HEADS * HEAD_DIM, SWA_KV_HEADS * HEAD_DIM
    proj = h @ w_in
    q = rms_norm(proj[..., :qw].reshape(b, s, SWA_Q_HEADS, HEAD_DIM), q_norm)
    k = rms_norm(proj[..., qw:qw + kw].reshape(b, s, SWA_KV_HEADS, HEAD_DIM), k_norm)
    v = proj[..., qw + kw:].reshape(b, s, SWA_KV_HEADS, HEAD_DIM)
    o, _ = banded_attention(q, k, v, band_bias(table[:, :SWA_Q_HEADS], 1), SWA_WINDOW - 1, sinks)
    return o.reshape(b, s, qw).astype(h.dtype) @ w_out


def conv_ffn(h, w_up, conv_w, conv_b, w_down):
    s = h.shape[1]
    gu = h @ w_up
    gate, up = gu[..., :D_FF], gu[..., D_FF:]
    gp = jnp.pad(gate, ((0, 0), (CONV_WIDTH - 1, 0), (0, 0)))
    conv = conv_b + conv_w[CONV_WIDTH - 1] * gate
    for j in range(CONV_WIDTH - 1):
        conv = conv + conv_w[j] * gp[:, j:j + s]
    return (jax.nn.silu(conv) * up) @ w_down


def setup_inputs(seed: int = 0) -> dict:
    key = jax.random.key(seed)
    ks = iter(jax.random.split(key, 40))
    nrm = lambda shape, scale: jax.random.normal(next(ks), shape, jnp.float32) * scale
    w = lambda shape: nrm(shape, shape[-2] ** -0.5)
    gain = lambda shape: 1.0 + nrm(shape, 0.02)
    return {
        "x": nrm((BATCH, SEQ, D_MODEL), 1.0),
        "rel_bias_table": nrm((NUM_BUCKETS, N_BIAS_HEADS), 0.2),
        "norm_mix": gain((DEPTH, D_MODEL)),
        "norm_ffn": gain((DEPTH, D_MODEL)),
        "a_w_in": w((N_A, D_MODEL, N_DIL * 3 * A_HEADS * HEAD_DIM)),
        "a_q_norm": gain((N_A, N_DIL, HEAD_DIM)),
        "a_k_norm": gain((N_A, N_DIL, HEAD_DIM)),
        "a_w_out": w((N_A, A_HEADS * HEAD_DIM, D_MODEL)),
        "b_w_in": w((N_B, D_MODEL, MLA_Q_RANK + MLA_KV_RANK + MLA_ROPE)),
        "b_q_a_norm": gain((N_B, MLA_Q_RANK)),
        "b_kv_a_norm": gain((N_B, MLA_KV_RANK)),
        "b_w_q_up": w((N_B, MLA_Q_RANK, MLA_HEADS * MLA_QK)),
        "b_w_kv_up": w((N_B, MLA_KV_RANK, MLA_HEADS * (MLA_NOPE + MLA_V))),
        "b_q_norm": gain((N_B, MLA_QK)),
        "b_k_norm": gain((N_B, MLA_QK)),
        "b_w_out": w((N_B, MLA_HEADS * MLA_V, D_MODEL)),
        "c_w_in": w((N_C, D_MODEL, DIFF_HEADS * 2 * HEAD_DIM * 3)),
        "c_q_norm": gain((N_C, HEAD_DIM)),
        "c_k_norm": gain((N_C, HEAD_DIM)),
        "c_lambda_q1": nrm((N_C, HEAD_DIM), 0.1),
        "c_lambda_k1": nrm((N_C, HEAD_DIM), 0.1),
        "c_lambda_q2": nrm((N_C, HEAD_DIM), 0.1),
        "c_lambda_k2": nrm((N_C, HEAD_DIM), 0.1),
        "c_subln": gain((N_C, 2 * HEAD_DIM)),
        "c_w_out": w((N_C, DIFF_HEADS * 2 * HEAD_DIM, D_MODEL)),
        "d_w_in": w((N_D, D_MODEL, (SWA_Q_HEADS + 2 * SWA_KV_HEADS) * HEAD_DIM)),
        "d_q_norm": gain((N_D, HEAD_DIM)),
        "d_k_norm": gain((N_D, HEAD_DIM)),
        "d_sinks": nrm((N_D, SWA_Q_HEADS), 0.5),
        "d_w_out": w((N_D, SWA_Q_HEADS * HEAD_DIM, D_MODEL)),
        "f_w_up": w((DEPTH, D_MODEL, 2 * D_FF)),
        "f_conv_w": nrm((DEPTH, CONV_WIDTH, D_FF), CONV_WIDTH ** -0.5),
        "f_conv_b": nrm((DEPTH, D_FF), 0.01),
        "f_w_down": w((DEPTH, D_FF, D_MODEL)),
    }


def reference(x, rel_bias_table, norm_mix, norm_ffn,
              a_w_in, a_q_norm, a_k_norm, a_w_out,
              b_w_in, b_q_a_norm, b_kv_a_norm, b_w_q_up, b_w_kv_up, b_q_norm, b_k_norm, b_w_out,
              c_w_in, c_q_norm, c_k_norm, c_lambda_q1, c_lambda_k1, c_lambda_q2, c_lambda_k2, c_subln, c_w_out,
              d_w_in, d_q_norm, d_k_norm, d_sinks, d_w_out,
              f_w_up, f_conv_w, f_conv_b, f_w_down):
    for i in range(DEPTH):
        m, j = i % N_MIXERS, i // N_MIXERS
        h = rms_norm(x, norm_mix[i])
        if m == 0:
            y = dilated_attention(h, a_w_in[j], a_q_norm[j], a_k_norm[j], a_w_out[j], rel_bias_table)
        elif m == 1:
            y = mla_attention(h, b_w_in[j], b_q_a_norm[j], b_kv_a_norm[j], b_w_q_up[j], b_w_kv_up[j],
                              b_q_norm[j], b_k_norm[j], b_w_out[j])
        elif m == 2:
            y = diff_attention(h, c_w_in[j], c_q_norm[j], c_k_norm[j], c_lambda_q1[j], c_lambda_k1[j],
                               c_lambda_q2[j], c_lambda_k2[j], c_subln[j], c_w_out[j], rel_bias_table, i)
        else:
            y = swa_sink_attention(h, d_w_in[j], d_q_norm[j], d_k_norm[j], d_sinks[j], d_w_out[j], rel_bias_table)
        x = x + y
        x = x + conv_ffn(rms_norm(x, norm_ffn[i]), f_w_up[i], f_conv_w[i], f_conv_b[i], f_w_down[i])
    return x
```

```python
import math
import numpy as np
from contextlib import ExitStack
import concourse.bass as bass
import concourse.mybir as mybir
from concourse.bass_utils import run_bass_kernel_spmd

F32 = mybir.dt.float32
BF16 = mybir.dt.bfloat16
AF = mybir.ActivationFunctionType
ALU = mybir.AluOpType

NCORES = 8
SEQ = 2048
DM = 1024
DFF = 2816
NJ = DFF // 128
NTB = SEQ // 512
EPS = 1e-6
NEG = -30000.0
HG_D_Q, HG_D_K, HG_D_SINK = 0, 1, 2
HG_A_Q, HG_A_K = 18, 21
HG_B_QA, HG_B_KVA, HG_B_Q, HG_B_K, HG_B_KPE = 24, 27, 29, 30, 31
HG_C_Q, HG_C_K, HG_C_SUB = 32, 33, 34
HG_A_QK = 35
NHG = 64


class Buf:
    __slots__ = ("name", "w", "r", "excl")

    def __init__(self, name, excl=False):
        self.name = name
        self.w = None
        self.r = []
        self.excl = excl


ATTACH_WAITS = True


class _FirstIns:
    def __init__(self, eng):
        self._eng = eng
        self.first = None

    def __getattr__(self, name):
        f = getattr(self._eng, name)

        def w(*a, **k):
            r = f(*a, **k)
            if self.first is None:
                self.first = r
            return r
        return w


class Sync:
    ENG = ("pe", "act", "dve", "pool", "sp")

    def __init__(self, nc, es):
        self.nc = nc
        self.es = es
        self.eng = {"pe": nc.tensor, "act": nc.scalar, "dve": nc.vector, "pool": nc.gpsimd, "sp": nc.sync}
        self.sems = {}
        self.cnt = {}
        self.seen = {e: {} for e in self.ENG}
        for e in self.ENG:
            self.sems[e] = es.enter_context(nc.semaphore("s_" + e))
            self.cnt[e] = 0
        self.nwaits = 0
        self.nops = 0

    def dma_sem(self, name):
        key = "d_" + name
        self.sems[key] = self.es.enter_context(self.nc.semaphore(key))
        self.cnt[key] = 0
        return key

    def _deps(self, e, reads, writes, relaxed=False):
        deps = {}

        def add(ev, same_ok):
            if ev is None:
                return
            k, v = ev
            if k == e and not same_ok:
                return
            if deps.get(k, 0) < v:
                deps[k] = v

        for b in reads:
            add(b.w, True)
            if b.excl:
                for ev in b.r:
                    add(ev, False)
        for b in writes:
            add(b.w, not relaxed)
            for ev in b.r:
                add(ev, not relaxed)
        return deps

    def _wait(self, e, deps):
        eng = self.eng[e]
        seen = self.seen[e]
        for k, v in deps.items():
            if seen.get(k, 0) < v:
                eng.wait_ge(self.sems[k], v)
                seen[k] = v
                self.nwaits += 1

    def _record(self, ev, reads, writes):
        for b in reads:
            b.r.append(ev)
            if len(b.r) > 64:
                best = {}
                for k, v in b.r:
                    if best.get(k, 0) < v:
                        best[k] = v
                b.r = list(best.items())
        for b in writes:
            b.w = ev
            b.r = []

    def op(self, e, reads, writes, fn, relaxed=False):
        deps = self._deps(e, reads, writes, relaxed)
        seen = self.seen[e]
        pend = [(k, v) for k, v in deps.items() if seen.get(k, 0) < v]
        attach = pend.pop() if (pend and ATTACH_WAITS) else None
        self._wait(e, dict(pend))
        cap = _FirstIns(self.eng[e])
        ins = fn(cap)
        if attach is not None:
            cap.first._wait_ge(self.sems[attach[0]], attach[1])
            seen[attach[0]] = attach[1]
        ins.then_inc(self.sems[e], 1)
        self.cnt[e] += 1
        self.nops += 1
        ev = (e, self.cnt[e])
        self._record(ev, reads, writes)
        return ev

    def dma(self, q, out, in_, reads, writes, key):
        self._wait(q, self._deps(q, reads, writes))
        self.eng[q].dma_start(out=out, in_=in_).then_inc(self.sems[key], 16)
        self.cnt[key] += 16
        ev = (key, self.cnt[key])
        self._record(ev, reads, writes)
        return ev

    def wait_all(self, e, bufs):
        self._wait(e, self._deps(e, bufs, bufs))


def fview(ap, shape):
    if len(shape) == 1:
        return ap
    if len(shape) == 2:
        return ap.rearrange("p (a b) -> p a b", a=shape[0])
    return ap.rearrange("p (a b c) -> p a b c", a=shape[0], b=shape[1])


class WRing:
    SLOT = 2048

    def __init__(self, kb, nslots):
        self.kb = kb
        self.ns = nslots
        self.t = kb.sb("wring", [128, nslots * self.SLOT], BF16)
        self.bufs = [Buf("wslot%d" % i) for i in range(nslots)]
        self.keys = [kb.T.dma_sem("w%d" % i) for i in range(nslots)]
        self.plan = []
        self.issued = 0
        self.consumed = 0
        self.released = 0

    def add(self, tag, dram_ap, shape):
        n = int(np.prod(shape))
        assert n <= self.SLOT, (tag, shape)
        self.plan.append((tag, dram_ap, tuple(shape), n))

    def _view(self, i):
        s = i % self.ns
        _, _, shape, n = self.plan[i]
        return fview(self.t[:, s * self.SLOT: s * self.SLOT + n], shape)

    def pump(self):
        T = self.kb.T
        while self.issued < len(self.plan) and self.issued - self.ns < self.released:
            i = self.issued
            s = i % self.ns
            T.dma("pool", self._view(i), self.plan[i][1], [], [self.bufs[s]], self.keys[s])
            self.issued += 1

    def get(self, tag):
        i = self.consumed
        assert self.plan[i][0] == tag, (self.plan[i][0], tag)
        self.pump()
        assert self.issued > i, "weight ring: too many tiles held"
        self.consumed += 1
        return self._view(i), self.bufs[i % self.ns]

    def done(self, n=1):
        self.released += n
        assert self.released <= self.consumed
        self.pump()


class KB:
    def __init__(self, nc, es):
        self.nc = nc
        self.es = es
        self.T = Sync(nc, es)

    def sb(self, name, shape, dt, es=None):
        self.uid = getattr(self, "uid", 0) + 1
        return (es or self.es).enter_context(self.nc.sbuf_tensor("s%d_%s" % (self.uid, name), list(shape), dt))

    def ps(self, name):
        return self.es.enter_context(self.nc.psum_tensor(name, [128, 512], F32))

    def din(self, name, shape, dt=F32):
        self.in_names = getattr(self, "in_names", []) + [name]
        return self.nc.dram_tensor(name, list(shape), dt, kind="ExternalInput").ap()

    def dout(self, name, shape, dt=F32):
        return self.nc.dram_tensor(name, list(shape), dt, kind="ExternalOutput").ap()


class Prog:
    def __init__(self, nseq, layers=(0, 1, 2, 3), mixers=True, ffns=True):
        self.nseq = nseq
        self.layers = tuple(layers)
        self.mixers = mixers
        self.ffns = ffns

    def build(self):
        nc = bass.Bass("TRN2", target_bir_lowering=False)
        self.nc = nc
        with ExitStack() as es:
            kb = KB(nc, es)
            self.kb = kb
            self.T = kb.T
            self._declare_io()
            self._alloc()
            self._plan_weights()
            self._load_consts()
            for s in range(self.nseq):
                self._run_seq(s)
            self.T.wait_all("sp", self.Xb)
        return nc

    def _declare_io(self):
        kb = self.kb
        ns = self.nseq
        self.d_x = kb.din("xT", [ns, 8, 128, SEQ])
        self.d_y = kb.dout("yT", [ns, 8, 128, SEQ])
        self.d_wup = kb.din("wup", [4, NJ, 128, 2048])
        self.d_wdn = kb.din("wdn", [4, 2, 8, 128, 11 * 128])
        self.d_gains = kb.din("gains", [128, 64])
        self.d_convp = kb.din("convp", [128, 4 * 4 * NJ])
        self.d_hg = kb.din("hgains", [128, NHG])
        if 0 in self.layers and self.mixers:
            self.d_av = kb.din("a_v", [2, 3, 128, 2048])
            self.d_aqk = kb.din("a_qk", [8, 3, 128, 1024])
            self.d_ao = kb.din("a_o", [8, 128, 512])
            self.d_g0 = kb.din("a_g0", [3, 8, 128, 256])
        if 1 in self.layers and self.mixers:
            self.d_bin = kb.din("b_in", [3, 128, 2048])
            self.d_bk = kb.din("b_kup", [128, 2048])
            self.d_bv = kb.din("b_vup", [128, 2048])
            self.d_bq = kb.din("b_qup", [4, 128, 1536])
            self.d_bo = kb.din("b_o", [8, 128, 1024])
            self.d_rope = kb.din("b_rope", [128, SEQ])
            self.d_tri = kb.din("b_tri", [128, 128])
        if 2 in self.layers and self.mixers:
            self.d_cqk = kb.din("c_qk", [8, 128, 2048])
            self.d_cv = kb.din("c_v", [8, 128, 1024])
            self.d_co = kb.din("c_o", [8, 128, 1024])
            self.d_cg = kb.din("c_g", [16, 128, SEQ])
            self.d_clam = kb.din("c_lam", [128, 256])
        if 3 in self.layers and self.mixers:
            self.d_dk = kb.din("d_k", [128, 2048])
            self.d_dv = kb.din("d_v", [128, 1024])
            self.d_dq = kb.din("d_q", [4, 128, 2048])
            self.d_do = kb.din("d_o", [8, 128, 1024])
            self.d_g3 = kb.din("d_g3", [16, 128, 256])

    def _alloc(self):
        kb = self.kb
        T = self.T
        self.X = kb.sb("X", [128, 8, SEQ], F32)
        self.Xb = [Buf("X%d" % c) for c in range(8)]
        self.Xk = [T.dma_sem("x%d" % c) for c in range(8)]
        self.h = kb.sb("h", [128, 8, SEQ], BF16)
        self.hb = [Buf("h%d" % tb) for tb in range(NTB)]
        self.W = WRing(kb, 5)
        self.gains = kb.sb("gains", [128, 64], F32)
        self.convp = kb.sb("convp", [128, 4 * 4 * NJ], F32)
        self.cb = Buf("consts")
        self.ck = T.dma_sem("consts")
        self.hg = kb.sb("hgains", [128, NHG], F32)
        self.g3k = [T.dma_sem("g3_%d" % i) for i in range(2)]
        self.xk = {n: T.dma_sem(n) for n in ("lam", "rope", "tri")}
        self.ones = kb.sb("ones", [128, 128], BF16)
        self.epsc = kb.sb("epsc", [128, 1], F32)
        self.zeros = kb.sb("zeros", [128, 512], BF16)
        self.bd64 = kb.sb("bd64", [128, 128], BF16)
        self.bank = [kb.ps("bank%d" % i) for i in range(8)]
        self.bankb = [Buf("bank%d" % i, excl=True) for i in range(8)]

    def _plan_weights(self):
        for s in range(self.nseq):
            for l in self.layers:
                if self.mixers:
                    self._plan_mixer(l)
                if self.ffns:
                    self._plan_ffn(l)

    def _load_consts(self):
        T = self.T
        T.dma("sp", self.gains[:], self.d_gains, [], [self.cb], self.ck)
        T.dma("sp", self.convp[:], self.d_convp, [], [self.cb], self.ck)
        T.dma("sp", self.hg[:], self.d_hg, [], [self.cb], self.ck)
        T.op("dve", [], [self.cb], lambda e: e.memset(self.ones[:], 1.0))
        T.op("dve", [], [self.cb], lambda e: e.memset(self.epsc[:], EPS))
        T.op("dve", [], [self.cb], lambda e: e.memset(self.zeros[:], 0.0))
        T.op("dve", [], [self.cb], lambda e: e.memset(self.bd64[:], 0.0))
        T.op("dve", [], [self.cb], lambda e: e.memset(self.bd64[0:64, 0:64], 1.0))
        T.op("dve", [], [self.cb], lambda e: e.memset(self.bd64[64:128, 64:128], 1.0))

    def barrier(self):
        T = self.T
        comp = ("pe", "act", "dve")
        for e in comp + ("sp",):
            T._wait(e, {k: T.cnt[k] for k in comp if T.cnt[k] > 0})

    def _run_seq(self, s):
        T = self.T
        if s == 0:
            for c in range(8):
                T.dma("sp", self.X[:, c, :], self.d_x[s, c], [], [self.Xb[c]], self.Xk[c])
        stored = set()

        def x_final(c):
            T.dma("sp", self.d_y[s, c], self.X[:, c, :], [self.Xb[c]], [], self.Xk[c])
            if s + 1 < self.nseq:
                T.dma("sp", self.X[:, c, :], self.d_x[s + 1, c], [], [self.Xb[c]], self.Xk[c])
            stored.add(c)
        for li, l in enumerate(self.layers):
            last = (li == len(self.layers) - 1)
            if self.mixers:
                self._mixer(l)
                self.barrier()
            if self.ffns:
                self._ffn(l, x_final if last else None)
                self.barrier()
        for c in range(8):
            if c not in stored:
                x_final(c)

    def _norm(self, gcol):
        T = self.T
        NBS = (7, 6)
        es_ = ExitStack()
        sq = [self.kb.sb("sq%d" % i, [128, 8, 512], BF16, es_) for i in range(2)]
        sqb = [Buf("sq%d" % i) for i in range(2)]
        rstd = [self.kb.sb("rstd%d" % i, [128, 512], F32, es_) for i in range(2)]
        rstdb = [Buf("rstd%d" % i) for i in range(2)]

        def A(tb):
            NB = NBS[tb % 2]
            ts = slice(tb * 512, (tb + 1) * 512)
            for c in range(8):
                T.op("act", [self.Xb[c]], [sqb[tb % 2]],
                     lambda e: e.activation(sq[tb % 2][:, c, :], self.X[:, c, ts], AF.Square), relaxed=(c > 0))
            self.mm_acc(self.bank[NB][:], [(self.ones[:], sq[tb % 2][:, c, :]) for c in range(8)],
                        [sqb[tb % 2], self.cb], self.bankb[NB])

        def B(tb):
            NB = NBS[tb % 2]
            ts = slice(tb * 512, (tb + 1) * 512)
            r, rb = rstd[tb % 2], rstdb[tb % 2]
            T.op("act", [self.bankb[NB], self.cb], [rb],
                 lambda e: e.activation(r[:], self.bank[NB][:], AF.Ln, bias=self.epsc[:, 0:1], scale=1.0 / DM))
            T.op("act", [rb], [rb], lambda e: e.activation(r[:], r[:], AF.Exp, scale=-0.5))
            for c in range(8):
                T.op("dve", [self.Xb[c], rb, self.cb], [self.hb[tb]],
                     lambda e: e.scalar_tensor_tensor(self.h[:, c, ts], self.X[:, c, ts],
                                                      self.gains[:, gcol + c: gcol + c + 1], r[:], ALU.mult, ALU.mult),
                     relaxed=(c > 0))
        self.pipe(NTB, A, B, 1)
        self.barrier()
        es_.close()

    def _plan_ffn(self, l):
        W = self.W
        for g in range(2):
            for jj in range(11):
                j = 11 * g + jj
                W.add("up%d_%d" % (l, j), self.d_wup[l, j], (8, 2, 128))
            for o in range(8):
                W.add("dn%d_%d_%d" % (l, g, o), self.d_wdn[l, g, o], (11, 128))

    def _ffn(self, l, x_final=None):
        T = self.T
        kb = self.kb
        self._norm(32 + l * 8)
        DBG = ""
        if DBG == "norm":
            return
        with ExitStack() as es:
            sb = lambda n, shp, dt: kb.sb(n, shp, dt, es)
            A = sb("A", [128, 11, SEQ], BF16)
            Ab = [[Buf("A%d_%d" % (j, tb)) for tb in range(NTB)] for j in range(11)]
            Gs = [sb("Gs%d" % i, [128, 2 + SEQ], F32) for i in range(2)]
            Gsb = [[Buf("Gs%d_%d" % (i, tb)) for tb in range(NTB)] for i in range(2)]
            C = [sb("C%d" % i, [128, 512], F32) for i in range(2)]
            Cb = [Buf("C%d" % i) for i in range(2)]
            S = [sb("S%d" % i, [128, 512], F32) for i in range(2)]
            Sb = [Buf("S%d" % i) for i in range(2)]
            for i in range(2):
                T.op("dve", [], [Gsb[i][0]], lambda e: e.memset(Gs[i][:, 0:2], 0.0))
            cp = lambda k, j: self.convp[:, (l * 4 + k) * NJ + j: (l * 4 + k) * NJ + j + 1]
            GB, UB, DB = (0, 1), (2, 3), (4, 5)
            it = 0
            dn = 0
            for g in range(2):
                for jj in range(11):
                    j = 11 * g + jj
                    wt, wb = self.W.get("up%d_%d" % (l, j))
                    gs = Gs[j % 2]
                    gsb = Gsb[j % 2]
                    for tb in range(NTB):
                        if DBG == "nogu":
                            break
                        ts = slice(tb * 512, (tb + 1) * 512)
                        bg, bu = GB[it % 2], UB[it % 2]
                        Cc, Ccb = C[it % 2], Cb[it % 2]
                        Ss, Ssb = S[it % 2], Sb[it % 2]
                        it += 1

                        def mmg(e, bnk, gi):
                            ins = None
                            for c in range(8):
                                ins = e.matmul(self.bank[bnk][:], wt[:, c, gi, :], self.h[:, c, ts],
                                               start=(c == 0), stop=(c == 7))
                            return ins
                        T.op("pe", [wb, self.hb[tb]], [self.bankb[bg]], lambda e: mmg(e, bg, 0))
                        T.op("pe", [wb, self.hb[tb]], [self.bankb[bu]], lambda e: mmg(e, bu, 1))
                        if "noact1" not in DBG:
                          T.op("act", [self.bankb[bg]], [gsb[tb]],
                             lambda e: e.copy(gs[:, 2 + tb * 512: 2 + (tb + 1) * 512], self.bank[bg][:]))
                        if "nots" not in DBG:
                          T.op("dve", [gsb[tb], self.cb], [Ccb],
                             lambda e: e.tensor_scalar(Cc[:], gs[:, 2 + tb * 512: 2 + (tb + 1) * 512], cp(2, j), cp(3, j),
                                                       ALU.mult, ALU.add))
                        rd = [gsb[tb], gsb[max(tb - 1, 0)]]
                        if "notaps" not in DBG:
                          T.op("dve", rd + [Ccb, self.cb], [Ccb],
                             lambda e: e.scalar_tensor_tensor(Cc[:], gs[:, 1 + tb * 512: 1 + (tb + 1) * 512], cp(1, j),
                                                              Cc[:], ALU.mult, ALU.add))
                          T.op("dve", rd + [Ccb, self.cb], [Ccb],
                             lambda e: e.scalar_tensor_tensor(Cc[:], gs[:, tb * 512: (tb + 1) * 512], cp(0, j),
                                                              Cc[:], ALU.mult, ALU.add))
                        if "nosilu" not in DBG:
                          T.op("act", [Ccb], [Ssb], lambda e: e.activation(Ss[:], Cc[:], AF.Silu))
                        if "nomul" not in DBG:
                          T.op("dve", [Ssb, self.bankb[bu]], [Ab[jj][tb]],
                             lambda e: e.tensor_tensor(A[:, jj, ts], Ss[:], self.bank[bu][:], ALU.mult))
                    self.W.done()
                for o in range(8):
                    wt, wb = self.W.get("dn%d_%d_%d" % (l, g, o))
                    for tb in range(NTB):
                        if "nodn" in DBG or "nogu" in DBG:
                            break
                        ts = slice(tb * 512, (tb + 1) * 512)
                        bd = DB[dn % 2]
                        dn += 1

                        def mmd(e):
                            ins = None
                            for jj in range(11):
                                ins = e.matmul(self.bank[bd][:], wt[:, jj, :], A[:, jj, ts],
                                               start=(jj == 0), stop=(jj == 10))
                            return ins
                        T.op("pe", [wb] + [Ab[jj][tb] for jj in range(11)], [self.bankb[bd]], mmd)
                        T.op("dve", [self.bankb[bd], self.Xb[o]], [self.Xb[o]],
                             lambda e: e.tensor_tensor(self.X[:, o, ts], self.X[:, o, ts], self.bank[bd][:], ALU.add))
                    self.W.done()
                    if g == 1 and x_final is not None:
                        x_final(o)
            self.barrier()

    def mm_acc(self, out_ap, pairs, reads, wbuf):
        def f(e):
            ins = None
            n = len(pairs)
            for i, (lt, rh) in enumerate(pairs):
                ins = e.matmul(out_ap, lt, rh, start=(i == 0), stop=(i == n - 1))
            return ins
        return self.T.op("pe", reads, [wbuf], f)

    @staticmethod
    def pipe(n, A, B, la, delay=2):
        pend = []
        for i in range(n + la):
            if i < n:
                A(i)
            if i >= la:
                fin = B(i - la)
                while pend and pend[0][0] <= i:
                    pend.pop(0)[1]()
                if fin is not None:
                    pend.append((i + delay, fin))
        for _, f in pend:
            f()

    def job_head(self, pairs, reads, rows, dsum, gain_ap, out_ap, out_buf, dil=1):
        T = self.T
        pv = (lambda a: a) if dil == 1 else (lambda a: a.rearrange("p (i r) -> p i r", r=dil))

        def post(b, r, rb):
            T.op("dve", [self.bankb[b], rb, self.cb], [out_buf],
                 lambda e: e.scalar_tensor_tensor(out_ap, pv(self.bank[b][0:rows, :]), gain_ap, pv(r[0:rows, :]),
                                                  ALU.mult, ALU.mult))
        return {"mm": lambda b: self.mm_acc(self.bank[b][0:rows, :], pairs, reads, self.bankb[b]),
                "sq_rows": dsum, "rrows": rows, "nd": dsum,
                "sums": lambda sq: [(self.ones[0:dsum, 0:rows], sq[0:dsum, :])], "post": post}

    def norm_pipeline(self, jobs, pbanks, sumb, hn):
        T = self.T
        n, nb = len(jobs), len(pbanks)

        def A(i):
            j, b = jobs[i], pbanks[i % nb]
            j["mm"](b)
            sq, sqb = hn["sqh"][i % 3], hn["sqhb"][i % 3]
            sr = j["sq_rows"]
            T.op("act", [self.bankb[b]], [sqb], lambda e: e.activation(sq[0:sr, :], self.bank[b][0:sr, :], AF.Square))

        def B(i):
            j, b = jobs[i], pbanks[i % nb]
            sq, sqb = hn["sqh"][i % 3], hn["sqhb"][i % 3]
            r, rb = hn["r"][i % 2], hn["rb"][i % 2]
            rr = j["rrows"]
            self.mm_acc(self.bank[sumb][0:rr, :], j["sums"](sq), [sqb, self.cb] + j.get("sum_reads", []), self.bankb[sumb])
            T.op("act", [self.bankb[sumb], self.cb], [rb],
                 lambda e: e.activation(r[0:rr, :], self.bank[sumb][0:rr, :], AF.Ln, bias=self.epsc[0:rr, 0:1], scale=1.0 / j["nd"]))
            T.op("act", [rb], [rb], lambda e: e.activation(r[0:rr, :], r[0:rr, :], AF.Exp, scale=-0.5))
            j["post"](b, r, rb)
        self.pipe(n, A, B, 2 if nb >= 4 else 1)

    def recip(self, out_ap, in_ap, reads, buf, bias=None):
        T = self.T
        if bias is None:
            T.op("act", reads, [buf], lambda e: e.activation(out_ap, in_ap, AF.Ln))
        else:
            T.op("act", reads, [buf], lambda e: e.activation(out_ap, in_ap, AF.Ln, bias=bias, scale=1.0))
        T.op("act", [buf], [buf], lambda e: e.activation(out_ap, out_ap, AF.Exp, scale=-1.0))

    def alloc_hn(self, es):
        kb = self.kb
        return {"sqh": [kb.sb("sqh%d" % i, [128, 512], BF16, es) for i in range(3)], "sqhb": [Buf("sqh%d" % i) for i in range(3)],
                "r": [kb.sb("hr%d" % i, [128, 512], F32, es) for i in range(2)], "rb": [Buf("hr%d" % i) for i in range(2)]}

    def out_proj(self, tagfmt, AO, AOb, nK):
        T = self.T
        it = 0
        for q in range(4):
            wt, wb = self.W.get(tagfmt % q)
            for i in range(2):
                o = 2 * q + i
                for tb in range(NTB):
                    ts = slice(tb * 512, (tb + 1) * 512)
                    bd = it % 2
                    it += 1
                    self.mm_acc(self.bank[bd][:], [(wt[:, i, a, :], AO[:, a, ts]) for a in range(nK)],
                                [wb] + [AOb[a][tb] for a in range(nK)], self.bankb[bd])
                    T.op("dve", [self.bankb[bd], self.Xb[o]], [self.Xb[o]],
                         lambda e: e.tensor_tensor(self.X[:, o, ts], self.X[:, o, ts], self.bank[bd][:], ALU.add))
            self.W.done()

    def _plan_mixer(self, l):
        getattr(self, "_plan_mix%d" % l)()

    def _mixer(self, l):
        self._norm(l * 8)
        getattr(self, "_mix%d" % l)()


    def _plan_mix0(self):
        W = self.W
        for hq in range(2):
            for g in range(3):
                W.add("a_v%d_%d" % (hq, g), self.d_av[hq, g], (8, 256))
            for hd in range(hq * 4, hq * 4 + 4):
                for g in range(3):
                    W.add("a_qk%d_%d" % (hd, g), self.d_aqk[hd, g], (8, 128))
        for q in range(4):
            W.add("a_o%d" % q, self.d_ao[2 * q:2 * q + 2].rearrange("t p k -> p t k"), (2, 4, 128))

    def _mix0(self):
        T = self.T
        kb = self.kb
        HG = self.hg
        DIL = (1, 4, 16)
        with ExitStack() as es:
            sb = lambda n, shp, dt: kb.sb(n, shp, dt, es)
            AO = sb("AO", [128, 4, SEQ], BF16)
            AOb = [[Buf("AO%d_%d" % (a, tb)) for tb in range(NTB)] for a in range(4)]
            VA = [sb("VA%d" % i, [128, 16, 384], BF16) for i in range(3)]
            VAb = [[Buf("VA%d_%d" % (i, q)) for q in range(8)] for i in range(3)]
            QT = [sb("QT%d" % i, [128, SEQ], BF16) for i in range(2)]
            QTb = [Buf("QT%d" % i) for i in range(2)]
            KT = [sb("KT%d" % i, [128, SEQ], BF16) for i in range(2)]
            KTb = [Buf("KT%d" % i) for i in range(2)]
            G3 = [sb("G0_%d" % i, [128, 256], F32) for i in range(2)]
            G3b = [Buf("G0_%d" % i) for i in range(2)]
            Sb_ = [sb("Sb%d" % i, [128, 256], F32) for i in range(3)]
            Sbb = [Buf("Sb%d" % i) for i in range(3)]
            PT = [sb("PT%d" % i, [128, 256], BF16) for i in range(3)]
            PTb = [Buf("PT%d" % i) for i in range(3)]
            R = [sb("R0", [128, 512], F32)] * 2
            Rb = [Buf("R0")] * 2
            KS = [sb("KS%d" % i, [128, 512], BF16) for i in range(2)]
            KSb = [Buf("KS%d" % i) for i in range(2)]
            hn = self.alloc_hn(es)
            PB, PBS, SUMB, STB, ACB = 0, (0, 2, 3), 1, (2, 3, 0, 1), (4, 5, 6, 7)
            for i in range(2):
                T.op("dve", [], [QTb[i]], lambda e: e.memset(QT[i][64:128, :], 0.0))
                T.op("dve", [], [KTb[i]], lambda e: e.memset(KT[i][64:128, :], 0.0))
            for i in range(3):
                T.op("dve", [], VAb[i], lambda e: e.memset(
                    VA[i][:].rearrange("p k (pr x) -> p k pr x", pr=2)[:, :, :, 64:128], 1.0))
            si = 0
            gi = 0
            for hq in range(2):
                for g in range(3):
                    dil = DIL[g]
                    L = SEQ // dil
                    wv, wvb = self.W.get("a_v%d_%d" % (hq, g))
                    for k2 in range(8):
                        def mmv(e):
                            ins = None
                            for i in range(2):
                                kc = 2 * k2 + i
                                n0 = kc * 128
                                r, i0 = n0 // L, n0 % L
                                t0 = i0 * dil + r
                                for c in range(8):
                                    ins = e.matmul(self.bank[PB][:, i * 256:(i + 1) * 256],
                                                   self.h[:, c, t0:min(t0 + 128 * dil, SEQ):dil], wv[:, c, :],
                                                   start=(c == 0), stop=(c == 7))
                            return ins
                        T.op("pe", [wvb] + self.hb, [self.bankb[PB]], mmv)
                        bv = self.bank[PB][:].rearrange("p (i h d) -> p i h d", i=2, h=4)
                        vv = VA[g][:, 2 * k2:2 * k2 + 2, :].rearrange("p i (pr x) -> p i pr x", pr=2)
                        for par in range(2):
                            T.op("act", [self.bankb[PB]], [VAb[g][k2]],
                                 lambda e: e.copy(vv[:, :, :, par * 128:par * 128 + 64], bv[:, :, par:4:2, :]))
                    self.W.done()
                for hl in range(4):
                    hd = hq * 4 + hl
                    urow = slice(0, 64) if hd % 2 == 0 else slice(64, 128)
                    drow = slice(64, 128) if hd % 2 == 0 else slice(0, 64)
                    for b in ACB:
                        T.op("pe", [self.cb], [self.bankb[b]],
                             lambda e: e.matmul(self.bank[b][:], self.zeros[:, 0:128], self.zeros[:], start=True, stop=False))
                    for g in range(3):
                        dil = DIL[g]
                        L = SEQ // dil
                        wqk, wqkb = self.W.get("a_qk%d_%d" % (hd, g))
                        qt, qtb = QT[gi % 2], QTb[gi % 2]
                        kt, ktb = KT[gi % 2], KTb[gi % 2]
                        g3, g3b = G3[gi % 2], G3b[gi % 2]
                        T.dma("sp", g3[:], self.d_g0[g, hd], [], [g3b], self.g3k[gi % 2])
                        gi += 1
                        jobs = []
                        qv = qt[0:64, :].rearrange("p (r i) -> p i r", r=dil) if dil > 1 else None
                        kv = kt[0:64, :].rearrange("p (r i) -> p i r", r=dil) if dil > 1 else None
                        pv = (lambda a: a) if dil == 1 else (lambda a: a.rearrange("p (i r) -> p i r", r=dil))
                        gc = HG_A_QK + g
                        for tb in range(NTB):
                            ts = slice(tb * 512, (tb + 1) * 512)
                            isl = slice(tb * 512 // dil, (tb + 1) * 512 // dil)

                            def post(b, r, rb, tb=tb, ts=ts, isl=isl):
                                T.op("dve", [self.bankb[b], rb, self.cb], [qtb],
                                     lambda e: e.scalar_tensor_tensor(qt[0:64, ts] if dil == 1 else qv[:, isl, :], pv(self.bank[b][0:64, :]),
                                                                      HG[0:64, gc:gc + 1], pv(r[0:64, :]), ALU.mult, ALU.mult), relaxed=(tb > 0))
                                kb_, kbb = KS[tb % 2], KSb[tb % 2]
                                T.op("dve", [self.bankb[b], rb, self.cb], [kbb],
                                     lambda e: e.scalar_tensor_tensor(kb_[64:128, :], self.bank[b][64:128, :], HG[64:128, gc:gc + 1],
                                                                      r[64:128, :], ALU.mult, ALU.mult))
                                T.op("act", [kbb], [ktb],
                                     lambda e: e.copy(kt[0:64, ts] if dil == 1 else kv[:, isl, :], pv(kb_[64:128, :])), relaxed=(tb > 0))
                            jobs.append({"mm": (lambda b, ts=ts, tb=tb: self.mm_acc(self.bank[b][:],
                                                                                 [(wqk[:, c, :], self.h[:, c, ts]) for c in range(8)],
                                                                                 [wqkb, self.hb[tb]], self.bankb[b])),
                                         "sq_rows": 128, "rrows": 128, "nd": 64, "sums": (lambda sq: [(self.bd64[:], sq[:])]), "post": post})
                        self.norm_pipeline(jobs, PBS, SUMB, hn)
                        self.W.done()
                        cpc = L // 128
                        base = si
                        si += 16

                        def A(kc):
                            n0 = kc * 128
                            ci = kc % cpc
                            N = 256 if ci < cpc - 1 else 128
                            bs = STB[(base + kc) % len(STB)]
                            self.mm_acc(self.bank[bs][:, 0:N], [(kt[:, n0:n0 + 128], qt[:, n0:n0 + N])],
                                        [ktb, qtb], self.bankb[bs])

                        def B(kc):
                            r, ci = kc // cpc, kc % cpc
                            N = 256 if ci < cpc - 1 else 128
                            bs = STB[(base + kc) % len(STB)]
                            sb_i = (base + kc) % 3
                            T.op("dve", [self.bankb[bs], g3b], [Sbb[sb_i]],
                                 lambda e: e.scalar_tensor_tensor(Sb_[sb_i][:, 0:N], self.bank[bs][:, 0:N], 0.125, g3[:, 0:N],
                                                                  ALU.mult, ALU.add))
                            T.op("act", [Sbb[sb_i]], [PTb[sb_i]],
                                 lambda e: e.activation(PT[sb_i][:, 0:N], Sb_[sb_i][:, 0:N], AF.Exp))
                            lhs = VA[g][:, kc, (hl // 2) * 192 + (hl % 2) * 64:(hl // 2) * 192 + (hl % 2) * 64 + 128]
                            last = (g == 2 and kc == 15)
                            for half in range(N // 128):
                                ib = ci + half
                                if dil == 1:
                                    dsts = [(ACB[ib // 4], slice((ib % 4) * 128, (ib % 4) * 128 + 128), slice(half * 128, half * 128 + 128))]
                                elif dil == 4:
                                    dsts = [(ACB[ib], slice(r, 512, 4), slice(half * 128, half * 128 + 128))]
                                else:
                                    dsts = [(ACB[b4], slice(r, 512, 16), slice(b4 * 32, b4 * 32 + 32)) for b4 in range(4)]
                                for (bk, osl, psl) in dsts:
                                    T.op("pe", [VAb[g][kc // 2], PTb[sb_i]], [self.bankb[bk]],
                                         lambda e: e.matmul(self.bank[bk][:, osl], lhs, PT[sb_i][:, psl], start=False, stop=last))
                        self.pipe(16, A, B, 3)
                    for m in range(4):
                        ts = slice(m * 512, (m + 1) * 512)
                        bk = ACB[m]
                        r_, rb = R[m % 2], Rb[m % 2]
                        self.recip(r_[drow, :], self.bank[bk][drow, :], [self.bankb[bk]], rb)
                        T.op("dve", [self.bankb[bk], rb], [AOb[hd // 2][m]],
                             lambda e: e.tensor_tensor(AO[urow, hd // 2, ts], self.bank[bk][urow, :], r_[drow, :], ALU.mult))
            self.out_proj("a_o%d", AO, AOb, 4)
            self.barrier()


    def _plan_mix1(self):
        W = self.W
        for i in range(3):
            W.add("b_in%d" % i, self.d_bin[i], (2048,))
        W.add("b_kup", self.d_bk, (16, 2, 64))
        W.add("b_vup", self.d_bv, (8, 2, 128))
        for g in range(4):
            W.add("b_qup%d" % g, self.d_bq[g], (4, 3, 128))
        for q in range(4):
            W.add("b_o%d" % q, self.d_bo[2 * q:2 * q + 2].rearrange("t p k -> p t k"), (2, 8, 128))

    def causal_attention(self, nheads_iter, QKfn, VAfn, scale, Kd, AO, AOb, bias_fn, banks, scr):
        raise NotImplementedError

    def _mix1(self):
        T = self.T
        kb = self.kb
        HG = self.hg
        SC = 96 ** -0.5
        with ExitStack() as es:
            sb = lambda n, shp, dt: kb.sb(n, shp, dt, es)
            AO = self.h
            AOb = [[Buf("AO%d_%d" % (a, tb)) for tb in range(NTB)] for a in range(8)]
            CQN = sb("CQN", [128, 3, SEQ], BF16)
            CQNb = [Buf("CQN%d" % tb) for tb in range(NTB)]
            CKVN = sb("CKVN", [128, 2, SEQ], BF16)
            CKVNb = [Buf("CKVN%d" % tb) for tb in range(NTB)]
            KR64 = sb("KR64", [128, SEQ], BF16)
            KR64b = [Buf("KR64_%d" % tb) for tb in range(NTB)]
            SQPE = sb("SQPE", [128, SEQ], BF16)
            SQPEb = [Buf("SQPE%d" % tb) for tb in range(NTB)]
            ROPE = sb("ROPE", [128, SEQ], F32)
            ROPEb = Buf("ROPE")
            VA = sb("VA", [128, 16, 192], BF16)
            VAb = [Buf("VA_%d" % q) for q in range(4)]
            QT = sb("QT", [128, SEQ], BF16)
            QTb = [Buf("QT%d" % tb) for tb in range(NTB)]
            KT = sb("KT", [128, SEQ], BF16)
            KTb = [Buf("KT%d" % tb) for tb in range(NTB)]
            QH2 = [sb("QH%d" % i, [128, 512], F32) for i in range(2)]
            QH2b = [Buf("QH%d" % i) for i in range(2)]
            TM4 = [sb("TM%d" % i, [128, 512], F32) for i in range(4)]
            TM4b = [Buf("TM%d" % i) for i in range(4)]
            QH, QHb = QH2[0], QH2b[0]
            TM, TMb = TM4[0:2], TM4b[0:2]
            SQ3 = sb("SQ3", [128, 3, 512], BF16)
            SQ3b = Buf("SQ3")
            TRI = sb("TRI", [128, 128], F32)
            TRIb = Buf("TRI")
            Sb_ = [sb("Sb%d" % i, [128, 128], F32) for i in range(3)]
            Sbb = [Buf("Sb%d" % i) for i in range(3)]
            PT = [sb("PT%d" % i, [128, 512], BF16) for i in range(3)]
            PTb = [Buf("PT%d" % i) for i in range(3)]
            R = [sb("R0", [128, 512], F32)] * 2
            Rb = [Buf("R0")] * 2
            hn = self.alloc_hn(es)
            T.dma("sp", ROPE[:], self.d_rope, [], [ROPEb], self.xk["rope"])
            T.dma("sp", TRI[:], self.d_tri, [], [TRIb], self.xk["tri"])
            T.op("dve", [], VAb, lambda e: e.memset(VA[:, :, 64:128], 1.0))
            T.op("dve", [], QTb, lambda e: e.memset(QT[96:128, :], 0.0))
            T.op("dve", [], KTb, lambda e: e.memset(KT[96:128, :], 0.0))
            w0, w0b = self.W.get("b_in0")
            w1, w1b = self.W.get("b_in1")
            w2, w2b = self.W.get("b_in2")
            wtile = [fview(w0[:, 0:1024], (8, 128)), fview(w0[:, 1024:2048], (8, 128)), fview(w1[:, 0:1024], (8, 128)),
                     fview(w1[:, 1024:2048], (8, 128)), fview(w2[:, 0:1024], (8, 128))]
            wpe = fview(w2[:, 1024:1536], (8, 64))
            wbufs = [w0b, w1b, w2b]
            for tb in range(NTB):
                ts = slice(tb * 512, (tb + 1) * 512)
                for (tiles, dst, dstb, gc, nd) in (((0, 1, 2), CQN, CQNb, HG_B_QA, 384), ((3, 4), CKVN, CKVNb, HG_B_KVA, 256)):
                    nt = len(tiles)
                    for i, ti in enumerate(tiles):
                        self.mm_acc(self.bank[i][:], [(wtile[ti][:, c, :], self.h[:, c, ts]) for c in range(8)],
                                    wbufs + [self.hb[tb]], self.bankb[i])
                        T.op("act", [self.bankb[i]], [SQ3b],
                             lambda e: e.activation(SQ3[:, i, :], self.bank[i][:], AF.Square), relaxed=(i > 0))
                    self.mm_acc(self.bank[3][:], [(self.ones[:], SQ3[:, i, :]) for i in range(nt)], [SQ3b, self.cb], self.bankb[3])
                    T.op("act", [self.bankb[3], self.cb], [hn["rb"][0]],
                         lambda e: e.activation(hn["r"][0][:], self.bank[3][:], AF.Ln, bias=self.epsc[:, 0:1], scale=1.0 / nd))
                    T.op("act", [hn["rb"][0]], [hn["rb"][0]], lambda e: e.activation(hn["r"][0][:], hn["r"][0][:], AF.Exp, scale=-0.5))
                    for i in range(nt):
                        T.op("dve", [self.bankb[i], hn["rb"][0], self.cb], [dstb[tb]],
                             lambda e: e.scalar_tensor_tensor(dst[:, i, ts], self.bank[i][:], HG[:, gc + i:gc + i + 1], hn["r"][0][:],
                                                              ALU.mult, ALU.mult), relaxed=(i > 0))
                self.mm_acc(self.bank[4][0:64, :], [(wpe[:, c, :], self.h[:, c, ts]) for c in range(8)],
                            wbufs + [self.hb[tb]], self.bankb[4])
                T.op("act", [self.bankb[4]], [SQPEb[tb]], lambda e: e.activation(SQPE[0:32, ts], self.bank[4][0:32, :], AF.Square))
                T.op("dve", [self.bankb[4], self.cb, ROPEb], [TMb[0]],
                     lambda e: e.scalar_tensor_tensor(TM[0][0:64, :], self.bank[4][0:64, :], HG[0:64, HG_B_KPE:HG_B_KPE + 1],
                                                      ROPE[0:64, ts], ALU.mult, ALU.mult))
                T.op("act", [TMb[0]], [TMb[1]], lambda e: e.copy(TM[1][0:32, :], TM[0][32:64, :]))
                T.op("dve", [TMb[0], TMb[1]], [TMb[1]],
                     lambda e: e.tensor_tensor(TM[1][0:32, :], TM[0][0:32, :], TM[1][0:32, :], ALU.add))
                T.op("act", [TMb[1]], [KR64b[tb]], lambda e: e.copy(KR64[64:96, ts], TM[1][0:32, :]))
            self.W.done(3)
            self.barrier()
            wk_r, wkb_r = self.W.get("b_kup")
            wv_r, wvb_r = self.W.get("b_vup")
            WKV = sb("WKV", [128, 4096], BF16)
            wkb = wvb = Buf("WKV")
            T.op("dve", [wkb_r], [wkb], lambda e: e.tensor_copy(WKV[:, 0:2048], wk_r.rearrange("p a b c -> p (a b c)")))
            T.op("dve", [wvb_r], [wkb], lambda e: e.tensor_copy(WKV[:, 2048:4096], wv_r.rearrange("p a b c -> p (a b c)")))
            self.W.done(2)
            wk = fview(WKV[:, 0:2048], (16, 2, 64))
            wv = fview(WKV[:, 2048:4096], (8, 2, 128))
            PB, SUMB, STB, ACB, VB = (0, 1, 3, 4), 2, (3, 4), (5, 6), 7
            STB = (3, 4, 7, 0)
            si = 0
            ai = 0
            for g in range(4):
                wq, wqb = self.W.get("b_qup%d" % g)
                for hh in range(4):
                    hd = 4 * g + hh
                    urow = slice(0, 64) if hd % 2 == 0 else slice(64, 128)
                    drow = slice(64, 128) if hd % 2 == 0 else slice(0, 64)
                    vo = (hd % 2) * 64
                    if hd % 2 == 0:
                        pr = hd // 2
                        for q4 in range(4):
                            def mmv(e):
                                ins = None
                                for i in range(4):
                                    tc = q4 * 4 + i
                                    for c in range(2):
                                        ins = e.matmul(self.bank[VB][:, i * 128:(i + 1) * 128], CKVN[:, c, tc * 128:(tc + 1) * 128],
                                                       wv[:, pr, c, :], start=(c == 0), stop=(c == 1))
                                return ins
                            T.op("pe", [wvb, CKVNb[q4]], [self.bankb[VB]], mmv)
                            bv = self.bank[VB][:].rearrange("p (i n) -> p i n", i=4)
                            for par in range(2):
                                T.op("act", [self.bankb[VB]], [VAb[q4]],
                                     lambda e: e.copy(VA[:, q4 * 4:(q4 + 1) * 4, par * 128:par * 128 + 64], bv[:, :, par * 64:(par + 1) * 64]))
                    jobs = []
                    for tb in range(NTB):
                        ts = slice(tb * 512, (tb + 1) * 512)

                        def kpost(b, r, rb, tb=tb, ts=ts):
                            T.op("dve", [self.bankb[b], rb, self.cb], [KTb[tb]],
                                 lambda e: e.scalar_tensor_tensor(KT[0:64, ts], self.bank[b][0:64, :], HG[0:64, HG_B_K:HG_B_K + 1],
                                                                  r[0:64, :], ALU.mult, ALU.mult))
                            T.op("dve", [KR64b[tb], rb], [KTb[tb]],
                                 lambda e: e.tensor_tensor(KT[64:96, ts], KR64[64:96, ts], r[64:96, :], ALU.mult), relaxed=True)
                        jobs.append({"mm": (lambda b, ts=ts, tb=tb: self.mm_acc(self.bank[b][0:64, :],
                                                                             [(wk[:, hd, c, :], CKVN[:, c, ts]) for c in range(2)],
                                                                             [wkb, CKVNb[tb]], self.bankb[b])),
                                     "sq_rows": 64, "rrows": 96, "nd": 96, "sum_reads": [SQPEb[tb]],
                                     "sums": (lambda sq, ts=ts: [(self.ones[0:64, 0:96], sq[0:64, :]), (self.ones[0:32, 0:96], SQPE[0:32, ts])]),
                                     "post": kpost})

                        def qpost(b, r, rb, tb=tb, ts=ts):
                            QH, QHb = QH2[tb % 2], QH2b[tb % 2]
                            TM, TMb = TM4[2 * (tb % 2):2 * (tb % 2) + 2], TM4b[2 * (tb % 2):2 * (tb % 2) + 2]
                            T.op("dve", [self.bankb[b], rb, self.cb], [QTb[tb]],
                                 lambda e: e.scalar_tensor_tensor(QT[0:64, ts], self.bank[b][0:64, :], HG[0:64, HG_B_Q:HG_B_Q + 1],
                                                                  r[0:64, :], ALU.mult, ALU.mult))
                            T.op("dve", [self.bankb[b], rb, self.cb], [QHb],
                                 lambda e: e.scalar_tensor_tensor(QH[64:128, :], self.bank[b][64:128, :], HG[64:128, HG_B_Q:HG_B_Q + 1],
                                                                  r[64:128, :], ALU.mult, ALU.mult))
                            T.op("dve", [QHb, ROPEb], [TMb[0]],
                                 lambda e: e.tensor_tensor(TM[0][96:128, :], QH[96:128, :], ROPE[96:128, ts], ALU.mult))
                            T.op("act", [TMb[0]], [TMb[1]], lambda e: e.copy(TM[1][64:96, :], TM[0][96:128, :]))
                            T.op("dve", [QHb, ROPEb], [QHb],
                                 lambda e: e.tensor_tensor(QH[64:96, :], QH[64:96, :], ROPE[64:96, ts], ALU.mult))
                            T.op("dve", [QHb, TMb[1]], [QTb[tb]],
                                 lambda e: e.tensor_tensor(QT[64:96, ts], QH[64:96, :], TM[1][64:96, :], ALU.add), relaxed=True)
                        jobs.append({"mm": (lambda b, ts=ts, tb=tb: self.mm_acc(self.bank[b][:],
                                                                             [(wq[:, hh, c, :], CQN[:, c, ts]) for c in range(3)],
                                                                             [wqb, CQNb[tb]], self.bankb[b])),
                                     "sq_rows": 96, "rrows": 128, "nd": 96,
                                     "sums": (lambda sq: [(self.ones[0:96, :], sq[0:96, :])]),
                                     "post": qpost})
                    self.norm_pipeline(jobs, PB, SUMB, hn)
                    items = [(qb, kc) for qb in range(NTB) for kc in range(4 * (qb + 1))]
                    base = si
                    si += len(items)
                    abase = ai
                    ai += NTB

                    def geom(idx):
                        qb, kc = items[idx]
                        di = kc - 4 * qb
                        c0 = 128 * di if di > 0 else 0
                        return qb, kc, di, c0, 512 - c0

                    def A(idx):
                        qb, kc, di, c0, N = geom(idx)
                        bs = STB[(base + idx) % 4]
                        self.mm_acc(self.bank[bs][:, 0:N], [(KT[:, kc * 128:kc * 128 + 128], QT[:, qb * 512 + c0:qb * 512 + 512])],
                                    [KTb[kc // 4], QTb[qb]], self.bankb[bs])

                    def B(idx):
                        qb, kc, di, c0, N = geom(idx)
                        bs = STB[(base + idx) % 4]
                        s_i = (base + idx) % 3
                        ab = ACB[(abase + qb) % 2]
                        nk = 4 * (qb + 1)
                        if di >= 0:
                            T.op("dve", [self.bankb[bs], TRIb], [Sbb[s_i]],
                                 lambda e: e.scalar_tensor_tensor(Sb_[s_i][:], self.bank[bs][:, 0:128], SC, TRI[:], ALU.mult, ALU.add))
                            T.op("act", [Sbb[s_i]], [PTb[s_i]], lambda e: e.activation(PT[s_i][:, 0:128], Sb_[s_i][:], AF.Exp))
                            if N > 128:
                                T.op("act", [self.bankb[bs]], [PTb[s_i]],
                                     lambda e: e.activation(PT[s_i][:, 128:N], self.bank[bs][:, 128:N], AF.Exp, scale=SC), relaxed=True)
                        else:
                            T.op("act", [self.bankb[bs]], [PTb[s_i]],
                                 lambda e: e.activation(PT[s_i][:, 0:N], self.bank[bs][:, 0:N], AF.Exp, scale=SC))
                        T.op("pe", [VAb[kc // 4], PTb[s_i]], [self.bankb[ab]],
                             lambda e: e.matmul(self.bank[ab][:, c0:512], VA[:, kc, vo:vo + 128], PT[s_i][:, 0:N],
                                                start=(kc == 0), stop=(kc == nk - 1)))
                        if kc == nk - 1:
                            def fin():
                                ts = slice(qb * 512, qb * 512 + 512)
                                r_, rb = R[qb % 2], Rb[qb % 2]
                                self.recip(r_[drow, :], self.bank[ab][drow, :], [self.bankb[ab]], rb)
                                T.op("dve", [self.bankb[ab], rb], [AOb[hd // 2][qb]],
                                     lambda e: e.tensor_tensor(AO[urow, hd // 2, ts], self.bank[ab][urow, :], r_[drow, :], ALU.mult))
                            return fin
                    self.pipe(len(items), A, B, 3)
                self.W.done()
            self.out_proj("b_o%d", AO, AOb, 8)
            self.barrier()


    def _plan_mix2(self):
        W = self.W
        for hd in range(8):
            W.add("c_qk%d" % hd, self.d_cqk[hd], (2, 8, 128))
            W.add("c_v%d" % hd, self.d_cv[hd], (8, 128))
        for q in range(4):
            W.add("c_o%d" % q, self.d_co[2 * q:2 * q + 2].rearrange("t p k -> p t k"), (2, 8, 128))

    def _mix2(self):
        T = self.T
        kb = self.kb
        HG = self.hg
        LAM_INIT = 0.8 - 0.6 * math.exp(-0.3 * 2)
        with ExitStack() as es:
            sb = lambda n, shp, dt: kb.sb(n, shp, dt, es)
            AO = sb("AO", [128, 8, SEQ], BF16)
            AOb = [[Buf("AO%d_%d" % (a, tb)) for tb in range(NTB)] for a in range(8)]
            QK = [sb("QK%d" % i, [128, SEQ], BF16) for i in range(3)]
            QKb = [[Buf("QK%d_%d" % (i, tb)) for tb in range(NTB)] for i in range(3)]
            T.op("dve", [], QKb[1], lambda e: e.memset(QK[1][64:128, :], 0.0))
            T.op("dve", [], QKb[2], lambda e: e.memset(QK[2][0:64, :], 0.0))
            VA = sb("VA", [128, 16, 128], BF16)
            VAb = [Buf("VA_%d" % q) for q in range(4)]
            GF = [sb("GF%d" % i, [128, SEQ], F32) for i in range(2)]
            GFb = [Buf("GF%d" % i) for i in range(2)]
            Sb_ = [sb("Sb%d" % i, [128, 512], F32) for i in range(3)]
            Sbb = [Buf("Sb%d" % i) for i in range(3)]
            PT = [sb("PT%d" % i, [128, 512], BF16) for i in range(3)]
            PTb = [Buf("PT%d" % i) for i in range(3)]
            TT = [sb("TT%d" % i, [128, 512], F32) for i in range(2)]
            TTb = [Buf("TT%d" % i) for i in range(2)]
            lamt = sb("lamt", [128, 256], F32)
            lamb = Buf("lamt")
            lams = sb("lams", [128, 8], F32)
            hn = self.alloc_hn(es)
            PB, SUMB, STB = 0, 1, (2, 3)
            OB, DB = (4, 6), (5, 7)
            T.dma("sp", lamt[:], self.d_clam, [], [lamb], self.xk["lam"])
            for i in range(2):
                T.op("dve", [lamb], [lamb],
                     lambda e: e.tensor_tensor(lamt[:, i * 128:i * 128 + 64], lamt[:, i * 128:i * 128 + 64],
                                               lamt[:, i * 128 + 64:i * 128 + 128], ALU.mult))
                T.op("dve", [lamb], [lamb],
                     lambda e: e.reduce_sum(lams[:, i:i + 1], lamt[:, i * 128:i * 128 + 64], mybir.AxisListType.X))
            T.op("act", [lamb], [lamb], lambda e: e.activation(lams[:, 2:4], lams[:, 0:2], AF.Exp))
            T.op("dve", [lamb], [lamb], lambda e: e.tensor_tensor(lams[:, 4:5], lams[:, 3:4], lams[:, 2:3], ALU.subtract))
            T.op("dve", [lamb], [lamb], lambda e: e.tensor_scalar(lams[:, 5:6], lams[:, 4:5], -LAM_INIT, None, ALU.add))
            T.op("dve", [], [lamb], lambda e: e.memset(lams[:, 6:7], math.log(1.0 - LAM_INIT)))
            neglam = lams[:, 5:6]
            PBS, STB = (0, 2, 3), (2, 3, 0, 1)
            si = 0
            for hd in range(8):
                wqk, wqkb = self.W.get("c_qk%d" % hd)
                wv, wvb = self.W.get("c_v%d" % hd)
                for m in range(2):
                    T.dma("sp", GF[m][:], self.d_cg[m * 8 + hd], [], [GFb[m]], self.g3k[m])
                jobs = []
                for tb in range(NTB):
                    ts = slice(tb * 512, (tb + 1) * 512)

                    def post_q(b, r, rb, tb=tb, ts=ts):
                        T.op("dve", [self.bankb[b], rb, self.cb], [QKb[0][tb]],
                             lambda e: e.scalar_tensor_tensor(QK[0][:, ts], self.bank[b][:], HG[:, HG_C_Q:HG_C_Q + 1], r[:], ALU.mult, ALU.mult))

                    def post_k(b, r, rb, tb=tb, ts=ts):
                        T.op("dve", [self.bankb[b], rb, self.cb], [QKb[1][tb]],
                             lambda e: e.scalar_tensor_tensor(QK[1][0:64, ts], self.bank[b][0:64, :], HG[0:64, HG_C_K:HG_C_K + 1],
                                                              r[0:64, :], ALU.mult, ALU.mult))
                        T.op("dve", [self.bankb[b], rb, self.cb], [QKb[2][tb]],
                             lambda e: e.scalar_tensor_tensor(QK[2][64:128, ts], self.bank[b][64:128, :], HG[64:128, HG_C_K:HG_C_K + 1],
                                                              r[64:128, :], ALU.mult, ALU.mult))
                    for i, post in ((0, post_q), (1, post_k)):
                        jobs.append({"mm": (lambda b, i=i, ts=ts, tb=tb: self.mm_acc(self.bank[b][:],
                                                                                  [(wqk[:, i, c, :], self.h[:, c, ts]) for c in range(8)],
                                                                                  [wqkb, self.hb[tb]], self.bankb[b])),
                                     "sq_rows": 128, "rrows": 128, "nd": 64,
                                     "sums": (lambda sq: [(self.bd64[:], sq[:])]), "post": post})
                self.norm_pipeline(jobs, PBS, SUMB, hn)
                for q4 in range(4):
                    b = PBS[q4 % 3]

                    def mmv(e):
                        ins = None
                        for i in range(4):
                            tc = q4 * 4 + i
                            for c in range(8):
                                ins = e.matmul(self.bank[b][:, i * 128:(i + 1) * 128], self.h[:, c, tc * 128:(tc + 1) * 128],
                                               wv[:, c, :], start=(c == 0), stop=(c == 7))
                        return ins
                    T.op("pe", [wvb, self.hb[q4]], [self.bankb[b]], mmv)
                    T.op("act", [self.bankb[b]], [VAb[q4]],
                         lambda e: e.copy(VA[:, q4 * 4:(q4 + 1) * 4, :], self.bank[b][:].rearrange("p (i n) -> p i n", i=4)))
                self.W.done(2)
                items = [(qb, m, kc) for qb in range(NTB) for m in range(2) for kc in range(4 * (qb + 1))]
                base = si
                si += len(items)

                def geom(idx):
                    qb, m, kc = items[idx]
                    di = kc - 4 * qb
                    c0 = 128 * di if di > 0 else 0
                    return qb, m, kc, c0, 512 - c0

                def A(idx):
                    qb, m, kc, c0, N = geom(idx)
                    bs = STB[(base + idx) % 4]
                    self.mm_acc(self.bank[bs][:, 0:N], [(QK[1 + m][:, kc * 128:kc * 128 + 128], QK[0][:, qb * 512 + c0:qb * 512 + 512])],
                                [QKb[1 + m][kc // 4], QKb[0][qb]], self.bankb[bs])

                def B(idx):
                    qb, m, kc, c0, N = geom(idx)
                    bs = STB[(base + idx) % 4]
                    s_i = (base + idx) % 3
                    ob, db = OB[m], DB[m]
                    nk = 4 * (qb + 1)
                    g0 = qb * 512 + c0 - kc * 128
                    T.op("dve", [self.bankb[bs], GFb[m]], [Sbb[s_i]],
                         lambda e: e.scalar_tensor_tensor(Sb_[s_i][:, 0:N], self.bank[bs][:, 0:N], 0.125, GF[m][:, g0:g0 + N],
                                                          ALU.mult, ALU.add))
                    T.op("act", [Sbb[s_i]], [PTb[s_i]], lambda e: e.activation(PT[s_i][:, 0:N], Sb_[s_i][:, 0:N], AF.Exp))
                    T.op("pe", [VAb[kc // 4], PTb[s_i]], [self.bankb[ob]],
                         lambda e: e.matmul(self.bank[ob][:, c0:512], VA[:, kc, :], PT[s_i][:, 0:N], start=(kc == 0), stop=(kc == nk - 1)))
                    T.op("pe", [PTb[s_i], self.cb], [self.bankb[db]],
                         lambda e: e.matmul(self.bank[db][:, c0:512], self.ones[:], PT[s_i][:, 0:N], start=(kc == 0), stop=(kc == nk - 1)))
                    if kc == nk - 1:
                        def fin():
                            ts = slice(qb * 512, qb * 512 + 512)
                            self.recip(TT[m][:], self.bank[db][:], [self.bankb[db]], TTb[m])
                            T.op("dve", [self.bankb[ob], TTb[m]], [TTb[m]],
                                 lambda e: e.tensor_tensor(TT[m][:], self.bank[ob][:], TT[m][:], ALU.mult))
                            if m == 0:
                                return
                            T.op("dve", [TTb[0], TTb[1], lamb], [TTb[0]],
                                 lambda e: e.scalar_tensor_tensor(TT[0][:], TT[1][:], neglam, TT[0][:], ALU.mult, ALU.add))
                            T.op("act", [TTb[0]], [hn["sqhb"][0]], lambda e: e.activation(hn["sqh"][0][:], TT[0][:], AF.Square))
                            FB = DB[1]
                            self.mm_acc(self.bank[FB][:], [(self.ones[:], hn["sqh"][0][:])], [hn["sqhb"][0], self.cb], self.bankb[FB])
                            T.op("act", [self.bankb[FB], self.cb], [hn["rb"][0]],
                                 lambda e: e.activation(hn["r"][0][:], self.bank[FB][:], AF.Ln, bias=self.epsc[:, 0:1], scale=1.0 / 128))
                            T.op("act", [hn["rb"][0], lamb], [hn["rb"][0]],
                                 lambda e: e.activation(hn["r"][0][:], hn["r"][0][:], AF.Exp, bias=lams[:, 6:7], scale=-0.5))
                            T.op("dve", [TTb[0], hn["rb"][0], self.cb], [AOb[hd][qb]],
                                 lambda e: e.scalar_tensor_tensor(AO[:, hd, ts], TT[0][:], HG[:, HG_C_SUB:HG_C_SUB + 1], hn["r"][0][:],
                                                                  ALU.mult, ALU.mult))
                        return fin
                self.pipe(len(items), A, B, 3)
            self.out_proj("c_o%d", AO, AOb, 8)
            self.barrier()

    def _plan_mix3(self):
        W = self.W
        W.add("d_k", self.d_dk, (2, 8, 128))
        W.add("d_v", self.d_dv, (8, 128))
        for g in range(4):
            W.add("d_q%d" % g, self.d_dq[g], (2, 8, 128))
        for q in range(4):
            W.add("d_o%d" % q, self.d_do[2 * q:2 * q + 2].rearrange("t p k -> p t k"), (2, 8, 128))

    def _mix3(self):
        T = self.T
        kb = self.kb
        HG = self.hg
        with ExitStack() as es:
            sb = lambda n, shp, dt: kb.sb(n, shp, dt, es)
            AO = sb("AO", [128, 8, SEQ], BF16)
            AOb = [[Buf("AO%d_%d" % (a, tb)) for tb in range(NTB)] for a in range(8)]
            KT = [[sb("KT%d_%d" % (i, j), [128, SEQ], BF16) for j in range(2)] for i in range(2)]
            KTb = [[[Buf("KT%d_%d_%d" % (i, j, tb)) for tb in range(NTB)] for j in range(2)] for i in range(2)]
            VA = [sb("VA%d" % i, [128, 16, 192], BF16) for i in range(2)]
            VAb = [[Buf("VA%d_%d" % (i, q)) for q in range(4)] for i in range(2)]
            QT = [sb("QT%d" % i, [128, SEQ], BF16) for i in range(2)]
            QTb = [[Buf("QT%d_%d" % (i, tb)) for tb in range(NTB)] for i in range(2)]
            G3 = [sb("G3_%d" % i, [128, 256], F32) for i in range(2)]
            G3b = [Buf("G3_%d" % i) for i in range(2)]
            Sb_ = [sb("Sb%d" % i, [128, 256], F32) for i in range(3)]
            Sbb = [Buf("Sb%d" % i) for i in range(3)]
            PT = [sb("PT%d" % i, [128, 256], BF16) for i in range(3)]
            PTb = [Buf("PT%d" % i) for i in range(3)]
            R = [sb("R0", [128, 512], F32)] * 2
            Rb = [Buf("R0")] * 2
            es_t = sb("esink", [128, 16], F32)
            esb = Buf("esink")
            hn = self.alloc_hn(es)
            PB, SUMB, STB, ACB = (0, 1, 3, 4), 2, (3, 4, 7, 0), (5, 6)
            T.op("act", [self.cb], [esb], lambda e: e.activation(es_t[:], HG[:, HG_D_SINK:HG_D_SINK + 16], AF.Exp))
            for i in range(2):
                T.op("dve", [], VAb[i], lambda e: e.memset(VA[i][:, :, 64:128], 1.0))
                T.op("dve", [], KTb[i][0], lambda e: e.memset(KT[i][0][64:128, :], 0.0))
                T.op("dve", [], KTb[i][1], lambda e: e.memset(KT[i][1][0:64, :], 0.0))
            wk, wkb = self.W.get("d_k")
            wv, wvb = self.W.get("d_v")
            jobs = []
            for kvh in range(2):
                for tb in range(NTB):
                    ts = slice(tb * 512, (tb + 1) * 512)

                    def post_k(b, r, rb, kvh=kvh, tb=tb, ts=ts):
                        T.op("dve", [self.bankb[b], rb, self.cb], [KTb[kvh][0][tb]],
                             lambda e: e.scalar_tensor_tensor(KT[kvh][0][0:64, ts], self.bank[b][0:64, :], HG[0:64, HG_D_K:HG_D_K + 1],
                                                              r[0:64, :], ALU.mult, ALU.mult))
                        T.op("dve", [self.bankb[b], rb, self.cb], [KTb[kvh][1][tb]],
                             lambda e: e.scalar_tensor_tensor(KT[kvh][1][64:128, ts], self.bank[b][64:128, :], HG[64:128, HG_D_K:HG_D_K + 1],
                                                              r[64:128, :], ALU.mult, ALU.mult))
                    jobs.append({"mm": (lambda b, kvh=kvh, ts=ts, tb=tb: self.mm_acc(self.bank[b][:],
                                                                                      [(wk[:, kvh, c, :], self.h[:, c, ts]) for c in range(8)],
                                                                                      [wkb, self.hb[tb]], self.bankb[b])),
                                 "sq_rows": 128, "rrows": 128, "nd": 64, "sums": (lambda sq: [(self.bd64[:], sq[:])]), "post": post_k})
            self.norm_pipeline(jobs, PB, SUMB, hn)
            for q4 in range(4):
                b = PB[q4 % 2]

                def mmv(e):
                    ins = None
                    for i in range(4):
                        tc = q4 * 4 + i
                        for c in range(8):
                            ins = e.matmul(self.bank[b][:, i * 128:(i + 1) * 128], self.h[:, c, tc * 128:(tc + 1) * 128],
                                           wv[:, c, :], start=(c == 0), stop=(c == 7))
                    return ins
                T.op("pe", [wvb, self.hb[q4]], [self.bankb[b]], mmv)
                bv = self.bank[b][:].rearrange("p (i n) -> p i n", i=4)
                for kvh in range(2):
                    for off in (0, 128):
                        T.op("act", [self.bankb[b]], [VAb[kvh][q4]],
                             lambda e: e.copy(VA[kvh][:, q4 * 4:(q4 + 1) * 4, off:off + 64], bv[:, :, kvh * 64:(kvh + 1) * 64]),
                             relaxed=(off > 0))
            self.W.done(2)
            cnt = [0]
            for g in range(4):
                wq, wqb = self.W.get("d_q%d" % g)
                for pr in range(2):
                    pair = 2 * g + pr
                    qt, qtb = QT[pair % 2], QTb[pair % 2]
                    jobs = []
                    for tb in range(NTB):
                        ts = slice(tb * 512, (tb + 1) * 512)

                        def post_q(b, r, rb, tb=tb, ts=ts):
                            T.op("dve", [self.bankb[b], rb, self.cb], [qtb[tb]],
                                 lambda e: e.scalar_tensor_tensor(qt[:, ts], self.bank[b][:], HG[:, HG_D_Q:HG_D_Q + 1], r[:],
                                                                  ALU.mult, ALU.mult))
                        jobs.append({"mm": (lambda b, ts=ts, tb=tb: self.mm_acc(self.bank[b][:],
                                                                             [(wq[:, pr, c, :], self.h[:, c, ts]) for c in range(8)],
                                                                             [wqb, self.hb[tb]], self.bankb[b])),
                                     "sq_rows": 128, "rrows": 128, "nd": 64, "sums": (lambda sq: [(self.bd64[:], sq[:])]), "post": post_q})
                    self.norm_pipeline(jobs, PB, SUMB, hn)
                    for h2 in range(2):
                        self._mix3_head(2 * pair + h2, qt, qtb, KT, KTb, VA, VAb, G3, G3b, Sb_, Sbb, PT, PTb, R, Rb, es_t, esb,
                                        AO, AOb, STB, ACB, cnt)
                self.W.done()
            self.out_proj("d_o%d", AO, AOb, 8)
            self.barrier()

    def _mix3_head(self, hd, qt, qtb, KT, KTb, VA, VAb, G3, G3b, Sb_, Sbb, PT, PTb, R, Rb, es_t, esb, AO, AOb, STB, ACB, cnt):
        T = self.T
        kvh = hd // 8
        par = hd % 2
        kt, ktb = KT[kvh][par], KTb[kvh][par]
        vo = par * 64
        urow = slice(0, 64) if par == 0 else slice(64, 128)
        drow = slice(64, 128) if par == 0 else slice(0, 64)
        g3, g3b = G3[hd % 2], G3b[hd % 2]
        T.dma("sp", g3[:], self.d_g3[hd], [], [g3b], self.g3k[hd % 2])
        base = cnt[0]
        cnt[0] += 16

        def A(kc):
            k0 = kc * 128
            N = 256 if kc < 15 else 128
            bs = STB[(base + kc) % 4]
            qbufs = [qtb[k0 // 512]] + ([qtb[(k0 + 128) // 512]] if kc < 15 else [])
            self.mm_acc(self.bank[bs][:, 0:N], [(kt[:, k0:k0 + 128], qt[:, k0:k0 + N])], [ktb[k0 // 512]] + qbufs, self.bankb[bs])

        def B(kc):
            N = 256 if kc < 15 else 128
            bs = STB[(base + kc) % 4]
            s_i = (base + kc) % 3
            T.op("dve", [self.bankb[bs], g3b], [Sbb[s_i]],
                 lambda e: e.scalar_tensor_tensor(Sb_[s_i][:, 0:N], self.bank[bs][:, 0:N], 0.125, g3[:, 0:N], ALU.mult, ALU.add))
            T.op("act", [Sbb[s_i]], [PTb[s_i]], lambda e: e.activation(PT[s_i][:, 0:N], Sb_[s_i][:, 0:N], AF.Exp))
            ab0 = ACB[(kc // 4) % 2]
            c0 = (kc % 4) * 128
            T.op("pe", [VAb[kvh][kc // 4], PTb[s_i]], [self.bankb[ab0]],
                 lambda e: e.matmul(self.bank[ab0][:, c0:c0 + 128], VA[kvh][:, kc, vo:vo + 128], PT[s_i][:, 0:128],
                                    start=(kc == 0), stop=True))
            if kc < 15:
                ab1 = ACB[((kc + 1) // 4) % 2]
                c1 = ((kc + 1) % 4) * 128
                T.op("pe", [VAb[kvh][kc // 4], PTb[s_i]], [self.bankb[ab1]],
                     lambda e: e.matmul(self.bank[ab1][:, c1:c1 + 128], VA[kvh][:, kc, vo:vo + 128], PT[s_i][:, 128:256],
                                        start=True, stop=False))
            if kc % 4 == 3:
                def fin():
                    m = kc // 4
                    ts = slice(m * 512, (m + 1) * 512)
                    r, rb = R[m % 2], Rb[m % 2]
                    self.recip(r[drow, :], self.bank[ab0][drow, :], [self.bankb[ab0], esb], rb, bias=es_t[drow, hd:hd + 1])
                    T.op("dve", [self.bankb[ab0], rb], [AOb[hd // 2][m]],
                         lambda e: e.tensor_tensor(AO[urow, hd // 2, ts], self.bank[ab0][urow, :], r[drow, :], ALU.mult))
                return fin
        self.pipe(16, A, B, 3)


def host_weights(inp):
    f32 = np.float32
    out = {}
    wup = inp["f_w_up"].reshape(4, 8, 128, 2, NJ, 128)
    out["wup"] = np.ascontiguousarray(wup.transpose(0, 4, 2, 1, 3, 5)).reshape(4, NJ, 128, 2048)
    wdn = inp["f_w_down"].reshape(4, 2, 11, 128, 8, 128)
    out["wdn"] = np.ascontiguousarray(wdn.transpose(0, 1, 4, 3, 2, 5)).reshape(4, 2, 8, 128, 11 * 128)
    gains = np.zeros((128, 64), f32)
    gains[:, 0:32] = inp["norm_mix"].reshape(4, 8, 128).transpose(2, 0, 1).reshape(128, 32)
    gains[:, 32:64] = inp["norm_ffn"].reshape(4, 8, 128).transpose(2, 0, 1).reshape(128, 32)
    out["gains"] = gains
    cw = inp["f_conv_w"].reshape(4, 3, NJ, 128)
    cbv = inp["f_conv_b"].reshape(4, 1, NJ, 128)
    convp = np.concatenate([cw, cbv], axis=1)
    out["convp"] = np.ascontiguousarray(convp.transpose(3, 0, 1, 2)).reshape(128, 4 * 4 * NJ)
    hg = np.zeros((128, NHG), f32)
    tab = inp["rel_bias_table"]
    dw = inp["d_w_in"][0]
    rep2 = lambda v: np.concatenate([v, v])
    hg[:, HG_D_Q] = rep2(inp["d_q_norm"][0])
    hg[:, HG_D_K] = rep2(inp["d_k_norm"][0])
    hg[:, HG_D_SINK:HG_D_SINK + 16] = inp["d_sinks"][0][None, :]
    wk = dw[:, 1024:1152].reshape(8, 128, 2, 64).transpose(1, 2, 0, 3)
    out["d_k"] = np.ascontiguousarray(np.concatenate([wk, wk], axis=3)).reshape(128, 2048)
    out["d_v"] = np.ascontiguousarray(dw[:, 1152:1280].reshape(8, 128, 128).transpose(1, 0, 2)).reshape(128, 1024)
    wq = dw[:, 0:1024].reshape(8, 128, 4, 2, 128).transpose(2, 1, 3, 0, 4)
    out["d_q"] = np.ascontiguousarray(wq).reshape(4, 128, 2048)
    out["d_o"] = tile_fm(inp["d_w_out"][0])
    jj = np.arange(256)[None, :] - np.arange(128)[:, None]
    valid = (jj >= 0) & (jj <= 127)
    bk = t5_bucket_np(np.maximum(jj, 0))
    g3 = np.where(valid[None], tab[bk].transpose(2, 0, 1), f32(NEG))
    out["d_g3"] = np.ascontiguousarray(g3.astype(f32))
    aw = inp["a_w_in"][0].reshape(8, 128, 3, 3, 8, 64)
    for g in range(3):
        hg[:, HG_A_Q + g] = rep2(inp["a_q_norm"][0][g])
        hg[:, HG_A_K + g] = rep2(inp["a_k_norm"][0][g])
        hg[:, HG_A_QK + g] = np.concatenate([inp["a_q_norm"][0][g], inp["a_k_norm"][0][g]])
    av = aw[:, :, :, 2].reshape(8, 128, 3, 2, 4, 64)
    out["a_v"] = np.ascontiguousarray(av.transpose(3, 2, 1, 0, 4, 5)).reshape(2, 3, 128, 2048)
    aqk = aw[:, :, :, 0:2]
    out["a_qk"] = np.ascontiguousarray(aqk.transpose(4, 2, 1, 0, 3, 5)).reshape(8, 3, 128, 1024)
    out["a_o"] = tile_fm(inp["a_w_out"][0])
    valid0 = (jj >= 0) & (jj <= 128)
    g0 = np.zeros((3, 8, 128, 256), f32)
    for g, dil in enumerate((1, 4, 16)):
        bk0 = t5_bucket_np(np.maximum(jj, 0) * dil)
        g0[g] = np.where(valid0[None], tab[:, 0:8][bk0].transpose(2, 0, 1), f32(NEG))
    out["a_g0"] = g0
    bw = inp["b_w_in"][0]
    part = np.concatenate([np.arange(16, 32), np.arange(0, 16)])
    hg[:, HG_B_QA:HG_B_QA + 3] = inp["b_q_a_norm"][0].reshape(3, 128).T
    hg[:, HG_B_KVA:HG_B_KVA + 2] = inp["b_kv_a_norm"][0].reshape(2, 128).T
    qn, kn = inp["b_q_norm"][0], inp["b_k_norm"][0]
    hg[:, HG_B_Q] = np.concatenate([qn, qn[64 + part]])
    hg[0:64, HG_B_K] = kn[0:64]
    hg[0:64, HG_B_KPE] = np.concatenate([kn[64:96], kn[64 + part]])
    tl = lambda w: w.reshape(8, 128, -1).transpose(1, 0, 2).reshape(128, -1)
    tiles = [tl(bw[:, i * 128:(i + 1) * 128]) for i in range(5)]
    pe_cols = np.concatenate([640 + np.arange(32), 640 + part])
    b_in = np.zeros((3, 128, 2048), f32)
    b_in[0, :, 0:1024], b_in[0, :, 1024:2048] = tiles[0], tiles[1]
    b_in[1, :, 0:1024], b_in[1, :, 1024:2048] = tiles[2], tiles[3]
    b_in[2, :, 0:1024], b_in[2, :, 1024:1536] = tiles[4], tl(bw[:, pe_cols])
    out["b_in"] = b_in
    kvu = inp["b_w_kv_up"][0].reshape(2, 128, 16, 2, 64)
    out["b_kup"] = np.ascontiguousarray(kvu[:, :, :, 0].transpose(1, 2, 0, 3)).reshape(128, 2048)
    vv = kvu[:, :, :, 1].reshape(2, 128, 8, 128)
    out["b_vup"] = np.ascontiguousarray(vv.transpose(1, 2, 0, 3)).reshape(128, 2048)
    qu = inp["b_w_q_up"][0].reshape(3, 128, 16, 96)
    qu = np.concatenate([qu, qu[:, :, :, 64 + part]], axis=3)
    out["b_qup"] = np.ascontiguousarray(qu.reshape(3, 128, 4, 4, 128).transpose(2, 1, 3, 0, 4)).reshape(4, 128, 1536)
    out["b_o"] = tile_fm(inp["b_w_out"][0])
    inv_freq = (np.float32(10000.0) ** (-np.arange(0, 32, 2, dtype=f32) / np.float32(32))).astype(f32)
    ang = (np.arange(SEQ, dtype=f32)[:, None] * inv_freq[None, :]).astype(f32)
    cos = np.cos(ang).astype(f32).T
    sin = np.sin(ang).astype(f32).T
    cos32 = np.concatenate([cos, cos], axis=0)
    sins32 = np.concatenate([-sin, sin], axis=0)
    out["b_rope"] = np.ascontiguousarray(np.concatenate([cos32, sins32, cos32, sins32], axis=0))
    pp = np.arange(128)
    out["b_tri"] = np.where(pp[None, :] >= pp[:, None], f32(0), f32(NEG)).astype(f32)
    cw = inp["c_w_in"][0]
    hg[:, HG_C_Q] = rep2(inp["c_q_norm"][0])
    hg[:, HG_C_K] = rep2(inp["c_k_norm"][0])
    hg[:, HG_C_SUB] = inp["c_subln"][0]
    cq = cw[:, 0:1024].reshape(8, 128, 8, 2, 64)
    ck = cw[:, 1024:2048].reshape(8, 128, 8, 2, 64)
    cqk = np.stack([cq.reshape(8, 128, 8, 128), ck.reshape(8, 128, 8, 128)], axis=3)
    out["c_qk"] = np.ascontiguousarray(cqk.transpose(2, 1, 3, 0, 4)).reshape(8, 128, 2048)
    cv = cw[:, 2048:3072].reshape(8, 128, 8, 128)
    out["c_v"] = np.ascontiguousarray(cv.transpose(2, 1, 0, 3)).reshape(8, 128, 1024)
    out["c_o"] = tile_fm(inp["c_w_out"][0])
    dd = np.arange(SEQ)[None, :] - np.arange(128)[:, None]
    bkd = t5_bucket_np(np.maximum(dd, 0))
    out["c_g"] = np.ascontiguousarray(np.where((dd >= 0)[None], tab[bkd].transpose(2, 0, 1), f32(NEG)).astype(f32))
    lam4 = np.concatenate([inp["c_lambda_q1"][0], inp["c_lambda_k1"][0], inp["c_lambda_q2"][0], inp["c_lambda_k2"][0]])
    out["c_lam"] = np.ascontiguousarray(np.broadcast_to(lam4[None, :], (128, 256))).astype(f32)
    out["hgains"] = hg
    return out


def tile_fm(w):
    K, N = w.shape
    return np.ascontiguousarray(w.reshape(K // 128, 128, N // 128, 128).transpose(2, 1, 0, 3)).reshape(N // 128, 128, K)


def t5_bucket_np(dist):
    d = np.maximum(dist, 1).astype(np.float32)
    large = 16 + (np.log(d / np.float32(16)) / np.float32(np.log(2048 / 16)) * np.float32(16)).astype(np.int32)
    return np.where(dist < 16, dist, np.minimum(large, 31)).astype(np.int64)


def host_x(x):
    b = x.shape[0]
    return np.ascontiguousarray(x.reshape(b, SEQ, 8, 128).transpose(0, 2, 3, 1))


def host_y(yT):
    b = yT.shape[0]
    return np.ascontiguousarray(yT.transpose(0, 3, 1, 2)).reshape(b, SEQ, DM)


def kernel(**inputs):
    x = np.asarray(inputs["x"], np.float32)
    nb = x.shape[0]
    per = nb // NCORES
    prog = Prog(per)
    nc = prog.build()
    wts = host_weights({k: np.asarray(v) for k, v in inputs.items()})
    xT = host_x(x)
    in_maps = []
    for c in range(NCORES):
        m = dict(wts)
        m["xT"] = xT[c * per:(c + 1) * per]
        in_maps.append(m)
    res = run_bass_kernel_spmd(nc, in_maps, core_ids=list(range(NCORES)))
    yT = np.concatenate([r["yT"] for r in res.results], axis=0)
    return host_y(yT)
```

```python
import math
import numpy as np
from contextlib import ExitStack
import concourse.bass as bass
import concourse.mybir as mybir
from concourse.bass_utils import run_bass_kernel_spmd

F32 = mybir.dt.float32
BF16 = mybir.dt.bfloat16
AF = mybir.ActivationFunctionType
ALU = mybir.AluOpType

NCORES = 8
SEQ = 2048
DM = 1024
DFF = 2816
NJ = DFF // 128
NTB = SEQ // 512
EPS = 1e-6
NEG = -30000.0
HG_D_Q, HG_D_K, HG_D_SINK = 0, 1, 2
HG_A_Q, HG_A_K = 18, 21
HG_B_QA, HG_B_KVA, HG_B_Q, HG_B_K, HG_B_KPE = 24, 27, 29, 30, 31
HG_C_Q, HG_C_K, HG_C_SUB = 32, 33, 34
NHG = 64


class Buf:
    __slots__ = ("name", "w", "r", "excl")

    def __init__(self, name, excl=False):
        self.name = name
        self.w = None
        self.r = []
        self.excl = excl


ATTACH_WAITS = True


class _FirstIns:
    def __init__(self, eng):
        self._eng = eng
        self.first = None

    def __getattr__(self, name):
        f = getattr(self._eng, name)

        def w(*a, **k):
            r = f(*a, **k)
            if self.first is None:
                self.first = r
            return r
        return w


class Sync:
    ENG = ("pe", "act", "dve", "pool", "sp")

    def __init__(self, nc, es):
        self.nc = nc
        self.es = es
        self.eng = {"pe": nc.tensor, "act": nc.scalar, "dve": nc.vector, "pool": nc.gpsimd, "sp": nc.sync}
        self.sems = {}
        self.cnt = {}
        self.seen = {e: {} for e in self.ENG}
        for e in self.ENG:
            self.sems[e] = es.enter_context(nc.semaphore("s_" + e))
            self.cnt[e] = 0
        self.nwaits = 0
        self.nops = 0
        self.snap = {}

    def dma_sem(self, name):
        key = "d_" + name
        self.sems[key] = self.es.enter_context(self.nc.semaphore(key))
        self.cnt[key] = 0
        return key

    def _deps(self, e, reads, writes, relaxed=False):
        deps = {}

        def add(ev, same_ok):
            if ev is None:
                return
            k, v = ev
            if k == e and not same_ok:
                return
            if deps.get(k, 0) < v:
                deps[k] = v

        for b in reads:
            add(b.w, True)
            if b.excl:
                for ev in b.r:
                    add(ev, False)
        for b in writes:
            add(b.w, not relaxed)
            for ev in b.r:
                add(ev, not relaxed)
        return deps

    def _learn(self, e, k, v):
        seen = self.seen[e]
        if seen.get(k, 0) < v:
            seen[k] = v
        sn = self.snap.get((k, v))
        if sn:
            for kk, vv in sn.items():
                if seen.get(kk, 0) < vv:
                    seen[kk] = vv

    def _wait(self, e, deps):
        eng = self.eng[e]
        seen = self.seen[e]
        for k, v in sorted(deps.items(), key=lambda kv: -len(self.snap.get(kv, ()))):
            if seen.get(k, 0) < v:
                eng.wait_ge(self.sems[k], v)
                self._learn(e, k, v)
                self.nwaits += 1

    def _record(self, ev, reads, writes):
        for b in reads:
            b.r.append(ev)
            if len(b.r) > 64:
                best = {}
                for k, v in b.r:
                    if best.get(k, 0) < v:
                        best[k] = v
                b.r = list(best.items())
        for b in writes:
            b.w = ev
            b.r = []

    def op(self, e, reads, writes, fn, relaxed=False):
        deps = self._deps(e, reads, writes, relaxed)
        seen = self.seen[e]
        pend = sorted([(k, v) for k, v in deps.items() if seen.get(k, 0) < v],
                      key=lambda kv: -len(self.snap.get(kv, ())))
        attach = pend.pop(0) if (pend and ATTACH_WAITS) else None
        if attach is not None:
            self._learn(e, attach[0], attach[1])
        self._wait(e, dict(pend))
        cap = _FirstIns(self.eng[e])
        ins = fn(cap)
        if attach is not None:
            cap.first._wait_ge(self.sems[attach[0]], attach[1])
        ins.then_inc(self.sems[e], 1)
        self.cnt[e] += 1
        self.nops += 1
        ev = (e, self.cnt[e])
        self.snap[ev] = dict(seen)
        self._record(ev, reads, writes)
        return ev

    def dma(self, q, out, in_, reads, writes, key):
        self._wait(q, self._deps(q, reads, writes))
        self.eng[q].dma_start(out=out, in_=in_).then_inc(self.sems[key], 16)
        self.cnt[key] += 16
        ev = (key, self.cnt[key])
        self.snap[ev] = dict(self.seen[q])
        self._record(ev, reads, writes)
        return ev

    def wait_all(self, e, bufs):
        self._wait(e, self._deps(e, bufs, bufs))


def fview(ap, shape):
    if len(shape) == 1:
        return ap
    if len(shape) == 2:
        return ap.rearrange("p (a b) -> p a b", a=shape[0])
    return ap.rearrange("p (a b c) -> p a b c", a=shape[0], b=shape[1])


class WRing:
    SLOT = 2048

    def __init__(self, kb, nslots):
        self.kb = kb
        self.ns = nslots
        self.t = kb.sb("wring", [128, nslots * self.SLOT], BF16)
        self.bufs = [Buf("wslot%d" % i) for i in range(nslots)]
        self.keys = [kb.T.dma_sem("w%d" % i) for i in range(nslots)]
        self.plan = []
        self.issued = 0
        self.consumed = 0
        self.released = 0

    def add(self, tag, dram_ap, shape):
        n = int(np.prod(shape))
        assert n <= self.SLOT, (tag, shape)
        self.plan.append((tag, dram_ap, tuple(shape), n))

    def _view(self, i):
        s = i % self.ns
        _, _, shape, n = self.plan[i]
        return fview(self.t[:, s * self.SLOT: s * self.SLOT + n], shape)

    def pump(self):
        T = self.kb.T
        while self.issued < len(self.plan) and self.issued - self.ns < self.released:
            i = self.issued
            s = i % self.ns
            T.dma("pool", self._view(i), self.plan[i][1], [], [self.bufs[s]], self.keys[s])
            self.issued += 1

    def get(self, tag):
        i = self.consumed
        assert self.plan[i][0] == tag, (self.plan[i][0], tag)
        self.pump()
        assert self.issued > i, "weight ring: too many tiles held"
        self.consumed += 1
        return self._view(i), self.bufs[i % self.ns]

    def done(self, n=1):
        self.released += n
        assert self.released <= self.consumed
        self.pump()


class KB:
    def __init__(self, nc, es):
        self.nc = nc
        self.es = es
        self.T = Sync(nc, es)

    def sb(self, name, shape, dt, es=None):
        self.uid = getattr(self, "uid", 0) + 1
        return (es or self.es).enter_context(self.nc.sbuf_tensor("s%d_%s" % (self.uid, name), list(shape), dt))

    def ps(self, name):
        return self.es.enter_context(self.nc.psum_tensor(name, [128, 512], F32))

    def din(self, name, shape, dt=F32):
        self.in_names = getattr(self, "in_names", []) + [name]
        return self.nc.dram_tensor(name, list(shape), dt, kind="ExternalInput").ap()

    def dout(self, name, shape, dt=F32):
        return self.nc.dram_tensor(name, list(shape), dt, kind="ExternalOutput").ap()


class Prog:
    def __init__(self, nseq, layers=(0, 1, 2, 3), mixers=True, ffns=True):
        self.nseq = nseq
        self.layers = tuple(layers)
        self.mixers = mixers
        self.ffns = ffns

    def build(self):
        nc = bass.Bass("TRN2", target_bir_lowering=False)
        self.nc = nc
        with ExitStack() as es:
            kb = KB(nc, es)
            self.kb = kb
            self.T = kb.T
            self._declare_io()
            self._alloc()
            self._plan_weights()
            self._load_consts()
            for s in range(self.nseq):
                self._run_seq(s)
            self.T.wait_all("sp", self.Xb)
        return nc

    def _declare_io(self):
        kb = self.kb
        ns = self.nseq
        self.d_x = kb.din("xT", [ns, 8, 128, SEQ])
        self.d_y = kb.dout("yT", [ns, 8, 128, SEQ])
        self.d_wup = kb.din("wup", [4, NJ, 128, 2048])
        self.d_wdn = kb.din("wdn", [4, 2, 8, 128, 11 * 128])
        self.d_gains = kb.din("gains", [128, 64])
        self.d_convp = kb.din("convp", [128, 4 * 4 * NJ])
        self.d_hg = kb.din("hgains", [128, NHG])
        if 0 in self.layers and self.mixers:
            self.d_av = kb.din("a_v", [2, 3, 128, 2048])
            self.d_aqk = kb.din("a_qk", [8, 3, 128, 1024])
            self.d_ao = kb.din("a_o", [8, 128, 512])
            self.d_g0 = kb.din("a_g0", [3, 8, 128, 256])
        if 1 in self.layers and self.mixers:
            self.d_bin = kb.din("b_in", [3, 128, 2048])
            self.d_bk = kb.din("b_kup", [128, 2048])
            self.d_bv = kb.din("b_vup", [128, 2048])
            self.d_bq = kb.din("b_qup", [4, 128, 1536])
            self.d_bo = kb.din("b_o", [8, 128, 1024])
            self.d_rope = kb.din("b_rope", [128, SEQ])
            self.d_tri = kb.din("b_tri", [128, 128])
        if 2 in self.layers and self.mixers:
            self.d_cqk = kb.din("c_qk", [8, 128, 2048])
            self.d_cv = kb.din("c_v", [8, 128, 1024])
            self.d_co = kb.din("c_o", [8, 128, 1024])
            self.d_cg = kb.din("c_g", [16, 128, SEQ])
            self.d_clam = kb.din("c_lam", [128, 256])
        if 3 in self.layers and self.mixers:
            self.d_dk = kb.din("d_k", [128, 2048])
            self.d_dv = kb.din("d_v", [128, 1024])
            self.d_dq = kb.din("d_q", [4, 128, 2048])
            self.d_do = kb.din("d_o", [8, 128, 1024])
            self.d_g3 = kb.din("d_g3", [16, 128, 256])

    def _alloc(self):
        kb = self.kb
        T = self.T
        self.X = kb.sb("X", [128, 8, SEQ], F32)
        self.Xb = [Buf("X%d" % c) for c in range(8)]
        self.Xk = [T.dma_sem("x%d" % c) for c in range(8)]
        self.h = kb.sb("h", [128, 8, SEQ], BF16)
        self.hb = [Buf("h%d" % tb) for tb in range(NTB)]
        self.W = WRing(kb, 5)
        self.gains = kb.sb("gains", [128, 64], F32)
        self.convp = kb.sb("convp", [128, 4 * 4 * NJ], F32)
        self.cb = Buf("consts")
        self.ck = T.dma_sem("consts")
        self.hg = kb.sb("hgains", [128, NHG], F32)
        self.g3k = [T.dma_sem("g3_%d" % i) for i in range(2)]
        self.xk = {n: T.dma_sem(n) for n in ("lam", "rope", "tri")}
        self.ones = kb.sb("ones", [128, 128], BF16)
        self.epsc = kb.sb("epsc", [128, 1], F32)
        self.zeros = kb.sb("zeros", [128, 512], BF16)
        self.bd64 = kb.sb("bd64", [128, 128], BF16)
        self.bank = [kb.ps("bank%d" % i) for i in range(8)]
        self.bankb = [Buf("bank%d" % i, excl=True) for i in range(8)]

    def _plan_weights(self):
        for s in range(self.nseq):
            for l in self.layers:
                if self.mixers:
                    self._plan_mixer(l)
                if self.ffns:
                    self._plan_ffn(l)

    def _load_consts(self):
        T = self.T
        T.dma("sp", self.gains[:], self.d_gains, [], [self.cb], self.ck)
        T.dma("sp", self.convp[:], self.d_convp, [], [self.cb], self.ck)
        T.dma("sp", self.hg[:], self.d_hg, [], [self.cb], self.ck)
        T.op("dve", [], [self.cb], lambda e: e.memset(self.ones[:], 1.0))
        T.op("dve", [], [self.cb], lambda e: e.memset(self.epsc[:], EPS))
        T.op("dve", [], [self.cb], lambda e: e.memset(self.zeros[:], 0.0))
        T.op("dve", [], [self.cb], lambda e: e.memset(self.bd64[:], 0.0))
        T.op("dve", [], [self.cb], lambda e: e.memset(self.bd64[0:64, 0:64], 1.0))
        T.op("dve", [], [self.cb], lambda e: e.memset(self.bd64[64:128, 64:128], 1.0))

    def barrier(self):
        T = self.T
        comp = ("pe", "act", "dve")
        for e in comp + ("sp",):
            T._wait(e, {k: T.cnt[k] for k in comp if T.cnt[k] > 0})

    def _run_seq(self, s):
        T = self.T
        if s == 0:
            for c in range(8):
                T.dma("sp", self.X[:, c, :], self.d_x[s, c], [], [self.Xb[c]], self.Xk[c])
        stored = set()

        def x_final(c):
            T.dma("sp", self.d_y[s, c], self.X[:, c, :], [self.Xb[c]], [], self.Xk[c])
            if s + 1 < self.nseq:
                T.dma("sp", self.X[:, c, :], self.d_x[s + 1, c], [], [self.Xb[c]], self.Xk[c])
            stored.add(c)
        for li, l in enumerate(self.layers):
            last = (li == len(self.layers) - 1)
            if self.mixers:
                self._mixer(l)
                self.barrier()
            if self.ffns:
                self._ffn(l, x_final if last else None)
                self.barrier()
        for c in range(8):
            if c not in stored:
                x_final(c)

    def _norm(self, gcol):
        T = self.T
        NBS = (7, 6)
        es_ = ExitStack()
        sq = [self.kb.sb("sq%d" % i, [128, 8, 512], BF16, es_) for i in range(2)]
        sqb = [Buf("sq%d" % i) for i in range(2)]
        rstd = [self.kb.sb("rstd%d" % i, [128, 512], F32, es_) for i in range(2)]
        rstdb = [Buf("rstd%d" % i) for i in range(2)]

        def A(tb):
            NB = NBS[tb % 2]
            ts = slice(tb * 512, (tb + 1) * 512)
            for c in range(8):
                T.op("act", [self.Xb[c]], [sqb[tb % 2]],
                     lambda e: e.activation(sq[tb % 2][:, c, :], self.X[:, c, ts], AF.Square), relaxed=(c > 0))
            self.mm_acc(self.bank[NB][:], [(self.ones[:], sq[tb % 2][:, c, :]) for c in range(8)],
                        [sqb[tb % 2], self.cb], self.bankb[NB])

        def B(tb):
            NB = NBS[tb % 2]
            ts = slice(tb * 512, (tb + 1) * 512)
            r, rb = rstd[tb % 2], rstdb[tb % 2]
            T.op("act", [self.bankb[NB], self.cb], [rb],
                 lambda e: e.activation(r[:], self.bank[NB][:], AF.Ln, bias=self.epsc[:, 0:1], scale=1.0 / DM))
            T.op("act", [rb], [rb], lambda e: e.activation(r[:], r[:], AF.Exp, scale=-0.5))
            for c in range(8):
                T.op("dve", [self.Xb[c], rb, self.cb], [self.hb[tb]],
                     lambda e: e.scalar_tensor_tensor(self.h[:, c, ts], self.X[:, c, ts],
                                                      self.gains[:, gcol + c: gcol + c + 1], r[:], ALU.mult, ALU.mult),
                     relaxed=(c > 0))
        self.pipe(NTB, A, B, 1)
        self.barrier()
        es_.close()

    def _plan_ffn(self, l):
        W = self.W
        for g in range(2):
            for jj in range(11):
                j = 11 * g + jj
                W.add("up%d_%d" % (l, j), self.d_wup[l, j], (8, 2, 128))
            for o in range(8):
                W.add("dn%d_%d_%d" % (l, g, o), self.d_wdn[l, g, o], (11, 128))

    def _ffn(self, l, x_final=None):
        T = self.T
        kb = self.kb
        self._norm(32 + l * 8)
        DBG = ""
        if DBG == "norm":
            return
        with ExitStack() as es:
            sb = lambda n, shp, dt: kb.sb(n, shp, dt, es)
            A = sb("A", [128, 11, SEQ], BF16)
            Ab = [[Buf("A%d_%d" % (j, tb)) for tb in range(NTB)] for j in range(11)]
            Gs = [sb("Gs%d" % i, [128, 2 + SEQ], F32) for i in range(2)]
            Gsb = [[Buf("Gs%d_%d" % (i, tb)) for tb in range(NTB)] for i in range(2)]
            C = [sb("C%d" % i, [128, 512], F32) for i in range(2)]
            Cb = [Buf("C%d" % i) for i in range(2)]
            S = [sb("S%d" % i, [128, 512], F32) for i in range(2)]
            Sb = [Buf("S%d" % i) for i in range(2)]
            for i in range(2):
                T.op("dve", [], [Gsb[i][0]], lambda e: e.memset(Gs[i][:, 0:2], 0.0))
            cp = lambda k, j: self.convp[:, (l * 4 + k) * NJ + j: (l * 4 + k) * NJ + j + 1]
            GB, UB, DB = (0, 1), (2, 3), (4, 5)
            it = 0
            dn = 0
            for g in range(2):
                for jj in range(11):
                    j = 11 * g + jj
                    wt, wb = self.W.get("up%d_%d" % (l, j))
                    gs = Gs[j % 2]
                    gsb = Gsb[j % 2]
                    for tb in range(NTB):
                        if DBG == "nogu":
                            break
                        ts = slice(tb * 512, (tb + 1) * 512)
                        bg, bu = GB[it % 2], UB[it % 2]
                        Cc, Ccb = C[it % 2], Cb[it % 2]
                        Ss, Ssb = S[it % 2], Sb[it % 2]
                        it += 1

                        def mmg(e, bnk, gi):
                            ins = None
                            for c in range(8):
                                ins = e.matmul(self.bank[bnk][:], wt[:, c, gi, :], self.h[:, c, ts],
                                               start=(c == 0), stop=(c == 7))
                            return ins
                        T.op("pe", [wb, self.hb[tb]], [self.bankb[bg]], lambda e: mmg(e, bg, 0))
                        T.op("pe", [wb, self.hb[tb]], [self.bankb[bu]], lambda e: mmg(e, bu, 1))
                        if "noact1" not in DBG:
                          T.op("act", [self.bankb[bg]], [gsb[tb]],
                             lambda e: e.copy(gs[:, 2 + tb * 512: 2 + (tb + 1) * 512], self.bank[bg][:]))
                        if "nots" not in DBG:
                          T.op("dve", [gsb[tb], self.cb], [Ccb],
                             lambda e: e.tensor_scalar(Cc[:], gs[:, 2 + tb * 512: 2 + (tb + 1) * 512], cp(2, j), cp(3, j),
                                                       ALU.mult, ALU.add))
                        rd = [gsb[tb], gsb[max(tb - 1, 0)]]
                        if "notaps" not in DBG:
                          T.op("dve", rd + [Ccb, self.cb], [Ccb],
                             lambda e: e.scalar_tensor_tensor(Cc[:], gs[:, 1 + tb * 512: 1 + (tb + 1) * 512], cp(1, j),
                                                              Cc[:], ALU.mult, ALU.add))
                          T.op("dve", rd + [Ccb, self.cb], [Ccb],
                             lambda e: e.scalar_tensor_tensor(Cc[:], gs[:, tb * 512: (tb + 1) * 512], cp(0, j),
                                                              Cc[:], ALU.mult, ALU.add))
                        if "nosilu" not in DBG:
                          T.op("act", [Ccb], [Ssb], lambda e: e.activation(Ss[:], Cc[:], AF.Silu))
                        if "nomul" not in DBG:
                          T.op("dve", [Ssb, self.bankb[bu]], [Ab[jj][tb]],
                             lambda e: e.tensor_tensor(A[:, jj, ts], Ss[:], self.bank[bu][:], ALU.mult))
                    self.W.done()
                for o in range(8):
                    wt, wb = self.W.get("dn%d_%d_%d" % (l, g, o))
                    for tb in range(NTB):
                        if "nodn" in DBG or "nogu" in DBG:
                            break
                        ts = slice(tb * 512, (tb + 1) * 512)
                        bd = DB[dn % 2]
                        dn += 1

                        def mmd(e):
                            ins = None
                            for jj in range(11):
                                ins = e.matmul(self.bank[bd][:], wt[:, jj, :], A[:, jj, ts],
                                               start=(jj == 0), stop=(jj == 10))
                            return ins
                        T.op("pe", [wb] + [Ab[jj][tb] for jj in range(11)], [self.bankb[bd]], mmd)
                        T.op("dve", [self.bankb[bd], self.Xb[o]], [self.Xb[o]],
                             lambda e: e.tensor_tensor(self.X[:, o, ts], self.X[:, o, ts], self.bank[bd][:], ALU.add))
                    self.W.done()
                    if g == 1 and x_final is not None:
                        x_final(o)
            self.barrier()

    def mm_acc(self, out_ap, pairs, reads, wbuf):
        def f(e):
            ins = None
            n = len(pairs)
            for i, (lt, rh) in enumerate(pairs):
                ins = e.matmul(out_ap, lt, rh, start=(i == 0), stop=(i == n - 1))
            return ins
        return self.T.op("pe", reads, [wbuf], f)

    @staticmethod
    def pipe(n, A, B, la, delay=2):
        pend = []
        for i in range(n + la):
            if i < n:
                A(i)
            if i >= la:
                fin = B(i - la)
                while pend and pend[0][0] <= i:
                    pend.pop(0)[1]()
                if fin is not None:
                    pend.append((i + delay, fin))
        for _, f in pend:
            f()

    def job_head(self, pairs, reads, rows, dsum, gain_ap, out_ap, out_buf, dil=1):
        T = self.T
        pv = (lambda a: a) if dil == 1 else (lambda a: a.rearrange("p (i r) -> p i r", r=dil))

        def post(b, r, rb):
            T.op("dve", [self.bankb[b], rb, self.cb], [out_buf],
                 lambda e: e.scalar_tensor_tensor(out_ap, pv(self.bank[b][0:rows, :]), gain_ap, pv(r[0:rows, :]),
                                                  ALU.mult, ALU.mult))
        return {"mm": lambda b: self.mm_acc(self.bank[b][0:rows, :], pairs, reads, self.bankb[b]),
                "sq_rows": dsum, "rrows": rows, "nd": dsum,
                "sums": lambda sq: [(self.ones[0:dsum, 0:rows], sq[0:dsum, :])], "post": post}

    def norm_pipeline(self, jobs, pbanks, sumb, hn):
        T = self.T
        n, nb = len(jobs), len(pbanks)

        def A(i):
            j, b = jobs[i], pbanks[i % nb]
            j["mm"](b)
            sq, sqb = hn["sqh"][i % 3], hn["sqhb"][i % 3]
            sr = j["sq_rows"]
            T.op("act", [self.bankb[b]], [sqb], lambda e: e.activation(sq[0:sr, :], self.bank[b][0:sr, :], AF.Square))

        def B(i):
            j, b = jobs[i], pbanks[i % nb]
            sq, sqb = hn["sqh"][i % 3], hn["sqhb"][i % 3]
            r, rb = hn["r"][i % 2], hn["rb"][i % 2]
            rr = j["rrows"]
            self.mm_acc(self.bank[sumb][0:rr, :], j["sums"](sq), [sqb, self.cb] + j.get("sum_reads", []), self.bankb[sumb])
            T.op("act", [self.bankb[sumb], self.cb], [rb],
                 lambda e: e.activation(r[0:rr, :], self.bank[sumb][0:rr, :], AF.Ln, bias=self.epsc[0:rr, 0:1], scale=1.0 / j["nd"]))
            T.op("act", [rb], [rb], lambda e: e.activation(r[0:rr, :], r[0:rr, :], AF.Exp, scale=-0.5))
            j["post"](b, r, rb)
        self.pipe(n, A, B, 2 if nb >= 4 else 1)

    def recip(self, out_ap, in_ap, reads, buf, bias=None):
        T = self.T
        if bias is None:
            T.op("act", reads, [buf], lambda e: e.activation(out_ap, in_ap, AF.Ln))
        else:
            T.op("act", reads, [buf], lambda e: e.activation(out_ap, in_ap, AF.Ln, bias=bias, scale=1.0))
        T.op("act", [buf], [buf], lambda e: e.activation(out_ap, out_ap, AF.Exp, scale=-1.0))

    def alloc_hn(self, es):
        kb = self.kb
        return {"sqh": [kb.sb("sqh%d" % i, [128, 512], BF16, es) for i in range(3)], "sqhb": [Buf("sqh%d" % i) for i in range(3)],
                "r": [kb.sb("hr%d" % i, [128, 512], F32, es) for i in range(2)], "rb": [Buf("hr%d" % i) for i in range(2)]}

    def out_proj(self, tagfmt, AO, AOb, nK):
        T = self.T
        it = 0
        for q in range(4):
            wt, wb = self.W.get(tagfmt % q)
            for i in range(2):
                o = 2 * q + i
                for tb in range(NTB):
                    ts = slice(tb * 512, (tb + 1) * 512)
                    bd = it % 2
                    it += 1
                    self.mm_acc(self.bank[bd][:], [(wt[:, i, a, :], AO[:, a, ts]) for a in range(nK)],
                                [wb] + [AOb[a][tb] for a in range(nK)], self.bankb[bd])
                    T.op("dve", [self.bankb[bd], self.Xb[o]], [self.Xb[o]],
                         lambda e: e.tensor_tensor(self.X[:, o, ts], self.X[:, o, ts], self.bank[bd][:], ALU.add))
            self.W.done()

    def _plan_mixer(self, l):
        getattr(self, "_plan_mix%d" % l)()

    def _mixer(self, l):
        self._norm(l * 8)
        getattr(self, "_mix%d" % l)()


    def _plan_mix0(self):
        W = self.W
        for hq in range(2):
            for g in range(3):
                W.add("a_v%d_%d" % (hq, g), self.d_av[hq, g], (8, 256))
            for hd in range(hq * 4, hq * 4 + 4):
                for g in range(3):
                    W.add("a_qk%d_%d" % (hd, g), self.d_aqk[hd, g], (2, 8, 64))
        for q in range(4):
            W.add("a_o%d" % q, self.d_ao[2 * q:2 * q + 2].rearrange("t p k -> p t k"), (2, 4, 128))

    def _mix0(self):
        T = self.T
        kb = self.kb
        HG = self.hg
        DIL = (1, 4, 16)
        with ExitStack() as es:
            sb = lambda n, shp, dt: kb.sb(n, shp, dt, es)
            AO = sb("AO", [128, 4, SEQ], BF16)
            AOb = [[Buf("AO%d_%d" % (a, tb)) for tb in range(NTB)] for a in range(4)]
            VA = [sb("VA%d" % i, [128, 16, 384], BF16) for i in range(3)]
            VAb = [[Buf("VA%d_%d" % (i, q)) for q in range(8)] for i in range(3)]
            QT = [sb("QT%d" % i, [128, SEQ], BF16) for i in range(2)]
            QTb = [Buf("QT%d" % i) for i in range(2)]
            KT = [sb("KT%d" % i, [128, SEQ], BF16) for i in range(2)]
            KTb = [Buf("KT%d" % i) for i in range(2)]
            G3 = [sb("G0_%d" % i, [128, 256], F32) for i in range(2)]
            G3b = [Buf("G0_%d" % i) for i in range(2)]
            Sb_ = [sb("Sb%d" % i, [128, 256], F32) for i in range(3)]
            Sbb = [Buf("Sb%d" % i) for i in range(3)]
            PT = [sb("PT%d" % i, [128, 256], BF16) for i in range(3)]
            PTb = [Buf("PT%d" % i) for i in range(3)]
            R = [sb("R%d" % i, [128, 512], F32) for i in range(2)]
            Rb = [Buf("R%d" % i) for i in range(2)]
            hn = self.alloc_hn(es)
            PB, PBS, SUMB, STB, ACB = 0, (0, 2, 3), 1, (2, 3, 0, 1), (4, 5, 6, 7)
            for i in range(2):
                T.op("dve", [], [QTb[i]], lambda e: e.memset(QT[i][64:128, :], 0.0))
                T.op("dve", [], [KTb[i]], lambda e: e.memset(KT[i][64:128, :], 0.0))
            for i in range(3):
                T.op("dve", [], VAb[i], lambda e: e.memset(
                    VA[i][:].rearrange("p k (pr x) -> p k pr x", pr=2)[:, :, :, 64:128], 1.0))
            si = 0
            gi = 0
            for hq in range(2):
                for g in range(3):
                    dil = DIL[g]
                    L = SEQ // dil
                    wv, wvb = self.W.get("a_v%d_%d" % (hq, g))
                    for k2 in range(8):
                        def mmv(e):
                            ins = None
                            for i in range(2):
                                kc = 2 * k2 + i
                                n0 = kc * 128
                                r, i0 = n0 // L, n0 % L
                                t0 = i0 * dil + r
                                for c in range(8):
                                    ins = e.matmul(self.bank[PB][:, i * 256:(i + 1) * 256],
                                                   self.h[:, c, t0:min(t0 + 128 * dil, SEQ):dil], wv[:, c, :],
                                                   start=(c == 0), stop=(c == 7))
                            return ins
                        T.op("pe", [wvb] + self.hb, [self.bankb[PB]], mmv)
                        bv = self.bank[PB][:].rearrange("p (i h d) -> p i h d", i=2, h=4)
                        vv = VA[g][:, 2 * k2:2 * k2 + 2, :].rearrange("p i (pr x) -> p i pr x", pr=2)
                        for par in range(2):
                            T.op("act", [self.bankb[PB]], [VAb[g][k2]],
                                 lambda e: e.copy(vv[:, :, :, par * 128:par * 128 + 64], bv[:, :, par:4:2, :]))
                    self.W.done()
                for hl in range(4):
                    hd = hq * 4 + hl
                    urow = slice(0, 64) if hd % 2 == 0 else slice(64, 128)
                    drow = slice(64, 128) if hd % 2 == 0 else slice(0, 64)
                    for b in ACB:
                        T.op("pe", [self.cb], [self.bankb[b]],
                             lambda e: e.matmul(self.bank[b][:], self.zeros[:, 0:128], self.zeros[:], start=True, stop=False))
                    for g in range(3):
                        dil = DIL[g]
                        L = SEQ // dil
                        wqk, wqkb = self.W.get("a_qk%d_%d" % (hd, g))
                        qt, qtb = QT[gi % 2], QTb[gi % 2]
                        kt, ktb = KT[gi % 2], KTb[gi % 2]
                        g3, g3b = G3[gi % 2], G3b[gi % 2]
                        T.dma("sp", g3[:], self.d_g0[g, hd], [], [g3b], self.g3k[gi % 2])
                        gi += 1
                        jobs = []
                        for which, dst, dstb, gcol in ((0, qt, qtb, HG_A_Q + g), (1, kt, ktb, HG_A_K + g)):
                            dv = dst[0:64, :].rearrange("p (r i) -> p i r", r=dil) if dil > 1 else None
                            for tb in range(NTB):
                                ts = slice(tb * 512, (tb + 1) * 512)
                                oap = dst[0:64, ts] if dil == 1 else dv[:, tb * 512 // dil:(tb + 1) * 512 // dil, :]
                                jobs.append(self.job_head([(wqk[:, which, c, :], self.h[:, c, ts]) for c in range(8)],
                                                          [wqkb, self.hb[tb]], 64, 64, HG[0:64, gcol:gcol + 1], oap, dstb, dil=dil))
                        self.norm_pipeline(jobs, PBS, SUMB, hn)
                        self.W.done()
                        cpc = L // 128
                        base = si
                        si += 16

                        def A(kc):
                            n0 = kc * 128
                            ci = kc % cpc
                            N = 256 if ci < cpc - 1 else 128
                            bs = STB[(base + kc) % len(STB)]
                            self.mm_acc(self.bank[bs][:, 0:N], [(kt[:, n0:n0 + 128], qt[:, n0:n0 + N])],
                                        [ktb, qtb], self.bankb[bs])

                        def B(kc):
                            r, ci = kc // cpc, kc % cpc
                            N = 256 if ci < cpc - 1 else 128
                            bs = STB[(base + kc) % len(STB)]
                            sb_i = (base + kc) % 3
                            T.op("dve", [self.bankb[bs], g3b], [Sbb[sb_i]],
                                 lambda e: e.scalar_tensor_tensor(Sb_[sb_i][:, 0:N], self.bank[bs][:, 0:N], 0.125, g3[:, 0:N],
                                                                  ALU.mult, ALU.add))
                            T.op("act", [Sbb[sb_i]], [PTb[sb_i]],
                                 lambda e: e.activation(PT[sb_i][:, 0:N], Sb_[sb_i][:, 0:N], AF.Exp))
                            lhs = VA[g][:, kc, (hl // 2) * 192 + (hl % 2) * 64:(hl // 2) * 192 + (hl % 2) * 64 + 128]
                            last = (g == 2 and kc == 15)
                            for half in range(N // 128):
                                ib = ci + half
                                if dil == 1:
                                    dsts = [(ACB[ib // 4], slice((ib % 4) * 128, (ib % 4) * 128 + 128), slice(half * 128, half * 128 + 128))]
                                elif dil == 4:
                                    dsts = [(ACB[ib], slice(r, 512, 4), slice(half * 128, half * 128 + 128))]
                                else:
                                    dsts = [(ACB[b4], slice(r, 512, 16), slice(b4 * 32, b4 * 32 + 32)) for b4 in range(4)]
                                for (bk, osl, psl) in dsts:
                                    T.op("pe", [VAb[g][kc // 2], PTb[sb_i]], [self.bankb[bk]],
                                         lambda e: e.matmul(self.bank[bk][:, osl], lhs, PT[sb_i][:, psl], start=False, stop=last))
                        self.pipe(16, A, B, 3)
                    for m in range(4):
                        ts = slice(m * 512, (m + 1) * 512)
                        bk = ACB[m]
                        r_, rb = R[m % 2], Rb[m % 2]
                        self.recip(r_[drow, :], self.bank[bk][drow, :], [self.bankb[bk]], rb)
                        T.op("dve", [self.bankb[bk], rb], [AOb[hd // 2][m]],
                             lambda e: e.tensor_tensor(AO[urow, hd // 2, ts], self.bank[bk][urow, :], r_[drow, :], ALU.mult))
            self.out_proj("a_o%d", AO, AOb, 4)
            self.barrier()


    def _plan_mix1(self):
        W = self.W
        for i in range(3):
            W.add("b_in%d" % i, self.d_bin[i], (2048,))
        W.add("b_kup", self.d_bk, (16, 2, 64))
        W.add("b_vup", self.d_bv, (8, 2, 128))
        for g in range(4):
            W.add("b_qup%d" % g, self.d_bq[g], (4, 3, 128))
        for q in range(4):
            W.add("b_o%d" % q, self.d_bo[2 * q:2 * q + 2].rearrange("t p k -> p t k"), (2, 8, 128))

    def causal_attention(self, nheads_iter, QKfn, VAfn, scale, Kd, AO, AOb, bias_fn, banks, scr):
        raise NotImplementedError

    def _mix1(self):
        T = self.T
        kb = self.kb
        HG = self.hg
        SC = 96 ** -0.5
        with ExitStack() as es:
            sb = lambda n, shp, dt: kb.sb(n, shp, dt, es)
            AO = self.h
            AOb = [[Buf("AO%d_%d" % (a, tb)) for tb in range(NTB)] for a in range(8)]
            CQN = sb("CQN", [128, 3, SEQ], BF16)
            CQNb = [Buf("CQN%d" % tb) for tb in range(NTB)]
            CKVN = sb("CKVN", [128, 2, SEQ], BF16)
            CKVNb = [Buf("CKVN%d" % tb) for tb in range(NTB)]
            KR64 = sb("KR64", [128, SEQ], BF16)
            KR64b = [Buf("KR64_%d" % tb) for tb in range(NTB)]
            SQPE = sb("SQPE", [128, SEQ], BF16)
            SQPEb = [Buf("SQPE%d" % tb) for tb in range(NTB)]
            ROPE = sb("ROPE", [128, SEQ], F32)
            ROPEb = Buf("ROPE")
            VA = sb("VA", [128, 16, 192], BF16)
            VAb = [Buf("VA_%d" % q) for q in range(4)]
            QT = sb("QT", [128, SEQ], BF16)
            QTb = [Buf("QT%d" % tb) for tb in range(NTB)]
            KT = sb("KT", [128, SEQ], BF16)
            KTb = [Buf("KT%d" % tb) for tb in range(NTB)]
            QH2 = [sb("QH%d" % i, [128, 512], F32) for i in range(2)]
            QH2b = [Buf("QH%d" % i) for i in range(2)]
            TM4 = [sb("TM%d" % i, [128, 512], F32) for i in range(4)]
            TM4b = [Buf("TM%d" % i) for i in range(4)]
            QH, QHb = QH2[0], QH2b[0]
            TM, TMb = TM4[0:2], TM4b[0:2]
            SQ3 = sb("SQ3", [128, 3, 512], BF16)
            SQ3b = Buf("SQ3")
            TRI = sb("TRI", [128, 128], F32)
            TRIb = Buf("TRI")
            Sb_ = [sb("Sb%d" % i, [128, 128], F32) for i in range(3)]
            Sbb = [Buf("Sb%d" % i) for i in range(3)]
            PT = [sb("PT%d" % i, [128, 512], BF16) for i in range(3)]
            PTb = [Buf("PT%d" % i) for i in range(3)]
            R = [sb("R0", [128, 512], F32)] * 2
            Rb = [Buf("R0")] * 2
            hn = self.alloc_hn(es)
            T.dma("sp", ROPE[:], self.d_rope, [], [ROPEb], self.xk["rope"])
            T.dma("sp", TRI[:], self.d_tri, [], [TRIb], self.xk["tri"])
            T.op("dve", [], VAb, lambda e: e.memset(VA[:, :, 64:128], 1.0))
            T.op("dve", [], QTb, lambda e: e.memset(QT[96:128, :], 0.0))
            T.op("dve", [], KTb, lambda e: e.memset(KT[96:128, :], 0.0))
            w0, w0b = self.W.get("b_in0")
            w1, w1b = self.W.get("b_in1")
            w2, w2b = self.W.get("b_in2")
            wtile = [fview(w0[:, 0:1024], (8, 128)), fview(w0[:, 1024:2048], (8, 128)), fview(w1[:, 0:1024], (8, 128)),
                     fview(w1[:, 1024:2048], (8, 128)), fview(w2[:, 0:1024], (8, 128))]
            wpe = fview(w2[:, 1024:1536], (8, 64))
            wbufs = [w0b, w1b, w2b]
            for tb in range(NTB):
                ts = slice(tb * 512, (tb + 1) * 512)
                for (tiles, dst, dstb, gc, nd) in (((0, 1, 2), CQN, CQNb, HG_B_QA, 384), ((3, 4), CKVN, CKVNb, HG_B_KVA, 256)):
                    nt = len(tiles)
                    for i, ti in enumerate(tiles):
                        self.mm_acc(self.bank[i][:], [(wtile[ti][:, c, :], self.h[:, c, ts]) for c in range(8)],
                                    wbufs + [self.hb[tb]], self.bankb[i])
                        T.op("act", [self.bankb[i]], [SQ3b],
                             lambda e: e.activation(SQ3[:, i, :], self.bank[i][:], AF.Square), relaxed=(i > 0))
                    self.mm_acc(self.bank[3][:], [(self.ones[:], SQ3[:, i, :]) for i in range(nt)], [SQ3b, self.cb], self.bankb[3])
                    T.op("act", [self.bankb[3], self.cb], [hn["rb"][0]],
                         lambda e: e.activation(hn["r"][0][:], self.bank[3][:], AF.Ln, bias=self.epsc[:, 0:1], scale=1.0 / nd))
                    T.op("act", [hn["rb"][0]], [hn["rb"][0]], lambda e: e.activation(hn["r"][0][:], hn["r"][0][:], AF.Exp, scale=-0.5))
                    for i in range(nt):
                        T.op("dve", [self.bankb[i], hn["rb"][0], self.cb], [dstb[tb]],
                             lambda e: e.scalar_tensor_tensor(dst[:, i, ts], self.bank[i][:], HG[:, gc + i:gc + i + 1], hn["r"][0][:],
                                                              ALU.mult, ALU.mult), relaxed=(i > 0))
                self.mm_acc(self.bank[4][0:64, :], [(wpe[:, c, :], self.h[:, c, ts]) for c in range(8)],
                            wbufs + [self.hb[tb]], self.bankb[4])
                T.op("act", [self.bankb[4]], [SQPEb[tb]], lambda e: e.activation(SQPE[0:32, ts], self.bank[4][0:32, :], AF.Square))
                T.op("dve", [self.bankb[4], self.cb, ROPEb], [TMb[0]],
                     lambda e: e.scalar_tensor_tensor(TM[0][0:64, :], self.bank[4][0:64, :], HG[0:64, HG_B_KPE:HG_B_KPE + 1],
                                                      ROPE[0:64, ts], ALU.mult, ALU.mult))
                T.op("act", [TMb[0]], [TMb[1]], lambda e: e.copy(TM[1][0:32, :], TM[0][32:64, :]))
                T.op("dve", [TMb[0], TMb[1]], [TMb[1]],
                     lambda e: e.tensor_tensor(TM[1][0:32, :], TM[0][0:32, :], TM[1][0:32, :], ALU.add))
                T.op("act", [TMb[1]], [KR64b[tb]], lambda e: e.copy(KR64[64:96, ts], TM[1][0:32, :]))
            self.W.done(3)
            self.barrier()
            wk_r, wkb_r = self.W.get("b_kup")
            wv_r, wvb_r = self.W.get("b_vup")
            WKV = sb("WKV", [128, 4096], BF16)
            wkb = wvb = Buf("WKV")
            T.op("dve", [wkb_r], [wkb], lambda e: e.tensor_copy(WKV[:, 0:2048], wk_r.rearrange("p a b c -> p (a b c)")))
            T.op("dve", [wvb_r], [wkb], lambda e: e.tensor_copy(WKV[:, 2048:4096], wv_r.rearrange("p a b c -> p (a b c)")))
            self.W.done(2)
            wk = fview(WKV[:, 0:2048], (16, 2, 64))
            wv = fview(WKV[:, 2048:4096], (8, 2, 128))
            PB, SUMB, STB, ACB, VB = (0, 1, 3, 4), 2, (3, 4), (5, 6), 7
            STB = (3, 4, 7, 0)
            si = 0
            ai = 0
            for g in range(4):
                wq, wqb = self.W.get("b_qup%d" % g)
                for hh in range(4):
                    hd = 4 * g + hh
                    urow = slice(0, 64) if hd % 2 == 0 else slice(64, 128)
                    drow = slice(64, 128) if hd % 2 == 0 else slice(0, 64)
                    vo = (hd % 2) * 64
                    if hd % 2 == 0:
                        pr = hd // 2
                        for q4 in range(4):
                            def mmv(e):
                                ins = None
                                for i in range(4):
                                    tc = q4 * 4 + i
                                    for c in range(2):
                                        ins = e.matmul(self.bank[VB][:, i * 128:(i + 1) * 128], CKVN[:, c, tc * 128:(tc + 1) * 128],
                                                       wv[:, pr, c, :], start=(c == 0), stop=(c == 1))
                                return ins
                            T.op("pe", [wvb, CKVNb[q4]], [self.bankb[VB]], mmv)
                            bv = self.bank[VB][:].rearrange("p (i n) -> p i n", i=4)
                            for par in range(2):
                                T.op("act", [self.bankb[VB]], [VAb[q4]],
                                     lambda e: e.copy(VA[:, q4 * 4:(q4 + 1) * 4, par * 128:par * 128 + 64], bv[:, :, par * 64:(par + 1) * 64]))
                    jobs = []
                    for tb in range(NTB):
                        ts = slice(tb * 512, (tb + 1) * 512)

                        def kpost(b, r, rb, tb=tb, ts=ts):
                            T.op("dve", [self.bankb[b], rb, self.cb], [KTb[tb]],
                                 lambda e: e.scalar_tensor_tensor(KT[0:64, ts], self.bank[b][0:64, :], HG[0:64, HG_B_K:HG_B_K + 1],
                                                                  r[0:64, :], ALU.mult, ALU.mult))
                            T.op("dve", [KR64b[tb], rb], [KTb[tb]],
                                 lambda e: e.tensor_tensor(KT[64:96, ts], KR64[64:96, ts], r[64:96, :], ALU.mult), relaxed=True)
                        jobs.append({"mm": (lambda b, ts=ts, tb=tb: self.mm_acc(self.bank[b][0:64, :],
                                                                             [(wk[:, hd, c, :], CKVN[:, c, ts]) for c in range(2)],
                                                                             [wkb, CKVNb[tb]], self.bankb[b])),
                                     "sq_rows": 64, "rrows": 96, "nd": 96, "sum_reads": [SQPEb[tb]],
                                     "sums": (lambda sq, ts=ts: [(self.ones[0:64, 0:96], sq[0:64, :]), (self.ones[0:32, 0:96], SQPE[0:32, ts])]),
                                     "post": kpost})

                        def qpost(b, r, rb, tb=tb, ts=ts):
                            QH, QHb = QH2[tb % 2], QH2b[tb % 2]
                            TM, TMb = TM4[2 * (tb % 2):2 * (tb % 2) + 2], TM4b[2 * (tb % 2):2 * (tb % 2) + 2]
                            T.op("dve", [self.bankb[b], rb, self.cb], [QTb[tb]],
                                 lambda e: e.scalar_tensor_tensor(QT[0:64, ts], self.bank[b][0:64, :], HG[0:64, HG_B_Q:HG_B_Q + 1],
                                                                  r[0:64, :], ALU.mult, ALU.mult))
                            T.op("dve", [self.bankb[b], rb, self.cb], [QHb],
                                 lambda e: e.scalar_tensor_tensor(QH[64:128, :], self.bank[b][64:128, :], HG[64:128, HG_B_Q:HG_B_Q + 1],
                                                                  r[64:128, :], ALU.mult, ALU.mult))
                            T.op("dve", [QHb, ROPEb], [TMb[0]],
                                 lambda e: e.tensor_tensor(TM[0][96:128, :], QH[96:128, :], ROPE[96:128, ts], ALU.mult))
                            T.op("act", [TMb[0]], [TMb[1]], lambda e: e.copy(TM[1][64:96, :], TM[0][96:128, :]))
                            T.op("dve", [QHb, ROPEb], [QHb],
                                 lambda e: e.tensor_tensor(QH[64:96, :], QH[64:96, :], ROPE[64:96, ts], ALU.mult))
                            T.op("dve", [QHb, TMb[1]], [QTb[tb]],
                                 lambda e: e.tensor_tensor(QT[64:96, ts], QH[64:96, :], TM[1][64:96, :], ALU.add), relaxed=True)
                        jobs.append({"mm": (lambda b, ts=ts, tb=tb: self.mm_acc(self.bank[b][:],
                                                                             [(wq[:, hh, c, :], CQN[:, c, ts]) for c in range(3)],
                                                                             [wqb, CQNb[tb]], self.bankb[b])),
                                     "sq_rows": 96, "rrows": 128, "nd": 96,
                                     "sums": (lambda sq: [(self.ones[0:96, :], sq[0:96, :])]),
                                     "post": qpost})
                    self.norm_pipeline(jobs, PB, SUMB, hn)
                    items = [(qb, kc) for qb in range(NTB) for kc in range(4 * (qb + 1))]
                    base = si
                    si += len(items)
                    abase = ai
                    ai += NTB

                    def geom(idx):
                        qb, kc = items[idx]
                        di = kc - 4 * qb
                        c0 = 128 * di if di > 0 else 0
                        return qb, kc, di, c0, 512 - c0

                    def A(idx):
                        qb, kc, di, c0, N = geom(idx)
                        bs = STB[(base + idx) % 4]
                        self.mm_acc(self.bank[bs][:, 0:N], [(KT[:, kc * 128:kc * 128 + 128], QT[:, qb * 512 + c0:qb * 512 + 512])],
                                    [KTb[kc // 4], QTb[qb]], self.bankb[bs])

                    def B(idx):
                        qb, kc, di, c0, N = geom(idx)
                        bs = STB[(base + idx) % 4]
                        s_i = (base + idx) % 3
                        ab = ACB[(abase + qb) % 2]
                        nk = 4 * (qb + 1)
                        if di >= 0:
                            T.op("dve", [self.bankb[bs], TRIb], [Sbb[s_i]],
                                 lambda e: e.scalar_tensor_tensor(Sb_[s_i][:], self.bank[bs][:, 0:128], SC, TRI[:], ALU.mult, ALU.add))
                            T.op("act", [Sbb[s_i]], [PTb[s_i]], lambda e: e.activation(PT[s_i][:, 0:128], Sb_[s_i][:], AF.Exp))
                            if N > 128:
                                T.op("act", [self.bankb[bs]], [PTb[s_i]],
                                     lambda e: e.activation(PT[s_i][:, 128:N], self.bank[bs][:, 128:N], AF.Exp, scale=SC), relaxed=True)
                        else:
                            T.op("act", [self.bankb[bs]], [PTb[s_i]],
                                 lambda e: e.activation(PT[s_i][:, 0:N], self.bank[bs][:, 0:N], AF.Exp, scale=SC))
                        T.op("pe", [VAb[kc // 4], PTb[s_i]], [self.bankb[ab]],
                             lambda e: e.matmul(self.bank[ab][:, c0:512], VA[:, kc, vo:vo + 128], PT[s_i][:, 0:N],
                                                start=(kc == 0), stop=(kc == nk - 1)))
                        if kc == nk - 1:
                            def fin():
                                ts = slice(qb * 512, qb * 512 + 512)
                                r_, rb = R[qb % 2], Rb[qb % 2]
                                self.recip(r_[drow, :], self.bank[ab][drow, :], [self.bankb[ab]], rb)
                                T.op("dve", [self.bankb[ab], rb], [AOb[hd // 2][qb]],
                                     lambda e: e.tensor_tensor(AO[urow, hd // 2, ts], self.bank[ab][urow, :], r_[drow, :], ALU.mult))
                            return fin
                    self.pipe(len(items), A, B, 3)
                self.W.done()
            self.out_proj("b_o%d", AO, AOb, 8)
            self.barrier()


    def _plan_mix2(self):
        W = self.W
        for hd in range(8):
            W.add("c_qk%d" % hd, self.d_cqk[hd], (2, 8, 128))
            W.add("c_v%d" % hd, self.d_cv[hd], (8, 128))
        for q in range(4):
            W.add("c_o%d" % q, self.d_co[2 * q:2 * q + 2].rearrange("t p k -> p t k"), (2, 8, 128))

    def _mix2(self):
        T = self.T
        kb = self.kb
        HG = self.hg
        LAM_INIT = 0.8 - 0.6 * math.exp(-0.3 * 2)
        with ExitStack() as es:
            sb = lambda n, shp, dt: kb.sb(n, shp, dt, es)
            AO = sb("AO", [128, 8, SEQ], BF16)
            AOb = [[Buf("AO%d_%d" % (a, tb)) for tb in range(NTB)] for a in range(8)]
            QK = [sb("QK%d" % i, [128, SEQ], BF16) for i in range(3)]
            QKb = [[Buf("QK%d_%d" % (i, tb)) for tb in range(NTB)] for i in range(3)]
            T.op("dve", [], QKb[1], lambda e: e.memset(QK[1][64:128, :], 0.0))
            T.op("dve", [], QKb[2], lambda e: e.memset(QK[2][0:64, :], 0.0))
            VA = sb("VA", [128, 16, 128], BF16)
            VAb = [Buf("VA_%d" % q) for q in range(4)]
            GF = [sb("GF%d" % i, [128, SEQ], F32) for i in range(2)]
            GFb = [Buf("GF%d" % i) for i in range(2)]
            Sb_ = [sb("Sb%d" % i, [128, 512], F32) for i in range(3)]
            Sbb = [Buf("Sb%d" % i) for i in range(3)]
            PT = [sb("PT%d" % i, [128, 512], BF16) for i in range(3)]
            PTb = [Buf("PT%d" % i) for i in range(3)]
            TT = [sb("TT%d" % i, [128, 512], F32) for i in range(2)]
            TTb = [Buf("TT%d" % i) for i in range(2)]
            lamt = sb("lamt", [128, 256], F32)
            lamb = Buf("lamt")
            lams = sb("lams", [128, 8], F32)
            hn = self.alloc_hn(es)
            PB, SUMB, STB = 0, 1, (2, 3)
            OB, DB = (4, 6), (5, 7)
            T.dma("sp", lamt[:], self.d_clam, [], [lamb], self.xk["lam"])
            for i in range(2):
                T.op("dve", [lamb], [lamb],
                     lambda e: e.tensor_tensor(lamt[:, i * 128:i * 128 + 64], lamt[:, i * 128:i * 128 + 64],
                                               lamt[:, i * 128 + 64:i * 128 + 128], ALU.mult))
                T.op("dve", [lamb], [lamb],
                     lambda e: e.reduce_sum(lams[:, i:i + 1], lamt[:, i * 128:i * 128 + 64], mybir.AxisListType.X))
            T.op("act", [lamb], [lamb], lambda e: e.activation(lams[:, 2:4], lams[:, 0:2], AF.Exp))
            T.op("dve", [lamb], [lamb], lambda e: e.tensor_tensor(lams[:, 4:5], lams[:, 3:4], lams[:, 2:3], ALU.subtract))
            T.op("dve", [lamb], [lamb], lambda e: e.tensor_scalar(lams[:, 5:6], lams[:, 4:5], -LAM_INIT, None, ALU.add))
            T.op("dve", [], [lamb], lambda e: e.memset(lams[:, 6:7], math.log(1.0 - LAM_INIT)))
            neglam = lams[:, 5:6]
            PBS, STB = (0, 2, 3), (2, 3, 0, 1)
            si = 0
            for hd in range(8):
                wqk, wqkb = self.W.get("c_qk%d" % hd)
                wv, wvb = self.W.get("c_v%d" % hd)
                for m in range(2):
                    T.dma("sp", GF[m][:], self.d_cg[m * 8 + hd], [], [GFb[m]], self.g3k[m])
                jobs = []
                for tb in range(NTB):
                    ts = slice(tb * 512, (tb + 1) * 512)

                    def post_q(b, r, rb, tb=tb, ts=ts):
                        T.op("dve", [self.bankb[b], rb, self.cb], [QKb[0][tb]],
                             lambda e: e.scalar_tensor_tensor(QK[0][:, ts], self.bank[b][:], HG[:, HG_C_Q:HG_C_Q + 1], r[:], ALU.mult, ALU.mult))

                    def post_k(b, r, rb, tb=tb, ts=ts):
                        T.op("dve", [self.bankb[b], rb, self.cb], [QKb[1][tb]],
                             lambda e: e.scalar_tensor_tensor(QK[1][0:64, ts], self.bank[b][0:64, :], HG[0:64, HG_C_K:HG_C_K + 1],
                                                              r[0:64, :], ALU.mult, ALU.mult))
                        T.op("dve", [self.bankb[b], rb, self.cb], [QKb[2][tb]],
                             lambda e: e.scalar_tensor_tensor(QK[2][64:128, ts], self.bank[b][64:128, :], HG[64:128, HG_C_K:HG_C_K + 1],
                                                              r[64:128, :], ALU.mult, ALU.mult))
                    for i, post in ((0, post_q), (1, post_k)):
                        jobs.append({"mm": (lambda b, i=i, ts=ts, tb=tb: self.mm_acc(self.bank[b][:],
                                                                                  [(wqk[:, i, c, :], self.h[:, c, ts]) for c in range(8)],
                                                                                  [wqkb, self.hb[tb]], self.bankb[b])),
                                     "sq_rows": 128, "rrows": 128, "nd": 64,
                                     "sums": (lambda sq: [(self.bd64[:], sq[:])]), "post": post})
                self.norm_pipeline(jobs, PBS, SUMB, hn)
                for q4 in range(4):
                    b = PBS[q4 % 3]

                    def mmv(e):
                        ins = None
                        for i in range(4):
                            tc = q4 * 4 + i
                            for c in range(8):
                                ins = e.matmul(self.bank[b][:, i * 128:(i + 1) * 128], self.h[:, c, tc * 128:(tc + 1) * 128],
                                               wv[:, c, :], start=(c == 0), stop=(c == 7))
                        return ins
                    T.op("pe", [wvb, self.hb[q4]], [self.bankb[b]], mmv)
                    T.op("act", [self.bankb[b]], [VAb[q4]],
                         lambda e: e.copy(VA[:, q4 * 4:(q4 + 1) * 4, :], self.bank[b][:].rearrange("p (i n) -> p i n", i=4)))
                self.W.done(2)
                items = [(qb, m, kc) for qb in range(NTB) for m in range(2) for kc in range(4 * (qb + 1))]
                base = si
                si += len(items)

                def geom(idx):
                    qb, m, kc = items[idx]
                    di = kc - 4 * qb
                    c0 = 128 * di if di > 0 else 0
                    return qb, m, kc, c0, 512 - c0

                def A(idx):
                    qb, m, kc, c0, N = geom(idx)
                    bs = STB[(base + idx) % 4]
                    self.mm_acc(self.bank[bs][:, 0:N], [(QK[1 + m][:, kc * 128:kc * 128 + 128], QK[0][:, qb * 512 + c0:qb * 512 + 512])],
                                [QKb[1 + m][kc // 4], QKb[0][qb]], self.bankb[bs])

                def B(idx):
                    qb, m, kc, c0, N = geom(idx)
                    bs = STB[(base + idx) % 4]
                    s_i = (base + idx) % 3
                    ob, db = OB[m], DB[m]
                    nk = 4 * (qb + 1)
                    g0 = qb * 512 + c0 - kc * 128
                    T.op("dve", [self.bankb[bs], GFb[m]], [Sbb[s_i]],
                         lambda e: e.scalar_tensor_tensor(Sb_[s_i][:, 0:N], self.bank[bs][:, 0:N], 0.125, GF[m][:, g0:g0 + N],
                                                          ALU.mult, ALU.add))
                    T.op("act", [Sbb[s_i]], [PTb[s_i]], lambda e: e.activation(PT[s_i][:, 0:N], Sb_[s_i][:, 0:N], AF.Exp))
                    T.op("pe", [VAb[kc // 4], PTb[s_i]], [self.bankb[ob]],
                         lambda e: e.matmul(self.bank[ob][:, c0:512], VA[:, kc, :], PT[s_i][:, 0:N], start=(kc == 0), stop=(kc == nk - 1)))
                    T.op("pe", [PTb[s_i], self.cb], [self.bankb[db]],
                         lambda e: e.matmul(self.bank[db][:, c0:512], self.ones[:], PT[s_i][:, 0:N], start=(kc == 0), stop=(kc == nk - 1)))
                    if kc == nk - 1:
                        def fin():
                            ts = slice(qb * 512, qb * 512 + 512)
                            self.recip(TT[m][:], self.bank[db][:], [self.bankb[db]], TTb[m])
                            T.op("dve", [self.bankb[ob], TTb[m]], [TTb[m]],
                                 lambda e: e.tensor_tensor(TT[m][:], self.bank[ob][:], TT[m][:], ALU.mult))
                            if m == 0:
                                return
                            T.op("dve", [TTb[0], TTb[1], lamb], [TTb[0]],
                                 lambda e: e.scalar_tensor_tensor(TT[0][:], TT[1][:], neglam, TT[0][:], ALU.mult, ALU.add))
                            T.op("act", [TTb[0]], [hn["sqhb"][0]], lambda e: e.activation(hn["sqh"][0][:], TT[0][:], AF.Square))
                            FB = DB[1]
                            self.mm_acc(self.bank[FB][:], [(self.ones[:], hn["sqh"][0][:])], [hn["sqhb"][0], self.cb], self.bankb[FB])
                            T.op("act", [self.bankb[FB], self.cb], [hn["rb"][0]],
                                 lambda e: e.activation(hn["r"][0][:], self.bank[FB][:], AF.Ln, bias=self.epsc[:, 0:1], scale=1.0 / 128))
                            T.op("act", [hn["rb"][0], lamb], [hn["rb"][0]],
                                 lambda e: e.activation(hn["r"][0][:], hn["r"][0][:], AF.Exp, bias=lams[:, 6:7], scale=-0.5))
                            T.op("dve", [TTb[0], hn["rb"][0], self.cb], [AOb[hd][qb]],
                                 lambda e: e.scalar_tensor_tensor(AO[:, hd, ts], TT[0][:], HG[:, HG_C_SUB:HG_C_SUB + 1], hn["r"][0][:],
                                                                  ALU.mult, ALU.mult))
                        return fin
                self.pipe(len(items), A, B, 3)
            self.out_proj("c_o%d", AO, AOb, 8)
            self.barrier()

    def _plan_mix3(self):
        W = self.W
        W.add("d_k", self.d_dk, (2, 8, 128))
        W.add("d_v", self.d_dv, (8, 128))
        for g in range(4):
            W.add("d_q%d" % g, self.d_dq[g], (2, 8, 128))
        for q in range(4):
            W.add("d_o%d" % q, self.d_do[2 * q:2 * q + 2].rearrange("t p k -> p t k"), (2, 8, 128))

    def _mix3(self):
        T = self.T
        kb = self.kb
        HG = self.hg
        with ExitStack() as es:
            sb = lambda n, shp, dt: kb.sb(n, shp, dt, es)
            AO = sb("AO", [128, 8, SEQ], BF16)
            AOb = [[Buf("AO%d_%d" % (a, tb)) for tb in range(NTB)] for a in range(8)]
            KT = [[sb("KT%d_%d" % (i, j), [128, SEQ], BF16) for j in range(2)] for i in range(2)]
            KTb = [[[Buf("KT%d_%d_%d" % (i, j, tb)) for tb in range(NTB)] for j in range(2)] for i in range(2)]
            VA = [sb("VA%d" % i, [128, 16, 192], BF16) for i in range(2)]
            VAb = [[Buf("VA%d_%d" % (i, q)) for q in range(4)] for i in range(2)]
            QT = [sb("QT%d" % i, [128, SEQ], BF16) for i in range(2)]
            QTb = [[Buf("QT%d_%d" % (i, tb)) for tb in range(NTB)] for i in range(2)]
            G3 = [sb("G3_%d" % i, [128, 256], F32) for i in range(2)]
            G3b = [Buf("G3_%d" % i) for i in range(2)]
            Sb_ = [sb("Sb%d" % i, [128, 256], F32) for i in range(3)]
            Sbb = [Buf("Sb%d" % i) for i in range(3)]
            PT = [sb("PT%d" % i, [128, 256], BF16) for i in range(3)]
            PTb = [Buf("PT%d" % i) for i in range(3)]
            R = [sb("R0", [128, 512], F32)] * 2
            Rb = [Buf("R0")] * 2
            es_t = sb("esink", [128, 16], F32)
            esb = Buf("esink")
            hn = self.alloc_hn(es)
            PB, SUMB, STB, ACB = (0, 1, 3, 4), 2, (3, 4, 7, 0), (5, 6)
            T.op("act", [self.cb], [esb], lambda e: e.activation(es_t[:], HG[:, HG_D_SINK:HG_D_SINK + 16], AF.Exp))
            for i in range(2):
                T.op("dve", [], VAb[i], lambda e: e.memset(VA[i][:, :, 64:128], 1.0))
                T.op("dve", [], KTb[i][0], lambda e: e.memset(KT[i][0][64:128, :], 0.0))
                T.op("dve", [], KTb[i][1], lambda e: e.memset(KT[i][1][0:64, :], 0.0))
            wk, wkb = self.W.get("d_k")
            wv, wvb = self.W.get("d_v")
            jobs = []
            for kvh in range(2):
                for tb in range(NTB):
                    ts = slice(tb * 512, (tb + 1) * 512)

                    def post_k(b, r, rb, kvh=kvh, tb=tb, ts=ts):
                        T.op("dve", [self.bankb[b], rb, self.cb], [KTb[kvh][0][tb]],
                             lambda e: e.scalar_tensor_tensor(KT[kvh][0][0:64, ts], self.bank[b][0:64, :], HG[0:64, HG_D_K:HG_D_K + 1],
                                                              r[0:64, :], ALU.mult, ALU.mult))
                        T.op("dve", [self.bankb[b], rb, self.cb], [KTb[kvh][1][tb]],
                             lambda e: e.scalar_tensor_tensor(KT[kvh][1][64:128, ts], self.bank[b][64:128, :], HG[64:128, HG_D_K:HG_D_K + 1],
                                                              r[64:128, :], ALU.mult, ALU.mult))
                    jobs.append({"mm": (lambda b, kvh=kvh, ts=ts, tb=tb: self.mm_acc(self.bank[b][:],
                                                                                      [(wk[:, kvh, c, :], self.h[:, c, ts]) for c in range(8)],
                                                                                      [wkb, self.hb[tb]], self.bankb[b])),
                                 "sq_rows": 128, "rrows": 128, "nd": 64, "sums": (lambda sq: [(self.bd64[:], sq[:])]), "post": post_k})
            self.norm_pipeline(jobs, PB, SUMB, hn)
            for q4 in range(4):
                b = PB[q4 % 2]

                def mmv(e):
                    ins = None
                    for i in range(4):
                        tc = q4 * 4 + i
                        for c in range(8):
                            ins = e.matmul(self.bank[b][:, i * 128:(i + 1) * 128], self.h[:, c, tc * 128:(tc + 1) * 128],
                                           wv[:, c, :], start=(c == 0), stop=(c == 7))
                    return ins
                T.op("pe", [wvb, self.hb[q4]], [self.bankb[b]], mmv)
                bv = self.bank[b][:].rearrange("p (i n) -> p i n", i=4)
                for kvh in range(2):
                    for off in (0, 128):
                        T.op("act", [self.bankb[b]], [VAb[kvh][q4]],
                             lambda e: e.copy(VA[kvh][:, q4 * 4:(q4 + 1) * 4, off:off + 64], bv[:, :, kvh * 64:(kvh + 1) * 64]),
                             relaxed=(off > 0))
            self.W.done(2)
            cnt = [0]
            for g in range(4):
                wq, wqb = self.W.get("d_q%d" % g)
                for pr in range(2):
                    pair = 2 * g + pr
                    qt, qtb = QT[pair % 2], QTb[pair % 2]
                    jobs = []
                    for tb in range(NTB):
                        ts = slice(tb * 512, (tb + 1) * 512)

                        def post_q(b, r, rb, tb=tb, ts=ts):
                            T.op("dve", [self.bankb[b], rb, self.cb], [qtb[tb]],
                                 lambda e: e.scalar_tensor_tensor(qt[:, ts], self.bank[b][:], HG[:, HG_D_Q:HG_D_Q + 1], r[:],
                                                                  ALU.mult, ALU.mult))
                        jobs.append({"mm": (lambda b, ts=ts, tb=tb: self.mm_acc(self.bank[b][:],
                                                                             [(wq[:, pr, c, :], self.h[:, c, ts]) for c in range(8)],
                                                                             [wqb, self.hb[tb]], self.bankb[b])),
                                     "sq_rows": 128, "rrows": 128, "nd": 64, "sums": (lambda sq: [(self.bd64[:], sq[:])]), "post": post_q})
                    self.norm_pipeline(jobs, PB, SUMB, hn)
                    for h2 in range(2):
                        self._mix3_head(2 * pair + h2, qt, qtb, KT, KTb, VA, VAb, G3, G3b, Sb_, Sbb, PT, PTb, R, Rb, es_t, esb,
                                        AO, AOb, STB, ACB, cnt)
                self.W.done()
            self.out_proj("d_o%d", AO, AOb, 8)
            self.barrier()

    def _mix3_head(self, hd, qt, qtb, KT, KTb, VA, VAb, G3, G3b, Sb_, Sbb, PT, PTb, R, Rb, es_t, esb, AO, AOb, STB, ACB, cnt):
        T = self.T
        kvh = hd // 8
        par = hd % 2
        kt, ktb = KT[kvh][par], KTb[kvh][par]
        vo = par * 64
        urow = slice(0, 64) if par == 0 else slice(64, 128)
        drow = slice(64, 128) if par == 0 else slice(0, 64)
        g3, g3b = G3[hd % 2], G3b[hd % 2]
        T.dma("sp", g3[:], self.d_g3[hd], [], [g3b], self.g3k[hd % 2])
        base = cnt[0]
        cnt[0] += 16

        def A(kc):
            k0 = kc * 128
            N = 256 if kc < 15 else 128
            bs = STB[(base + kc) % 4]
            qbufs = [qtb[k0 // 512]] + ([qtb[(k0 + 128) // 512]] if kc < 15 else [])
            self.mm_acc(self.bank[bs][:, 0:N], [(kt[:, k0:k0 + 128], qt[:, k0:k0 + N])], [ktb[k0 // 512]] + qbufs, self.bankb[bs])

        def B(kc):
            N = 256 if kc < 15 else 128
            bs = STB[(base + kc) % 4]
            s_i = (base + kc) % 3
            T.op("dve", [self.bankb[bs], g3b], [Sbb[s_i]],
                 lambda e: e.scalar_tensor_tensor(Sb_[s_i][:, 0:N], self.bank[bs][:, 0:N], 0.125, g3[:, 0:N], ALU.mult, ALU.add))
            T.op("act", [Sbb[s_i]], [PTb[s_i]], lambda e: e.activation(PT[s_i][:, 0:N], Sb_[s_i][:, 0:N], AF.Exp))
            ab0 = ACB[(kc // 4) % 2]
            c0 = (kc % 4) * 128
            T.op("pe", [VAb[kvh][kc // 4], PTb[s_i]], [self.bankb[ab0]],
                 lambda e: e.matmul(self.bank[ab0][:, c0:c0 + 128], VA[kvh][:, kc, vo:vo + 128], PT[s_i][:, 0:128],
                                    start=(kc == 0), stop=True))
            if kc < 15:
                ab1 = ACB[((kc + 1) // 4) % 2]
                c1 = ((kc + 1) % 4) * 128
                T.op("pe", [VAb[kvh][kc // 4], PTb[s_i]], [self.bankb[ab1]],
                     lambda e: e.matmul(self.bank[ab1][:, c1:c1 + 128], VA[kvh][:, kc, vo:vo + 128], PT[s_i][:, 128:256],
                                        start=True, stop=False))
            if kc % 4 == 3:
                def fin():
                    m = kc // 4
                    ts = slice(m * 512, (m + 1) * 512)
                    r, rb = R[m % 2], Rb[m % 2]
                    self.recip(r[drow, :], self.bank[ab0][drow, :], [self.bankb[ab0], esb], rb, bias=es_t[drow, hd:hd + 1])
                    T.op("dve", [self.bankb[ab0], rb], [AOb[hd // 2][m]],
                         lambda e: e.tensor_tensor(AO[urow, hd // 2, ts], self.bank[ab0][urow, :], r[drow, :], ALU.mult))
                return fin
        self.pipe(16, A, B, 3)


def host_weights(inp):
    f32 = np.float32
    out = {}
    wup = inp["f_w_up"].reshape(4, 8, 128, 2, NJ, 128)
    out["wup"] = np.ascontiguousarray(wup.transpose(0, 4, 2, 1, 3, 5)).reshape(4, NJ, 128, 2048)
    wdn = inp["f_w_down"].reshape(4, 2, 11, 128, 8, 128)
    out["wdn"] = np.ascontiguousarray(wdn.transpose(0, 1, 4, 3, 2, 5)).reshape(4, 2, 8, 128, 11 * 128)
    gains = np.zeros((128, 64), f32)
    gains[:, 0:32] = inp["norm_mix"].reshape(4, 8, 128).transpose(2, 0, 1).reshape(128, 32)
    gains[:, 32:64] = inp["norm_ffn"].reshape(4, 8, 128).transpose(2, 0, 1).reshape(128, 32)
    out["gains"] = gains
    cw = inp["f_conv_w"].reshape(4, 3, NJ, 128)
    cbv = inp["f_conv_b"].reshape(4, 1, NJ, 128)
    convp = np.concatenate([cw, cbv], axis=1)
    out["convp"] = np.ascontiguousarray(convp.transpose(3, 0, 1, 2)).reshape(128, 4 * 4 * NJ)
    hg = np.zeros((128, NHG), f32)
    tab = inp["rel_bias_table"]
    dw = inp["d_w_in"][0]
    rep2 = lambda v: np.concatenate([v, v])
    hg[:, HG_D_Q] = rep2(inp["d_q_norm"][0])
    hg[:, HG_D_K] = rep2(inp["d_k_norm"][0])
    hg[:, HG_D_SINK:HG_D_SINK + 16] = inp["d_sinks"][0][None, :]
    wk = dw[:, 1024:1152].reshape(8, 128, 2, 64).transpose(1, 2, 0, 3)
    out["d_k"] = np.ascontiguousarray(np.concatenate([wk, wk], axis=3)).reshape(128, 2048)
    out["d_v"] = np.ascontiguousarray(dw[:, 1152:1280].reshape(8, 128, 128).transpose(1, 0, 2)).reshape(128, 1024)
    wq = dw[:, 0:1024].reshape(8, 128, 4, 2, 128).transpose(2, 1, 3, 0, 4)
    out["d_q"] = np.ascontiguousarray(wq).reshape(4, 128, 2048)
    out["d_o"] = tile_fm(inp["d_w_out"][0])
    jj = np.arange(256)[None, :] - np.arange(128)[:, None]
    valid = (jj >= 0) & (jj <= 127)
    bk = t5_bucket_np(np.maximum(jj, 0))
    g3 = np.where(valid[None], tab[bk].transpose(2, 0, 1), f32(NEG))
    out["d_g3"] = np.ascontiguousarray(g3.astype(f32))
    aw = inp["a_w_in"][0].reshape(8, 128, 3, 3, 8, 64)
    for g in range(3):
        hg[:, HG_A_Q + g] = rep2(inp["a_q_norm"][0][g])
        hg[:, HG_A_K + g] = rep2(inp["a_k_norm"][0][g])
    av = aw[:, :, :, 2].reshape(8, 128, 3, 2, 4, 64)
    out["a_v"] = np.ascontiguousarray(av.transpose(3, 2, 1, 0, 4, 5)).reshape(2, 3, 128, 2048)
    aqk = aw[:, :, :, 0:2]
    out["a_qk"] = np.ascontiguousarray(aqk.transpose(4, 2, 1, 3, 0, 5)).reshape(8, 3, 128, 1024)
    out["a_o"] = tile_fm(inp["a_w_out"][0])
    valid0 = (jj >= 0) & (jj <= 128)
    g0 = np.zeros((3, 8, 128, 256), f32)
    for g, dil in enumerate((1, 4, 16)):
        bk0 = t5_bucket_np(np.maximum(jj, 0) * dil)
        g0[g] = np.where(valid0[None], tab[:, 0:8][bk0].transpose(2, 0, 1), f32(NEG))
    out["a_g0"] = g0
    bw = inp["b_w_in"][0]
    part = np.concatenate([np.arange(16, 32), np.arange(0, 16)])
    hg[:, HG_B_QA:HG_B_QA + 3] = inp["b_q_a_norm"][0].reshape(3, 128).T
    hg[:, HG_B_KVA:HG_B_KVA + 2] = inp["b_kv_a_norm"][0].reshape(2, 128).T
    qn, kn = inp["b_q_norm"][0], inp["b_k_norm"][0]
    hg[:, HG_B_Q] = np.concatenate([qn, qn[64 + part]])
    hg[0:64, HG_B_K] = kn[0:64]
    hg[0:64, HG_B_KPE] = np.concatenate([kn[64:96], kn[64 + part]])
    tl = lambda w: w.reshape(8, 128, -1).transpose(1, 0, 2).reshape(128, -1)
    tiles = [tl(bw[:, i * 128:(i + 1) * 128]) for i in range(5)]
    pe_cols = np.concatenate([640 + np.arange(32), 640 + part])
    b_in = np.zeros((3, 128, 2048), f32)
    b_in[0, :, 0:1024], b_in[0, :, 1024:2048] = tiles[0], tiles[1]
    b_in[1, :, 0:1024], b_in[1, :, 1024:2048] = tiles[2], tiles[3]
    b_in[2, :, 0:1024], b_in[2, :, 1024:1536] = tiles[4], tl(bw[:, pe_cols])
    out["b_in"] = b_in
    kvu = inp["b_w_kv_up"][0].reshape(2, 128, 16, 2, 64)
    out["b_kup"] = np.ascontiguousarray(kvu[:, :, :, 0].transpose(1, 2, 0, 3)).reshape(128, 2048)
    vv = kvu[:, :, :, 1].reshape(2, 128, 8, 128)
    out["b_vup"] = np.ascontiguousarray(vv.transpose(1, 2, 0, 3)).reshape(128, 2048)
    qu = inp["b_w_q_up"][0].reshape(3, 128, 16, 96)
    qu = np.concatenate([qu, qu[:, :, :, 64 + part]], axis=3)
    out["b_qup"] = np.ascontiguousarray(qu.reshape(3, 128, 4, 4, 128).transpose(2, 1, 3, 0, 4)).reshape(4, 128, 1536)
    out["b_o"] = tile_fm(inp["b_w_out"][0])
    inv_freq = (np.float32(10000.0) ** (-np.arange(0, 32, 2, dtype=f32) / np.float32(32))).astype(f32)
    ang = (np.arange(SEQ, dtype=f32)[:, None] * inv_freq[None, :]).astype(f32)
    cos = np.cos(ang).astype(f32).T
    sin = np.sin(ang).astype(f32).T
    cos32 = np.concatenate([cos, cos], axis=0)
    sins32 = np.concatenate([-sin, sin], axis=0)
    out["b_rope"] = np.ascontiguousarray(np.concatenate([cos32, sins32, cos32, sins32], axis=0))
    pp = np.arange(128)
    out["b_tri"] = np.where(pp[None, :] >= pp[:, None], f32(0), f32(NEG)).astype(f32)
    cw = inp["c_w_in"][0]
    hg[:, HG_C_Q] = rep2(inp["c_q_norm"][0])
    hg[:, HG_C_K] = rep2(inp["c_k_norm"][0])
    hg[:, HG_C_SUB] = inp["c_subln"][0]
    cq = cw[:, 0:1024].reshape(8, 128, 8, 2, 64)
    ck = cw[:, 1024:2048].reshape(8, 128, 8, 2, 64)
    cqk = np.stack([cq.reshape(8, 128, 8, 128), ck.reshape(8, 128, 8, 128)], axis=3)
    out["c_qk"] = np.ascontiguousarray(cqk.transpose(2, 1, 3, 0, 4)).reshape(8, 128, 2048)
    cv = cw[:, 2048:3072].reshape(8, 128, 8, 128)
    out["c_v"] = np.ascontiguousarray(cv.transpose(2, 1, 0, 3)).reshape(8, 128, 1024)
    out["c_o"] = tile_fm(inp["c_w_out"][0])
    dd = np.arange(SEQ)[None, :] - np.arange(128)[:, None]
    bkd = t5_bucket_np(np.maximum(dd, 0))
    out["c_g"] = np.ascontiguousarray(np.where((dd >= 0)[None], tab[bkd].transpose(2, 0, 1), f32(NEG)).astype(f32))
    lam4 = np.concatenate([inp["c_lambda_q1"][0], inp["c_lambda_k1"][0], inp["c_lambda_q2"][0], inp["c_lambda_k2"][0]])
    out["c_lam"] = np.ascontiguousarray(np.broadcast_to(lam4[None, :], (128, 256))).astype(f32)
    out["hgains"] = hg
    return out


def tile_fm(w):
    K, N = w.shape
    return np.ascontiguousarray(w.reshape(K // 128, 128, N // 128, 128).transpose(2, 1, 0, 3)).reshape(N // 128, 128, K)


def t5_bucket_np(dist):
    d = np.maximum(dist, 1).astype(np.float32)
    large = 16 + (np.log(d / np.float32(16)) / np.float32(np.log(2048 / 16)) * np.float32(16)).astype(np.int32)
    return np.where(dist < 16, dist, np.minimum(large, 31)).astype(np.int64)


def host_x(x):
    b = x.shape[0]
    return np.ascontiguousarray(x.reshape(b, SEQ, 8, 128).transpose(0, 2, 3, 1))


def host_y(yT):
    b = yT.shape[0]
    return np.ascontiguousarray(yT.transpose(0, 3, 1, 2)).reshape(b, SEQ, DM)


def kernel(**inputs):
    x = np.asarray(inputs["x"], np.float32)
    nb = x.shape[0]
    per = nb // NCORES
    prog = Prog(per)
    nc = prog.build()
    wts = host_weights({k: np.asarray(v) for k, v in inputs.items()})
    xT = host_x(x)
    in_maps = []
    for c in range(NCORES):
        m = dict(wts)
        m["xT"] = xT[c * per:(c + 1) * per]
        in_maps.append(m)
    res = run_bass_kernel_spmd(nc, in_maps, core_ids=list(range(NCORES)))
    yT = np.concatenate([r["yT"] for r in res.results], axis=0)
    return host_y(yT)
```

```python
import math
import numpy as np
from contextlib import ExitStack
import concourse.bass as bass
import concourse.mybir as mybir
from concourse.bass_utils import run_bass_kernel_spmd

F32 = mybir.dt.float32
BF16 = mybir.dt.bfloat16
AF = mybir.ActivationFunctionType
ALU = mybir.AluOpType

NCORES = 8
SEQ = 2048
DM = 1024
DFF = 2816
NJ = DFF // 128
NTB = SEQ // 512
EPS = 1e-6
NEG = -30000.0
HG_D_Q, HG_D_K, HG_D_SINK = 0, 1, 2
HG_A_Q, HG_A_K = 18, 21
HG_B_QA, HG_B_KVA, HG_B_Q, HG_B_K, HG_B_KPE = 24, 27, 29, 30, 31
HG_C_Q, HG_C_K, HG_C_SUB = 32, 33, 34
NHG = 64


class Buf:
    __slots__ = ("name", "w", "r", "excl")

    def __init__(self, name, excl=False):
        self.name = name
        self.w = None
        self.r = []
        self.excl = excl


ATTACH_WAITS = True


class _FirstIns:
    def __init__(self, eng):
        self._eng = eng
        self.first = None

    def __getattr__(self, name):
        f = getattr(self._eng, name)

        def w(*a, **k):
            r = f(*a, **k)
            if self.first is None:
                self.first = r
            return r
        return w


class Sync:
    ENG = ("pe", "act", "dve", "pool", "sp")

    def __init__(self, nc, es):
        self.nc = nc
        self.es = es
        self.eng = {"pe": nc.tensor, "act": nc.scalar, "dve": nc.vector, "pool": nc.gpsimd, "sp": nc.sync}
        self.sems = {}
        self.cnt = {}
        self.seen = {e: {} for e in self.ENG}
        for e in self.ENG:
            self.sems[e] = es.enter_context(nc.semaphore("s_" + e))
            self.cnt[e] = 0
        self.nwaits = 0
        self.nops = 0
        self.snap = {}

    def dma_sem(self, name):
        key = "d_" + name
        self.sems[key] = self.es.enter_context(self.nc.semaphore(key))
        self.cnt[key] = 0
        return key

    def _deps(self, e, reads, writes, relaxed=False):
        deps = {}

        def add(ev, same_ok):
            if ev is None:
                return
            k, v = ev
            if k == e and not same_ok:
                return
            if deps.get(k, 0) < v:
                deps[k] = v

        for b in reads:
            add(b.w, True)
            if b.excl:
                for ev in b.r:
                    add(ev, False)
        for b in writes:
            add(b.w, not relaxed)
            for ev in b.r:
                add(ev, not relaxed)
        return deps

    def _learn(self, e, k, v):
        seen = self.seen[e]
        if seen.get(k, 0) < v:
            seen[k] = v
        sn = self.snap.get((k, v))
        if sn:
            for kk, vv in sn.items():
                if seen.get(kk, 0) < vv:
                    seen[kk] = vv

    def _wait(self, e, deps):
        eng = self.eng[e]
        seen = self.seen[e]
        for k, v in sorted(deps.items(), key=lambda kv: -len(self.snap.get(kv, ()))):
            if seen.get(k, 0) < v:
                eng.wait_ge(self.sems[k], v)
                self._learn(e, k, v)
                self.nwaits += 1

    def _record(self, ev, reads, writes):
        for b in reads:
            b.r.append(ev)
            if len(b.r) > 64:
                best = {}
                for k, v in b.r:
                    if best.get(k, 0) < v:
                        best[k] = v
                b.r = list(best.items())
        for b in writes:
            b.w = ev
            b.r = []

    def op(self, e, reads, writes, fn, relaxed=False):
        deps = self._deps(e, reads, writes, relaxed)
        seen = self.seen[e]
        pend = sorted([(k, v) for k, v in deps.items() if seen.get(k, 0) < v],
                      key=lambda kv: -len(self.snap.get(kv, ())))
        attach = pend.pop(0) if (pend and ATTACH_WAITS) else None
        if attach is not None:
            self._learn(e, attach[0], attach[1])
        self._wait(e, dict(pend))
        cap = _FirstIns(self.eng[e])
        ins = fn(cap)
        if attach is not None:
            cap.first._wait_ge(self.sems[attach[0]], attach[1])
        ins.then_inc(self.sems[e], 1)
        self.cnt[e] += 1
        self.nops += 1
        ev = (e, self.cnt[e])
        self.snap[ev] = dict(seen)
        self._record(ev, reads, writes)
        return ev

    def dma(self, q, out, in_, reads, writes, key):
        self._wait(q, self._deps(q, reads, writes))
        self.eng[q].dma_start(out=out, in_=in_).then_inc(self.sems[key], 16)
        self.cnt[key] += 16
        ev = (key, self.cnt[key])
        self.snap[ev] = dict(self.seen[q])
        self._record(ev, reads, writes)
        return ev

    def wait_all(self, e, bufs):
        self._wait(e, self._deps(e, bufs, bufs))


def fview(ap, shape):
    if len(shape) == 1:
        return ap
    if len(shape) == 2:
        return ap.rearrange("p (a b) -> p a b", a=shape[0])
    return ap.rearrange("p (a b c) -> p a b c", a=shape[0], b=shape[1])


class WRing:
    SLOT = 2048

    def __init__(self, kb, nslots):
        self.kb = kb
        self.ns = nslots
        self.t = kb.sb("wring", [128, nslots * self.SLOT], BF16)
        self.bufs = [Buf("wslot%d" % i) for i in range(nslots)]
        self.keys = [kb.T.dma_sem("w%d" % i) for i in range(nslots)]
        self.plan = []
        self.issued = 0
        self.consumed = 0
        self.released = 0

    def add(self, tag, dram_ap, shape):
        n = int(np.prod(shape))
        assert n <= self.SLOT, (tag, shape)
        self.plan.append((tag, dram_ap, tuple(shape), n))

    def _view(self, i):
        s = i % self.ns
        _, _, shape, n = self.plan[i]
        return fview(self.t[:, s * self.SLOT: s * self.SLOT + n], shape)

    def pump(self):
        T = self.kb.T
        while self.issued < len(self.plan) and self.issued - self.ns < self.released:
            i = self.issued
            s = i % self.ns
            T.dma("pool", self._view(i), self.plan[i][1], [], [self.bufs[s]], self.keys[s])
            self.issued += 1

    def get(self, tag):
        i = self.consumed
        assert self.plan[i][0] == tag, (self.plan[i][0], tag)
        self.pump()
        assert self.issued > i, "weight ring: too many tiles held"
        self.consumed += 1
        return self._view(i), self.bufs[i % self.ns]

    def done(self, n=1):
        self.released += n
        assert self.released <= self.consumed
        self.pump()


class KB:
    def __init__(self, nc, es):
        self.nc = nc
        self.es = es
        self.T = Sync(nc, es)

    def sb(self, name, shape, dt, es=None):
        self.uid = getattr(self, "uid", 0) + 1
        return (es or self.es).enter_context(self.nc.sbuf_tensor("s%d_%s" % (self.uid, name), list(shape), dt))

    def ps(self, name):
        return self.es.enter_context(self.nc.psum_tensor(name, [128, 512], F32))

    def din(self, name, shape, dt=F32):
        self.in_names = getattr(self, "in_names", []) + [name]
        return self.nc.dram_tensor(name, list(shape), dt, kind="ExternalInput").ap()

    def dout(self, name, shape, dt=F32):
        return self.nc.dram_tensor(name, list(shape), dt, kind="ExternalOutput").ap()


class Prog:
    def __init__(self, nseq, layers=(0, 1, 2, 3), mixers=True, ffns=True):
        self.nseq = nseq
        self.layers = tuple(layers)
        self.mixers = mixers
        self.ffns = ffns

    def build(self):
        nc = bass.Bass("TRN2", target_bir_lowering=False)
        self.nc = nc
        with ExitStack() as es:
            kb = KB(nc, es)
            self.kb = kb
            self.T = kb.T
            self._declare_io()
            self._alloc()
            self._plan_weights()
            self._load_consts()
            for s in range(self.nseq):
                self._run_seq(s)
            self.T.wait_all("sp", self.Xb)
        return nc

    def _declare_io(self):
        kb = self.kb
        ns = self.nseq
        self.d_x = kb.din("xT", [ns, 8, 128, SEQ])
        self.d_y = kb.dout("yT", [ns, 8, 128, SEQ])
        self.d_wup = kb.din("wup", [4, NJ, 128, 2048])
        self.d_wdn = kb.din("wdn", [4, 2, 8, 128, 11 * 128])
        self.d_gains = kb.din("gains", [128, 64])
        self.d_convp = kb.din("convp", [128, 4 * 4 * NJ])
        self.d_hg = kb.din("hgains", [128, NHG])
        if 0 in self.layers and self.mixers:
            self.d_av = kb.din("a_v", [2, 3, 128, 2048])
            self.d_aqk = kb.din("a_qk", [8, 3, 128, 1024])
            self.d_ao = kb.din("a_o", [8, 128, 512])
            self.d_g0 = kb.din("a_g0", [3, 8, 128, 256])
        if 1 in self.layers and self.mixers:
            self.d_bin = kb.din("b_in", [3, 128, 2048])
            self.d_bk = kb.din("b_kup", [128, 2048])
            self.d_bv = kb.din("b_vup", [128, 2048])
            self.d_bq = kb.din("b_qup", [4, 128, 1536])
            self.d_bo = kb.din("b_o", [8, 128, 1024])
            self.d_rope = kb.din("b_rope", [128, SEQ])
            self.d_tri = kb.din("b_tri", [128, 128])
        if 2 in self.layers and self.mixers:
            self.d_cqk = kb.din("c_qk", [8, 128, 2048])
            self.d_cv = kb.din("c_v", [8, 128, 1024])
            self.d_co = kb.din("c_o", [8, 128, 1024])
            self.d_cg = kb.din("c_g", [16, 128, SEQ])
            self.d_clam = kb.din("c_lam", [128, 256])
        if 3 in self.layers and self.mixers:
            self.d_dk = kb.din("d_k", [128, 2048])
            self.d_dv = kb.din("d_v", [128, 1024])
            self.d_dq = kb.din("d_q", [4, 128, 2048])
            self.d_do = kb.din("d_o", [8, 128, 1024])
            self.d_g3 = kb.din("d_g3", [16, 128, 256])

    def _alloc(self):
        kb = self.kb
        T = self.T
        self.X = kb.sb("X", [128, 8, SEQ], F32)
        self.Xb = [Buf("X%d" % c) for c in range(8)]
        self.Xk = [T.dma_sem("x%d" % c) for c in range(8)]
        self.h = kb.sb("h", [128, 8, SEQ], BF16)
        self.hb = [Buf("h%d" % tb) for tb in range(NTB)]
        self.W = WRing(kb, 5)
        self.gains = kb.sb("gains", [128, 64], F32)
        self.convp = kb.sb("convp", [128, 4 * 4 * NJ], F32)
        self.cb = Buf("consts")
        self.ck = T.dma_sem("consts")
        self.hg = kb.sb("hgains", [128, NHG], F32)
        self.g3k = [T.dma_sem("g3_%d" % i) for i in range(2)]
        self.xk = {n: T.dma_sem(n) for n in ("lam", "rope", "tri")}
        self.ones = kb.sb("ones", [128, 128], BF16)
        self.epsc = kb.sb("epsc", [128, 1], F32)
        self.zeros = kb.sb("zeros", [128, 512], BF16)
        self.bd64 = kb.sb("bd64", [128, 128], BF16)
        self.bank = [kb.ps("bank%d" % i) for i in range(8)]
        self.bankb = [Buf("bank%d" % i, excl=True) for i in range(8)]

    def _plan_weights(self):
        for s in range(self.nseq):
            for l in self.layers:
                if self.mixers:
                    self._plan_mixer(l)
                if self.ffns:
                    self._plan_ffn(l)

    def _load_consts(self):
        T = self.T
        T.dma("sp", self.gains[:], self.d_gains, [], [self.cb], self.ck)
        T.dma("sp", self.convp[:], self.d_convp, [], [self.cb], self.ck)
        T.dma("sp", self.hg[:], self.d_hg, [], [self.cb], self.ck)
        T.op("dve", [], [self.cb], lambda e: e.memset(self.ones[:], 1.0))
        T.op("dve", [], [self.cb], lambda e: e.memset(self.epsc[:], EPS))
        T.op("dve", [], [self.cb], lambda e: e.memset(self.zeros[:], 0.0))
        T.op("dve", [], [self.cb], lambda e: e.memset(self.bd64[:], 0.0))
        T.op("dve", [], [self.cb], lambda e: e.memset(self.bd64[0:64, 0:64], 1.0))
        T.op("dve", [], [self.cb], lambda e: e.memset(self.bd64[64:128, 64:128], 1.0))

    def barrier(self):
        T = self.T
        comp = ("pe", "act", "dve")
        for e in comp + ("sp",):
            T._wait(e, {k: T.cnt[k] for k in comp if T.cnt[k] > 0})

    def _run_seq(self, s):
        T = self.T
        if s == 0:
            for c in range(8):
                T.dma("sp", self.X[:, c, :], self.d_x[s, c], [], [self.Xb[c]], self.Xk[c])
        stored = set()

        def x_final(c):
            T.dma("sp", self.d_y[s, c], self.X[:, c, :], [self.Xb[c]], [], self.Xk[c])
            if s + 1 < self.nseq:
                T.dma("sp", self.X[:, c, :], self.d_x[s + 1, c], [], [self.Xb[c]], self.Xk[c])
            stored.add(c)
        for li, l in enumerate(self.layers):
            last = (li == len(self.layers) - 1)
            if self.mixers:
                self._mixer(l)
                self.barrier()
            if self.ffns:
                self._ffn(l, x_final if last else None)
                self.barrier()
        for c in range(8):
            if c not in stored:
                x_final(c)

    def _norm(self, gcol):
        T = self.T
        NBS = (7, 6)
        es_ = ExitStack()
        sq = [self.kb.sb("sq%d" % i, [128, 8, 512], BF16, es_) for i in range(2)]
        sqb = [Buf("sq%d" % i) for i in range(2)]
        rstd = [self.kb.sb("rstd%d" % i, [128, 512], F32, es_) for i in range(2)]
        rstdb = [Buf("rstd%d" % i) for i in range(2)]

        def A(tb):
            NB = NBS[tb % 2]
            ts = slice(tb * 512, (tb + 1) * 512)
            for c in range(8):
                T.op("act", [self.Xb[c]], [sqb[tb % 2]],
                     lambda e: e.activation(sq[tb % 2][:, c, :], self.X[:, c, ts], AF.Square), relaxed=(c > 0))
            self.mm_acc(self.bank[NB][:], [(self.ones[:], sq[tb % 2][:, c, :]) for c in range(8)],
                        [sqb[tb % 2], self.cb], self.bankb[NB])

        def B(tb):
            NB = NBS[tb % 2]
            ts = slice(tb * 512, (tb + 1) * 512)
            r, rb = rstd[tb % 2], rstdb[tb % 2]
            T.op("act", [self.bankb[NB], self.cb], [rb],
                 lambda e: e.activation(r[:], self.bank[NB][:], AF.Ln, bias=self.epsc[:, 0:1], scale=1.0 / DM))
            T.op("act", [rb], [rb], lambda e: e.activation(r[:], r[:], AF.Exp, scale=-0.5))
            for c in range(8):
                T.op("dve", [self.Xb[c], rb, self.cb], [self.hb[tb]],
                     lambda e: e.scalar_tensor_tensor(self.h[:, c, ts], self.X[:, c, ts],
                                                      self.gains[:, gcol + c: gcol + c + 1], r[:], ALU.mult, ALU.mult),
                     relaxed=(c > 0))
        self.pipe(NTB, A, B, 1)
        self.barrier()
        es_.close()

    def _plan_ffn(self, l):
        W = self.W
        for g in range(2):
            for jj in range(11):
                j = 11 * g + jj
                W.add("up%d_%d" % (l, j), self.d_wup[l, j], (8, 2, 128))
            for o in range(8):
                W.add("dn%d_%d_%d" % (l, g, o), self.d_wdn[l, g, o], (11, 128))

    def _ffn(self, l, x_final=None):
        T = self.T
        kb = self.kb
        self._norm(32 + l * 8)
        DBG = ""
        if DBG == "norm":
            return
        with ExitStack() as es:
            sb = lambda n, shp, dt: kb.sb(n, shp, dt, es)
            A = sb("A", [128, 11, SEQ], BF16)
            Ab = [[Buf("A%d_%d" % (j, tb)) for tb in range(NTB)] for j in range(11)]
            Gs = [sb("Gs%d" % i, [128, 2 + SEQ], F32) for i in range(2)]
            Gsb = [[Buf("Gs%d_%d" % (i, tb)) for tb in range(NTB)] for i in range(2)]
            C = [sb("C%d" % i, [128, 512], F32) for i in range(2)]
            Cb = [Buf("C%d" % i) for i in range(2)]
            S = [sb("S%d" % i, [128, 512], F32) for i in range(2)]
            Sb = [Buf("S%d" % i) for i in range(2)]
            for i in range(2):
                T.op("dve", [], [Gsb[i][0]], lambda e: e.memset(Gs[i][:, 0:2], 0.0))
            cp = lambda k, j: self.convp[:, (l * 4 + k) * NJ + j: (l * 4 + k) * NJ + j + 1]
            GB, UB, DB = (0, 1), (2, 3), (4, 5)
            it = 0
            dn = 0
            for g in range(2):
                for jj in range(11):
                    j = 11 * g + jj
                    wt, wb = self.W.get("up%d_%d" % (l, j))
                    gs = Gs[j % 2]
                    gsb = Gsb[j % 2]
                    for tb in range(NTB):
                        if DBG == "nogu":
                            break
                        ts = slice(tb * 512, (tb + 1) * 512)
                        bg, bu = GB[it % 2], UB[it % 2]
                        Cc, Ccb = C[it % 2], Cb[it % 2]
                        Ss, Ssb = S[it % 2], Sb[it % 2]
                        it += 1

                        def mmg(e, bnk, gi):
                            ins = None
                            for c in range(8):
                                ins = e.matmul(self.bank[bnk][:], wt[:, c, gi, :], self.h[:, c, ts],
                                               start=(c == 0), stop=(c == 7))
                            return ins
                        T.op("pe", [wb, self.hb[tb]], [self.bankb[bg]], lambda e: mmg(e, bg, 0))
                        T.op("pe", [wb, self.hb[tb]], [self.bankb[bu]], lambda e: mmg(e, bu, 1))
                        if "noact1" not in DBG:
                          T.op("act", [self.bankb[bg]], [gsb[tb]],
                             lambda e: e.copy(gs[:, 2 + tb * 512: 2 + (tb + 1) * 512], self.bank[bg][:]))
                        if "nots" not in DBG:
                          T.op("dve", [gsb[tb], self.cb], [Ccb],
                             lambda e: e.tensor_scalar(Cc[:], gs[:, 2 + tb * 512: 2 + (tb + 1) * 512], cp(2, j), cp(3, j),
                                                       ALU.mult, ALU.add))
                        rd = [gsb[tb], gsb[max(tb - 1, 0)]]
                        if "notaps" not in DBG:
                          T.op("dve", rd + [Ccb, self.cb], [Ccb],
                             lambda e: e.scalar_tensor_tensor(Cc[:], gs[:, 1 + tb * 512: 1 + (tb + 1) * 512], cp(1, j),
                                                              Cc[:], ALU.mult, ALU.add))
                          T.op("dve", rd + [Ccb, self.cb], [Ccb],
                             lambda e: e.scalar_tensor_tensor(Cc[:], gs[:, tb * 512: (tb + 1) * 512], cp(0, j),
                                                              Cc[:], ALU.mult, ALU.add))
                        if "nosilu" not in DBG:
                          T.op("act", [Ccb], [Ssb], lambda e: e.activation(Ss[:], Cc[:], AF.Silu))
                        if "nomul" not in DBG:
                          T.op("dve", [Ssb, self.bankb[bu]], [Ab[jj][tb]],
                             lambda e: e.tensor_tensor(A[:, jj, ts], Ss[:], self.bank[bu][:], ALU.mult))
                    self.W.done()
                for o in range(8):
                    wt, wb = self.W.get("dn%d_%d_%d" % (l, g, o))
                    for tb in range(NTB):
                        if "nodn" in DBG or "nogu" in DBG:
                            break
                        ts = slice(tb * 512, (tb + 1) * 512)
                        bd = DB[dn % 2]
                        dn += 1

                        def mmd(e):
                            ins = None
                            for jj in range(11):
                                ins = e.matmul(self.bank[bd][:], wt[:, jj, :], A[:, jj, ts],
                                               start=(jj == 0), stop=(jj == 10))
                            return ins
                        T.op("pe", [wb] + [Ab[jj][tb] for jj in range(11)], [self.bankb[bd]], mmd)
                        T.op("dve", [self.bankb[bd], self.Xb[o]], [self.Xb[o]],
                             lambda e: e.tensor_tensor(self.X[:, o, ts], self.X[:, o, ts], self.bank[bd][:], ALU.add))
                    self.W.done()
                    if g == 1 and x_final is not None:
                        x_final(o)
            self.barrier()

    def mm_acc(self, out_ap, pairs, reads, wbuf):
        def f(e):
            ins = None
            n = len(pairs)
            for i, (lt, rh) in enumerate(pairs):
                ins = e.matmul(out_ap, lt, rh, start=(i == 0), stop=(i == n - 1))
            return ins
        return self.T.op("pe", reads, [wbuf], f)

    @staticmethod
    def pipe(n, A, B, la, delay=2):
        pend = []
        for i in range(n + la):
            if i < n:
                A(i)
            if i >= la:
                fin = B(i - la)
                while pend and pend[0][0] <= i:
                    pend.pop(0)[1]()
                if fin is not None:
                    pend.append((i + delay, fin))
        for _, f in pend:
            f()

    def job_head(self, pairs, reads, rows, dsum, gain_ap, out_ap, out_buf, dil=1):
        T = self.T
        pv = (lambda a: a) if dil == 1 else (lambda a: a.rearrange("p (i r) -> p i r", r=dil))

        def post(b, r, rb):
            T.op("dve", [self.bankb[b], rb, self.cb], [out_buf],
                 lambda e: e.scalar_tensor_tensor(out_ap, pv(self.bank[b][0:rows, :]), gain_ap, pv(r[0:rows, :]),
                                                  ALU.mult, ALU.mult))
        return {"mm": lambda b: self.mm_acc(self.bank[b][0:rows, :], pairs, reads, self.bankb[b]),
                "sq_rows": dsum, "rrows": rows, "nd": dsum,
                "sums": lambda sq: [(self.ones[0:dsum, 0:rows], sq[0:dsum, :])], "post": post}

    def norm_pipeline(self, jobs, pbanks, sumb, hn):
        T = self.T
        n, nb = len(jobs), len(pbanks)

        def A(i):
            j, b = jobs[i], pbanks[i % nb]
            j["mm"](b)
            sq, sqb = hn["sqh"][i % 3], hn["sqhb"][i % 3]
            sr = j["sq_rows"]
            T.op("act", [self.bankb[b]], [sqb], lambda e: e.activation(sq[0:sr, :], self.bank[b][0:sr, :], AF.Square))

        def B(i):
            j, b = jobs[i], pbanks[i % nb]
            sq, sqb = hn["sqh"][i % 3], hn["sqhb"][i % 3]
            r, rb = hn["r"][i % 2], hn["rb"][i % 2]
            rr = j["rrows"]
            self.mm_acc(self.bank[sumb][0:rr, :], j["sums"](sq), [sqb, self.cb] + j.get("sum_reads", []), self.bankb[sumb])
            T.op("act", [self.bankb[sumb], self.cb], [rb],
                 lambda e: e.activation(r[0:rr, :], self.bank[sumb][0:rr, :], AF.Ln, bias=self.epsc[0:rr, 0:1], scale=1.0 / j["nd"]))
            T.op("act", [rb], [rb], lambda e: e.activation(r[0:rr, :], r[0:rr, :], AF.Exp, scale=-0.5))
            j["post"](b, r, rb)
        self.pipe(n, A, B, 2 if nb >= 4 else 1)

    def recip(self, out_ap, in_ap, reads, buf, bias=None):
        T = self.T
        if bias is None:
            T.op("act", reads, [buf], lambda e: e.activation(out_ap, in_ap, AF.Ln))
        else:
            T.op("act", reads, [buf], lambda e: e.activation(out_ap, in_ap, AF.Ln, bias=bias, scale=1.0))
        T.op("act", [buf], [buf], lambda e: e.activation(out_ap, out_ap, AF.Exp, scale=-1.0))

    def alloc_hn(self, es):
        kb = self.kb
        return {"sqh": [kb.sb("sqh%d" % i, [128, 512], BF16, es) for i in range(3)], "sqhb": [Buf("sqh%d" % i) for i in range(3)],
                "r": [kb.sb("hr%d" % i, [128, 512], F32, es) for i in range(2)], "rb": [Buf("hr%d" % i) for i in range(2)]}

    def out_proj(self, tagfmt, AO, AOb, nK):
        T = self.T
        it = 0
        for q in range(4):
            wt, wb = self.W.get(tagfmt % q)
            for i in range(2):
                o = 2 * q + i
                for tb in range(NTB):
                    ts = slice(tb * 512, (tb + 1) * 512)
                    bd = it % 2
                    it += 1
                    self.mm_acc(self.bank[bd][:], [(wt[:, i, a, :], AO[:, a, ts]) for a in range(nK)],
                                [wb] + [AOb[a][tb] for a in range(nK)], self.bankb[bd])
                    T.op("dve", [self.bankb[bd], self.Xb[o]], [self.Xb[o]],
                         lambda e: e.tensor_tensor(self.X[:, o, ts], self.X[:, o, ts], self.bank[bd][:], ALU.add))
            self.W.done()

    def _plan_mixer(self, l):
        getattr(self, "_plan_mix%d" % l)()

    def _mixer(self, l):
        self._norm(l * 8)
        getattr(self, "_mix%d" % l)()


    def _plan_mix0(self):
        W = self.W
        for hq in range(2):
            for g in range(3):
                W.add("a_v%d_%d" % (hq, g), self.d_av[hq, g], (8, 256))
            for hd in range(hq * 4, hq * 4 + 4):
                for g in range(3):
                    W.add("a_qk%d_%d" % (hd, g), self.d_aqk[hd, g], (2, 8, 64))
        for q in range(4):
            W.add("a_o%d" % q, self.d_ao[2 * q:2 * q + 2].rearrange("t p k -> p t k"), (2, 4, 128))

    def _mix0(self):
        T = self.T
        kb = self.kb
        HG = self.hg
        DIL = (1, 4, 16)
        with ExitStack() as es:
            sb = lambda n, shp, dt: kb.sb(n, shp, dt, es)
            AO = sb("AO", [128, 4, SEQ], BF16)
            AOb = [[Buf("AO%d_%d" % (a, tb)) for tb in range(NTB)] for a in range(4)]
            VA = [sb("VA%d" % i, [128, 16, 384], BF16) for i in range(3)]
            VAb = [[Buf("VA%d_%d" % (i, q)) for q in range(8)] for i in range(3)]
            QT = [sb("QT%d" % i, [128, SEQ], BF16) for i in range(2)]
            QTb = [Buf("QT%d" % i) for i in range(2)]
            KT = [sb("KT%d" % i, [128, SEQ], BF16) for i in range(2)]
            KTb = [Buf("KT%d" % i) for i in range(2)]
            G3 = [sb("G0_%d" % i, [128, 256], F32) for i in range(2)]
            G3b = [Buf("G0_%d" % i) for i in range(2)]
            Sb_ = [sb("Sb%d" % i, [128, 256], F32) for i in range(3)]
            Sbb = [Buf("Sb%d" % i) for i in range(3)]
            PT = [sb("PT%d" % i, [128, 256], BF16) for i in range(3)]
            PTb = [Buf("PT%d" % i) for i in range(3)]
            R = [sb("R%d" % i, [128, 512], F32) for i in range(2)]
            Rb = [Buf("R%d" % i) for i in range(2)]
            hn = self.alloc_hn(es)
            PB, PBS, SUMB, STB, ACB = 0, (0, 2, 3), 1, (2, 3, 0, 1), (4, 5, 6, 7)
            for i in range(2):
                T.op("dve", [], [QTb[i]], lambda e: e.memset(QT[i][64:128, :], 0.0))
                T.op("dve", [], [KTb[i]], lambda e: e.memset(KT[i][64:128, :], 0.0))
            for i in range(3):
                T.op("dve", [], VAb[i], lambda e: e.memset(
                    VA[i][:].rearrange("p k (pr x) -> p k pr x", pr=2)[:, :, :, 64:128], 1.0))
            si = 0
            gi = 0
            for hq in range(2):
                for g in range(3):
                    dil = DIL[g]
                    L = SEQ // dil
                    wv, wvb = self.W.get("a_v%d_%d" % (hq, g))
                    for k2 in range(8):
                        def mmv(e):
                            ins = None
                            for i in range(2):
                                kc = 2 * k2 + i
                                n0 = kc * 128
                                r, i0 = n0 // L, n0 % L
                                t0 = i0 * dil + r
                                for c in range(8):
                                    ins = e.matmul(self.bank[PB][:, i * 256:(i + 1) * 256],
                                                   self.h[:, c, t0:min(t0 + 128 * dil, SEQ):dil], wv[:, c, :],
                                                   start=(c == 0), stop=(c == 7))
                            return ins
                        T.op("pe", [wvb] + self.hb, [self.bankb[PB]], mmv)
                        bv = self.bank[PB][:].rearrange("p (i h d) -> p i h d", i=2, h=4)
                        vv = VA[g][:, 2 * k2:2 * k2 + 2, :].rearrange("p i (pr x) -> p i pr x", pr=2)
                        for par in range(2):
                            T.op("act", [self.bankb[PB]], [VAb[g][k2]],
                                 lambda e: e.copy(vv[:, :, :, par * 128:par * 128 + 64], bv[:, :, par:4:2, :]))
                    self.W.done()
                for hl in range(4):
                    hd = hq * 4 + hl
                    urow = slice(0, 64) if hd % 2 == 0 else slice(64, 128)
                    drow = slice(64, 128) if hd % 2 == 0 else slice(0, 64)
                    for b in ACB:
                        T.op("pe", [self.cb], [self.bankb[b]],
                             lambda e: e.matmul(self.bank[b][:], self.zeros[:, 0:128], self.zeros[:], start=True, stop=False))
                    for g in range(3):
                        dil = DIL[g]
                        L = SEQ // dil
                        wqk, wqkb = self.W.get("a_qk%d_%d" % (hd, g))
                        qt, qtb = QT[gi % 2], QTb[gi % 2]
                        kt, ktb = KT[gi % 2], KTb[gi % 2]
                        g3, g3b = G3[gi % 2], G3b[gi % 2]
                        T.dma("sp", g3[:], self.d_g0[g, hd], [], [g3b], self.g3k[gi % 2])
                        gi += 1
                        jobs = []
                        for which, dst, dstb, gcol in ((0, qt, qtb, HG_A_Q + g), (1, kt, ktb, HG_A_K + g)):
                            dv = dst[0:64, :].rearrange("p (r i) -> p i r", r=dil) if dil > 1 else None
                            for tb in range(NTB):
                                ts = slice(tb * 512, (tb + 1) * 512)
                                oap = dst[0:64, ts] if dil == 1 else dv[:, tb * 512 // dil:(tb + 1) * 512 // dil, :]
                                jobs.append(self.job_head([(wqk[:, which, c, :], self.h[:, c, ts]) for c in range(8)],
                                                          [wqkb, self.hb[tb]], 64, 64, HG[0:64, gcol:gcol + 1], oap, dstb, dil=dil))
                        self.norm_pipeline(jobs, PBS, SUMB, hn)
                        self.W.done()
                        cpc = L // 128
                        base = si
                        si += 16

                        def A(kc):
                            n0 = kc * 128
                            ci = kc % cpc
                            N = 256 if ci < cpc - 1 else 128
                            bs = STB[(base + kc) % len(STB)]
                            self.mm_acc(self.bank[bs][:, 0:N], [(kt[:, n0:n0 + 128], qt[:, n0:n0 + N])],
                                        [ktb, qtb], self.bankb[bs])

                        def B(kc):
                            r, ci = kc // cpc, kc % cpc
                            N = 256 if ci < cpc - 1 else 128
                            bs = STB[(base + kc) % len(STB)]
                            sb_i = (base + kc) % 3
                            T.op("dve", [self.bankb[bs], g3b], [Sbb[sb_i]],
                                 lambda e: e.scalar_tensor_tensor(Sb_[sb_i][:, 0:N], self.bank[bs][:, 0:N], 0.125, g3[:, 0:N],
                                                                  ALU.mult, ALU.add))
                            T.op("act", [Sbb[sb_i]], [PTb[sb_i]],
                                 lambda e: e.activation(PT[sb_i][:, 0:N], Sb_[sb_i][:, 0:N], AF.Exp))
                            lhs = VA[g][:, kc, (hl // 2) * 192 + (hl % 2) * 64:(hl // 2) * 192 + (hl % 2) * 64 + 128]
                            last = (g == 2 and kc == 15)
                            for half in range(N // 128):
                                ib = ci + half
                                if dil == 1:
                                    dsts = [(ACB[ib // 4], slice((ib % 4) * 128, (ib % 4) * 128 + 128), slice(half * 128, half * 128 + 128))]
                                elif dil == 4:
                                    dsts = [(ACB[ib], slice(r, 512, 4), slice(half * 128, half * 128 + 128))]
                                else:
                                    dsts = [(ACB[b4], slice(r, 512, 16), slice(b4 * 32, b4 * 32 + 32)) for b4 in range(4)]
                                for (bk, osl, psl) in dsts:
                                    T.op("pe", [VAb[g][kc // 2], PTb[sb_i]], [self.bankb[bk]],
                                         lambda e: e.matmul(self.bank[bk][:, osl], lhs, PT[sb_i][:, psl], start=False, stop=last))
                        self.pipe(16, A, B, 3)
                    for m in range(4):
                        ts = slice(m * 512, (m + 1) * 512)
                        bk = ACB[m]
                        r_, rb = R[m % 2], Rb[m % 2]
                        self.recip(r_[drow, :], self.bank[bk][drow, :], [self.bankb[bk]], rb)
                        T.op("dve", [self.bankb[bk], rb], [AOb[hd // 2][m]],
                             lambda e: e.tensor_tensor(AO[urow, hd // 2, ts], self.bank[bk][urow, :], r_[drow, :], ALU.mult))
            self.out_proj("a_o%d", AO, AOb, 4)
            self.barrier()


    def _plan_mix1(self):
        W = self.W
        for i in range(3):
            W.add("b_in%d" % i, self.d_bin[i], (2048,))
        W.add("b_kup", self.d_bk, (16, 2, 64))
        W.add("b_vup", self.d_bv, (8, 2, 128))
        for g in range(4):
            W.add("b_qup%d" % g, self.d_bq[g], (4, 3, 128))
        for q in range(4):
            W.add("b_o%d" % q, self.d_bo[2 * q:2 * q + 2].rearrange("t p k -> p t k"), (2, 8, 128))

    def causal_attention(self, nheads_iter, QKfn, VAfn, scale, Kd, AO, AOb, bias_fn, banks, scr):
        raise NotImplementedError

    def _mix1(self):
        T = self.T
        kb = self.kb
        HG = self.hg
        SC = 96 ** -0.5
        with ExitStack() as es:
            sb = lambda n, shp, dt: kb.sb(n, shp, dt, es)
            AO = self.h
            AOb = [[Buf("AO%d_%d" % (a, tb)) for tb in range(NTB)] for a in range(8)]
            CQN = sb("CQN", [128, 3, SEQ], BF16)
            CQNb = [Buf("CQN%d" % tb) for tb in range(NTB)]
            CKVN = sb("CKVN", [128, 2, SEQ], BF16)
            CKVNb = [Buf("CKVN%d" % tb) for tb in range(NTB)]
            KR64 = sb("KR64", [128, SEQ], BF16)
            KR64b = [Buf("KR64_%d" % tb) for tb in range(NTB)]
            SQPE = sb("SQPE", [128, SEQ], BF16)
            SQPEb = [Buf("SQPE%d" % tb) for tb in range(NTB)]
            ROPE = sb("ROPE", [128, SEQ], F32)
            ROPEb = Buf("ROPE")
            VA = sb("VA", [128, 16, 192], BF16)
            VAb = [Buf("VA_%d" % q) for q in range(4)]
            QT = sb("QT", [128, SEQ], BF16)
            QTb = [Buf("QT%d" % tb) for tb in range(NTB)]
            KT = sb("KT", [128, SEQ], BF16)
            KTb = [Buf("KT%d" % tb) for tb in range(NTB)]
            QH2 = [sb("QH%d" % i, [128, 512], F32) for i in range(2)]
            QH2b = [Buf("QH%d" % i) for i in range(2)]
            TM4 = [sb("TM%d" % i, [128, 512], F32) for i in range(4)]
            TM4b = [Buf("TM%d" % i) for i in range(4)]
            QH, QHb = QH2[0], QH2b[0]
            TM, TMb = TM4[0:2], TM4b[0:2]
            SQ3 = sb("SQ3", [128, 3, 512], BF16)
            SQ3b = Buf("SQ3")
            TRI = sb("TRI", [128, 128], F32)
            TRIb = Buf("TRI")
            Sb_ = [sb("Sb%d" % i, [128, 128], F32) for i in range(3)]
            Sbb = [Buf("Sb%d" % i) for i in range(3)]
            PT = [sb("PT%d" % i, [128, 512], BF16) for i in range(3)]
            PTb = [Buf("PT%d" % i) for i in range(3)]
            R = [sb("R0", [128, 512], F32)] * 2
            Rb = [Buf("R0")] * 2
            hn = self.alloc_hn(es)
            T.dma("sp", ROPE[:], self.d_rope, [], [ROPEb], self.xk["rope"])
            T.dma("sp", TRI[:], self.d_tri, [], [TRIb], self.xk["tri"])
            T.op("dve", [], VAb, lambda e: e.memset(VA[:, :, 64:128], 1.0))
            T.op("dve", [], QTb, lambda e: e.memset(QT[96:128, :], 0.0))
            T.op("dve", [], KTb, lambda e: e.memset(KT[96:128, :], 0.0))
            w0, w0b = self.W.get("b_in0")
            w1, w1b = self.W.get("b_in1")
            w2, w2b = self.W.get("b_in2")
            wtile = [fview(w0[:, 0:1024], (8, 128)), fview(w0[:, 1024:2048], (8, 128)), fview(w1[:, 0:1024], (8, 128)),
                     fview(w1[:, 1024:2048], (8, 128)), fview(w2[:, 0:1024], (8, 128))]
            wpe = fview(w2[:, 1024:1536], (8, 64))
            wbufs = [w0b, w1b, w2b]
            sqt = lambda rr, j: SQ3[:, j, :] if rr == 0 else hn["sqh"][j][:]
            sqbuf = lambda rr, j: SQ3b if rr == 0 else hn["sqhb"][j]
            for tb in range(NTB):
                ts = slice(tb * 512, (tb + 1) * 512)
                for (tiles, dst, dstb, gc, nd, b0, sbk, rr) in (((0, 1, 2), CQN, CQNb, HG_B_QA, 384, 0, 3, 0),
                                                               ((3, 4), CKVN, CKVNb, HG_B_KVA, 256, 4, 6, 1)):
                    nt = len(tiles)
                    for i_, ti in enumerate(tiles):
                        i = b0 + i_
                        self.mm_acc(self.bank[i][:], [(wtile[ti][:, c, :], self.h[:, c, ts]) for c in range(8)],
                                    wbufs + [self.hb[tb]], self.bankb[i])
                        T.op("act", [self.bankb[i]], [sqbuf(rr, i_)],
                             lambda e: e.activation(sqt(rr, i_), self.bank[i][:], AF.Square), relaxed=(i_ > 0 and rr == 0))
                    self.mm_acc(self.bank[sbk][:], [(self.ones[:], sqt(rr, j_)) for j_ in range(nt)],
                                [sqbuf(rr, j_) for j_ in range(nt)] + [self.cb], self.bankb[sbk])
                    T.op("act", [self.bankb[sbk], self.cb], [hn["rb"][rr]],
                         lambda e: e.activation(hn["r"][rr][:], self.bank[sbk][:], AF.Ln, bias=self.epsc[:, 0:1], scale=1.0 / nd))
                    T.op("act", [hn["rb"][rr]], [hn["rb"][rr]],
                         lambda e: e.activation(hn["r"][rr][:], hn["r"][rr][:], AF.Exp, scale=-0.5))
                    for i_ in range(nt):
                        i = b0 + i_
                        T.op("dve", [self.bankb[i], hn["rb"][rr], self.cb], [dstb[tb]],
                             lambda e: e.scalar_tensor_tensor(dst[:, i_, ts], self.bank[i][:], HG[:, gc + i_:gc + i_ + 1], hn["r"][rr][:],
                                                              ALU.mult, ALU.mult), relaxed=(i_ > 0))
                self.mm_acc(self.bank[7][0:64, :], [(wpe[:, c, :], self.h[:, c, ts]) for c in range(8)],
                            wbufs + [self.hb[tb]], self.bankb[7])
                T.op("act", [self.bankb[7]], [SQPEb[tb]], lambda e: e.activation(SQPE[0:32, ts], self.bank[7][0:32, :], AF.Square))
                T.op("dve", [self.bankb[7], self.cb, ROPEb], [TMb[0]],
                     lambda e: e.scalar_tensor_tensor(TM[0][0:64, :], self.bank[7][0:64, :], HG[0:64, HG_B_KPE:HG_B_KPE + 1],
                                                      ROPE[0:64, ts], ALU.mult, ALU.mult))
                T.op("act", [TMb[0]], [TMb[1]], lambda e: e.copy(TM[1][0:32, :], TM[0][32:64, :]))
                T.op("dve", [TMb[0], TMb[1]], [TMb[1]],
                     lambda e: e.tensor_tensor(TM[1][0:32, :], TM[0][0:32, :], TM[1][0:32, :], ALU.add))
                T.op("act", [TMb[1]], [KR64b[tb]], lambda e: e.copy(KR64[64:96, ts], TM[1][0:32, :]))
            self.W.done(3)
            self.barrier()
            wk_r, wkb_r = self.W.get("b_kup")
            wv_r, wvb_r = self.W.get("b_vup")
            WKV = sb("WKV", [128, 4096], BF16)
            wkb = wvb = Buf("WKV")
            T.op("dve", [wkb_r], [wkb], lambda e: e.tensor_copy(WKV[:, 0:2048], wk_r.rearrange("p a b c -> p (a b c)")))
            T.op("dve", [wvb_r], [wkb], lambda e: e.tensor_copy(WKV[:, 2048:4096], wv_r.rearrange("p a b c -> p (a b c)")))
            self.W.done(2)
            wk = fview(WKV[:, 0:2048], (16, 2, 64))
            wv = fview(WKV[:, 2048:4096], (8, 2, 128))
            PB, SUMB, STB, ACB, VB = (0, 1, 3, 4), 2, (3, 4), (5, 6), 7
            STB = (3, 4, 7, 0)
            si = 0
            ai = 0
            for g in range(4):
                wq, wqb = self.W.get("b_qup%d" % g)
                for hh in range(4):
                    hd = 4 * g + hh
                    urow = slice(0, 64) if hd % 2 == 0 else slice(64, 128)
                    drow = slice(64, 128) if hd % 2 == 0 else slice(0, 64)
                    vo = (hd % 2) * 64
                    if hd % 2 == 0:
                        pr = hd // 2
                        for q4 in range(4):
                            def mmv(e):
                                ins = None
                                for i in range(4):
                                    tc = q4 * 4 + i
                                    for c in range(2):
                                        ins = e.matmul(self.bank[VB][:, i * 128:(i + 1) * 128], CKVN[:, c, tc * 128:(tc + 1) * 128],
                                                       wv[:, pr, c, :], start=(c == 0), stop=(c == 1))
                                return ins
                            T.op("pe", [wvb, CKVNb[q4]], [self.bankb[VB]], mmv)
                            bv = self.bank[VB][:].rearrange("p (i n) -> p i n", i=4)
                            for par in range(2):
                                T.op("act", [self.bankb[VB]], [VAb[q4]],
                                     lambda e: e.copy(VA[:, q4 * 4:(q4 + 1) * 4, par * 128:par * 128 + 64], bv[:, :, par * 64:(par + 1) * 64]))
                    jobs = []
                    for tb in range(NTB):
                        ts = slice(tb * 512, (tb + 1) * 512)

                        def kpost(b, r, rb, tb=tb, ts=ts):
                            T.op("dve", [self.bankb[b], rb, self.cb], [KTb[tb]],
                                 lambda e: e.scalar_tensor_tensor(KT[0:64, ts], self.bank[b][0:64, :], HG[0:64, HG_B_K:HG_B_K + 1],
                                                                  r[0:64, :], ALU.mult, ALU.mult))
                            T.op("dve", [KR64b[tb], rb], [KTb[tb]],
                                 lambda e: e.tensor_tensor(KT[64:96, ts], KR64[64:96, ts], r[64:96, :], ALU.mult), relaxed=True)
                        jobs.append({"mm": (lambda b, ts=ts, tb=tb: self.mm_acc(self.bank[b][0:64, :],
                                                                             [(wk[:, hd, c, :], CKVN[:, c, ts]) for c in range(2)],
                                                                             [wkb, CKVNb[tb]], self.bankb[b])),
                                     "sq_rows": 64, "rrows": 96, "nd": 96, "sum_reads": [SQPEb[tb]],
                                     "sums": (lambda sq, ts=ts: [(self.ones[0:64, 0:96], sq[0:64, :]), (self.ones[0:32, 0:96], SQPE[0:32, ts])]),
                                     "post": kpost})

                        def qpost(b, r, rb, tb=tb, ts=ts):
                            QH, QHb = QH2[tb % 2], QH2b[tb % 2]
                            TM, TMb = TM4[2 * (tb % 2):2 * (tb % 2) + 2], TM4b[2 * (tb % 2):2 * (tb % 2) + 2]
                            T.op("dve", [self.bankb[b], rb, self.cb], [QTb[tb]],
                                 lambda e: e.scalar_tensor_tensor(QT[0:64, ts], self.bank[b][0:64, :], HG[0:64, HG_B_Q:HG_B_Q + 1],
                                                                  r[0:64, :], ALU.mult, ALU.mult))
                            T.op("dve", [self.bankb[b], rb, self.cb], [QHb],
                                 lambda e: e.scalar_tensor_tensor(QH[64:128, :], self.bank[b][64:128, :], HG[64:128, HG_B_Q:HG_B_Q + 1],
                                                                  r[64:128, :], ALU.mult, ALU.mult))
                            T.op("dve", [QHb, ROPEb], [TMb[0]],
                                 lambda e: e.tensor_tensor(TM[0][96:128, :], QH[96:128, :], ROPE[96:128, ts], ALU.mult))
                            T.op("act", [TMb[0]], [TMb[1]], lambda e: e.copy(TM[1][64:96, :], TM[0][96:128, :]))
                            T.op("dve", [QHb, ROPEb], [QHb],
                                 lambda e: e.tensor_tensor(QH[64:96, :], QH[64:96, :], ROPE[64:96, ts], ALU.mult))
                            T.op("dve", [QHb, TMb[1]], [QTb[tb]],
                                 lambda e: e.tensor_tensor(QT[64:96, ts], QH[64:96, :], TM[1][64:96, :], ALU.add), relaxed=True)
                        jobs.append({"mm": (lambda b, ts=ts, tb=tb: self.mm_acc(self.bank[b][:],
                                                                             [(wq[:, hh, c, :], CQN[:, c, ts]) for c in range(3)],
                                                                             [wqb, CQNb[tb]], self.bankb[b])),
                                     "sq_rows": 96, "rrows": 128, "nd": 96,
                                     "sums": (lambda sq: [(self.ones[0:96, :], sq[0:96, :])]),
                                     "post": qpost})
                    self.norm_pipeline(jobs, PB, SUMB, hn)
                    items = [(qb, kc) for qb in range(NTB) for kc in range(4 * (qb + 1))]
                    base = si
                    si += len(items)
                    abase = ai
                    ai += NTB

                    def geom(idx):
                        qb, kc = items[idx]
                        di = kc - 4 * qb
                        c0 = 128 * di if di > 0 else 0
                        return qb, kc, di, c0, 512 - c0

                    def A(idx):
                        qb, kc, di, c0, N = geom(idx)
                        bs = STB[(base + idx) % 4]
                        self.mm_acc(self.bank[bs][:, 0:N], [(KT[:, kc * 128:kc * 128 + 128], QT[:, qb * 512 + c0:qb * 512 + 512])],
                                    [KTb[kc // 4], QTb[qb]], self.bankb[bs])

                    def B(idx):
                        qb, kc, di, c0, N = geom(idx)
                        bs = STB[(base + idx) % 4]
                        s_i = (base + idx) % 3
                        ab = ACB[(abase + qb) % 2]
                        nk = 4 * (qb + 1)
                        if di >= 0:
                            T.op("dve", [self.bankb[bs], TRIb], [Sbb[s_i]],
                                 lambda e: e.scalar_tensor_tensor(Sb_[s_i][:], self.bank[bs][:, 0:128], SC, TRI[:], ALU.mult, ALU.add))
                            T.op("act", [Sbb[s_i]], [PTb[s_i]], lambda e: e.activation(PT[s_i][:, 0:128], Sb_[s_i][:], AF.Exp))
                            if N > 128:
                                T.op("act", [self.bankb[bs]], [PTb[s_i]],
                                     lambda e: e.activation(PT[s_i][:, 128:N], self.bank[bs][:, 128:N], AF.Exp, scale=SC), relaxed=True)
                        else:
                            T.op("act", [self.bankb[bs]], [PTb[s_i]],
                                 lambda e: e.activation(PT[s_i][:, 0:N], self.bank[bs][:, 0:N], AF.Exp, scale=SC))
                        T.op("pe", [VAb[kc // 4], PTb[s_i]], [self.bankb[ab]],
                             lambda e: e.matmul(self.bank[ab][:, c0:512], VA[:, kc, vo:vo + 128], PT[s_i][:, 0:N],
                                                start=(kc == 0), stop=(kc == nk - 1)))
                        if kc == nk - 1:
                            def fin():
                                ts = slice(qb * 512, qb * 512 + 512)
                                r_, rb = R[qb % 2], Rb[qb % 2]
                                self.recip(r_[drow, :], self.bank[ab][drow, :], [self.bankb[ab]], rb)
                                T.op("dve", [self.bankb[ab], rb], [AOb[hd // 2][qb]],
                                     lambda e: e.tensor_tensor(AO[urow, hd // 2, ts], self.bank[ab][urow, :], r_[drow, :], ALU.mult))
                            return fin
                    self.pipe(len(items), A, B, 3)
                self.W.done()
            self.out_proj("b_o%d", AO, AOb, 8)
            self.barrier()


    def _plan_mix2(self):
        W = self.W
        for hd in range(8):
            W.add("c_qk%d" % hd, self.d_cqk[hd], (2, 8, 128))
            W.add("c_v%d" % hd, self.d_cv[hd], (8, 128))
        for q in range(4):
            W.add("c_o%d" % q, self.d_co[2 * q:2 * q + 2].rearrange("t p k -> p t k"), (2, 8, 128))

    def _mix2(self):
        T = self.T
        kb = self.kb
        HG = self.hg
        LAM_INIT = 0.8 - 0.6 * math.exp(-0.3 * 2)
        with ExitStack() as es:
            sb = lambda n, shp, dt: kb.sb(n, shp, dt, es)
            AO = sb("AO", [128, 8, SEQ], BF16)
            AOb = [[Buf("AO%d_%d" % (a, tb)) for tb in range(NTB)] for a in range(8)]
            QK = [sb("QK%d" % i, [128, SEQ], BF16) for i in range(3)]
            QKb = [[Buf("QK%d_%d" % (i, tb)) for tb in range(NTB)] for i in range(3)]
            T.op("dve", [], QKb[1], lambda e: e.memset(QK[1][64:128, :], 0.0))
            T.op("dve", [], QKb[2], lambda e: e.memset(QK[2][0:64, :], 0.0))
            VA = sb("VA", [128, 16, 128], BF16)
            VAb = [Buf("VA_%d" % q) for q in range(4)]
            GF = [sb("GF%d" % i, [128, SEQ], F32) for i in range(2)]
            GFb = [Buf("GF%d" % i) for i in range(2)]
            Sb_ = [sb("Sb%d" % i, [128, 512], F32) for i in range(3)]
            Sbb = [Buf("Sb%d" % i) for i in range(3)]
            PT = [sb("PT%d" % i, [128, 512], BF16) for i in range(3)]
            PTb = [Buf("PT%d" % i) for i in range(3)]
            TT = [sb("TT%d" % i, [128, 512], F32) for i in range(2)]
            TTb = [Buf("TT%d" % i) for i in range(2)]
            lamt = sb("lamt", [128, 256], F32)
            lamb = Buf("lamt")
            lams = sb("lams", [128, 8], F32)
            hn = self.alloc_hn(es)
            PB, SUMB, STB = 0, 1, (2, 3)
            OB, DB = (4, 6), (5, 7)
            T.dma("sp", lamt[:], self.d_clam, [], [lamb], self.xk["lam"])
            for i in range(2):
                T.op("dve", [lamb], [lamb],
                     lambda e: e.tensor_tensor(lamt[:, i * 128:i * 128 + 64], lamt[:, i * 128:i * 128 + 64],
                                               lamt[:, i * 128 + 64:i * 128 + 128], ALU.mult))
                T.op("dve", [lamb], [lamb],
                     lambda e: e.reduce_sum(lams[:, i:i + 1], lamt[:, i * 128:i * 128 + 64], mybir.AxisListType.X))
            T.op("act", [lamb], [lamb], lambda e: e.activation(lams[:, 2:4], lams[:, 0:2], AF.Exp))
            T.op("dve", [lamb], [lamb], lambda e: e.tensor_tensor(lams[:, 4:5], lams[:, 3:4], lams[:, 2:3], ALU.subtract))
            T.op("dve", [lamb], [lamb], lambda e: e.tensor_scalar(lams[:, 5:6], lams[:, 4:5], -LAM_INIT, None, ALU.add))
            T.op("dve", [], [lamb], lambda e: e.memset(lams[:, 6:7], math.log(1.0 - LAM_INIT)))
            neglam = lams[:, 5:6]
            PBS, STB = (0, 2, 3), (2, 3, 0, 1)
            si = 0
            for hd in range(8):
                wqk, wqkb = self.W.get("c_qk%d" % hd)
                wv, wvb = self.W.get("c_v%d" % hd)
                for m in range(2):
                    T.dma("sp", GF[m][:], self.d_cg[m * 8 + hd], [], [GFb[m]], self.g3k[m])
                jobs = []
                for tb in range(NTB):
                    ts = slice(tb * 512, (tb + 1) * 512)

                    def post_q(b, r, rb, tb=tb, ts=ts):
                        T.op("dve", [self.bankb[b], rb, self.cb], [QKb[0][tb]],
                             lambda e: e.scalar_tensor_tensor(QK[0][:, ts], self.bank[b][:], HG[:, HG_C_Q:HG_C_Q + 1], r[:], ALU.mult, ALU.mult))

                    def post_k(b, r, rb, tb=tb, ts=ts):
                        T.op("dve", [self.bankb[b], rb, self.cb], [QKb[1][tb]],
                             lambda e: e.scalar_tensor_tensor(QK[1][0:64, ts], self.bank[b][0:64, :], HG[0:64, HG_C_K:HG_C_K + 1],
                                                              r[0:64, :], ALU.mult, ALU.mult))
                        T.op("dve", [self.bankb[b], rb, self.cb], [QKb[2][tb]],
                             lambda e: e.scalar_tensor_tensor(QK[2][64:128, ts], self.bank[b][64:128, :], HG[64:128, HG_C_K:HG_C_K + 1],
                                                              r[64:128, :], ALU.mult, ALU.mult))
                    for i, post in ((0, post_q), (1, post_k)):
                        jobs.append({"mm": (lambda b, i=i, ts=ts, tb=tb: self.mm_acc(self.bank[b][:],
                                                                                  [(wqk[:, i, c, :], self.h[:, c, ts]) for c in range(8)],
                                                                                  [wqkb, self.hb[tb]], self.bankb[b])),
                                     "sq_rows": 128, "rrows": 128, "nd": 64,
                                     "sums": (lambda sq: [(self.bd64[:], sq[:])]), "post": post})
                self.norm_pipeline(jobs, PBS, SUMB, hn)
                for q4 in range(4):
                    b = PBS[q4 % 3]

                    def mmv(e):
                        ins = None
                        for i in range(4):
                            tc = q4 * 4 + i
                            for c in range(8):
                                ins = e.matmul(self.bank[b][:, i * 128:(i + 1) * 128], self.h[:, c, tc * 128:(tc + 1) * 128],
                                               wv[:, c, :], start=(c == 0), stop=(c == 7))
                        return ins
                    T.op("pe", [wvb, self.hb[q4]], [self.bankb[b]], mmv)
                    T.op("act", [self.bankb[b]], [VAb[q4]],
                         lambda e: e.copy(VA[:, q4 * 4:(q4 + 1) * 4, :], self.bank[b][:].rearrange("p (i n) -> p i n", i=4)))
                self.W.done(2)
                items = [(qb, m, kc) for qb in range(NTB) for m in range(2) for kc in range(4 * (qb + 1))]
                base = si
                si += len(items)

                def geom(idx):
                    qb, m, kc = items[idx]
                    di = kc - 4 * qb
                    c0 = 128 * di if di > 0 else 0
                    return qb, m, kc, c0, 512 - c0

                def A(idx):
                    qb, m, kc, c0, N = geom(idx)
                    bs = STB[(base + idx) % 4]
                    self.mm_acc(self.bank[bs][:, 0:N], [(QK[1 + m][:, kc * 128:kc * 128 + 128], QK[0][:, qb * 512 + c0:qb * 512 + 512])],
                                [QKb[1 + m][kc // 4], QKb[0][qb]], self.bankb[bs])

                def B(idx):
                    qb, m, kc, c0, N = geom(idx)
                    bs = STB[(base + idx) % 4]
                    s_i = (base + idx) % 3
                    ob, db = OB[m], DB[m]
                    nk = 4 * (qb + 1)
                    g0 = qb * 512 + c0 - kc * 128
                    T.op("dve", [self.bankb[bs], GFb[m]], [Sbb[s_i]],
                         lambda e: e.scalar_tensor_tensor(Sb_[s_i][:, 0:N], self.bank[bs][:, 0:N], 0.125, GF[m][:, g0:g0 + N],
                                                          ALU.mult, ALU.add))
                    T.op("act", [Sbb[s_i]], [PTb[s_i]], lambda e: e.activation(PT[s_i][:, 0:N], Sb_[s_i][:, 0:N], AF.Exp))
                    T.op("pe", [VAb[kc // 4], PTb[s_i]], [self.bankb[ob]],
                         lambda e: e.matmul(self.bank[ob][:, c0:512], VA[:, kc, :], PT[s_i][:, 0:N], start=(kc == 0), stop=(kc == nk - 1)))
                    T.op("pe", [PTb[s_i], self.cb], [self.bankb[db]],
                         lambda e: e.matmul(self.bank[db][:, c0:512], self.ones[:], PT[s_i][:, 0:N], start=(kc == 0), stop=(kc == nk - 1)))
                    if kc == nk - 1:
                        def fin():
                            ts = slice(qb * 512, qb * 512 + 512)
                            self.recip(TT[m][:], self.bank[db][:], [self.bankb[db]], TTb[m])
                            T.op("dve", [self.bankb[ob], TTb[m]], [TTb[m]],
                                 lambda e: e.tensor_tensor(TT[m][:], self.bank[ob][:], TT[m][:], ALU.mult))
                            if m == 0:
                                return
                            T.op("dve", [TTb[0], TTb[1], lamb], [TTb[0]],
                                 lambda e: e.scalar_tensor_tensor(TT[0][:], TT[1][:], neglam, TT[0][:], ALU.mult, ALU.add))
                            T.op("act", [TTb[0]], [hn["sqhb"][0]], lambda e: e.activation(hn["sqh"][0][:], TT[0][:], AF.Square))
                            FB = DB[1]
                            self.mm_acc(self.bank[FB][:], [(self.ones[:], hn["sqh"][0][:])], [hn["sqhb"][0], self.cb], self.bankb[FB])
                            T.op("act", [self.bankb[FB], self.cb], [hn["rb"][0]],
                                 lambda e: e.activation(hn["r"][0][:], self.bank[FB][:], AF.Ln, bias=self.epsc[:, 0:1], scale=1.0 / 128))
                            T.op("act", [hn["rb"][0], lamb], [hn["rb"][0]],
                                 lambda e: e.activation(hn["r"][0][:], hn["r"][0][:], AF.Exp, bias=lams[:, 6:7], scale=-0.5))
                            T.op("dve", [TTb[0], hn["rb"][0], self.cb], [AOb[hd][qb]],
                                 lambda e: e.scalar_tensor_tensor(AO[:, hd, ts], TT[0][:], HG[:, HG_C_SUB:HG_C_SUB + 1], hn["r"][0][:],
                                                                  ALU.mult, ALU.mult))
                        return fin
                self.pipe(len(items), A, B, 3)
            self.out_proj("c_o%d", AO, AOb, 8)
            self.barrier()

    def _plan_mix3(self):
        W = self.W
        W.add("d_k", self.d_dk, (2, 8, 128))
        W.add("d_v", self.d_dv, (8, 128))
        for g in range(4):
            W.add("d_q%d" % g, self.d_dq[g], (2, 8, 128))
        for q in range(4):
            W.add("d_o%d" % q, self.d_do[2 * q:2 * q + 2].rearrange("t p k -> p t k"), (2, 8, 128))

    def _mix3(self):
        T = self.T
        kb = self.kb
        HG = self.hg
        with ExitStack() as es:
            sb = lambda n, shp, dt: kb.sb(n, shp, dt, es)
            AO = sb("AO", [128, 8, SEQ], BF16)
            AOb = [[Buf("AO%d_%d" % (a, tb)) for tb in range(NTB)] for a in range(8)]
            KT = [[sb("KT%d_%d" % (i, j), [128, SEQ], BF16) for j in range(2)] for i in range(2)]
            KTb = [[[Buf("KT%d_%d_%d" % (i, j, tb)) for tb in range(NTB)] for j in range(2)] for i in range(2)]
            VA = [sb("VA%d" % i, [128, 16, 192], BF16) for i in range(2)]
            VAb = [[Buf("VA%d_%d" % (i, q)) for q in range(4)] for i in range(2)]
            QT = [sb("QT%d" % i, [128, SEQ], BF16) for i in range(2)]
            QTb = [[Buf("QT%d_%d" % (i, tb)) for tb in range(NTB)] for i in range(2)]
            G3 = [sb("G3_%d" % i, [128, 256], F32) for i in range(2)]
            G3b = [Buf("G3_%d" % i) for i in range(2)]
            Sb_ = [sb("Sb%d" % i, [128, 256], F32) for i in range(3)]
            Sbb = [Buf("Sb%d" % i) for i in range(3)]
            PT = [sb("PT%d" % i, [128, 256], BF16) for i in range(3)]
            PTb = [Buf("PT%d" % i) for i in range(3)]
            R = [sb("R0", [128, 512], F32)] * 2
            Rb = [Buf("R0")] * 2
            es_t = sb("esink", [128, 16], F32)
            esb = Buf("esink")
            hn = self.alloc_hn(es)
            PB, SUMB, STB, ACB = (0, 1, 3, 4), 2, (3, 4, 7, 0), (5, 6)
            T.op("act", [self.cb], [esb], lambda e: e.activation(es_t[:], HG[:, HG_D_SINK:HG_D_SINK + 16], AF.Exp))
            for i in range(2):
                T.op("dve", [], VAb[i], lambda e: e.memset(VA[i][:, :, 64:128], 1.0))
                T.op("dve", [], KTb[i][0], lambda e: e.memset(KT[i][0][64:128, :], 0.0))
                T.op("dve", [], KTb[i][1], lambda e: e.memset(KT[i][1][0:64, :], 0.0))
            wk, wkb = self.W.get("d_k")
            wv, wvb = self.W.get("d_v")
            jobs = []
            for kvh in range(2):
                for tb in range(NTB):
                    ts = slice(tb * 512, (tb + 1) * 512)

                    def post_k(b, r, rb, kvh=kvh, tb=tb, ts=ts):
                        T.op("dve", [self.bankb[b], rb, self.cb], [KTb[kvh][0][tb]],
                             lambda e: e.scalar_tensor_tensor(KT[kvh][0][0:64, ts], self.bank[b][0:64, :], HG[0:64, HG_D_K:HG_D_K + 1],
                                                              r[0:64, :], ALU.mult, ALU.mult))
                        T.op("dve", [self.bankb[b], rb, self.cb], [KTb[kvh][1][tb]],
                             lambda e: e.scalar_tensor_tensor(KT[kvh][1][64:128, ts], self.bank[b][64:128, :], HG[64:128, HG_D_K:HG_D_K + 1],
                                                              r[64:128, :], ALU.mult, ALU.mult))
                    jobs.append({"mm": (lambda b, kvh=kvh, ts=ts, tb=tb: self.mm_acc(self.bank[b][:],
                                                                                      [(wk[:, kvh, c, :], self.h[:, c, ts]) for c in range(8)],
                                                                                      [wkb, self.hb[tb]], self.bankb[b])),
                                 "sq_rows": 128, "rrows": 128, "nd": 64, "sums": (lambda sq: [(self.bd64[:], sq[:])]), "post": post_k})
            self.norm_pipeline(jobs, PB, SUMB, hn)
            for q4 in range(4):
                b = PB[q4 % 2]

                def mmv(e):
                    ins = None
                    for i in range(4):
                        tc = q4 * 4 + i
                        for c in range(8):
                            ins = e.matmul(self.bank[b][:, i * 128:(i + 1) * 128], self.h[:, c, tc * 128:(tc + 1) * 128],
                                           wv[:, c, :], start=(c == 0), stop=(c == 7))
                    return ins
                T.op("pe", [wvb, self.hb[q4]], [self.bankb[b]], mmv)
                bv = self.bank[b][:].rearrange("p (i n) -> p i n", i=4)
                for kvh in range(2):
                    for off in (0, 128):
                        T.op("act", [self.bankb[b]], [VAb[kvh][q4]],
                             lambda e: e.copy(VA[kvh][:, q4 * 4:(q4 + 1) * 4, off:off + 64], bv[:, :, kvh * 64:(kvh + 1) * 64]),
                             relaxed=(off > 0))
            self.W.done(2)
            cnt = [0]
            for g in range(4):
                wq, wqb = self.W.get("d_q%d" % g)
                for pr in range(2):
                    pair = 2 * g + pr
                    qt, qtb = QT[pair % 2], QTb[pair % 2]
                    jobs = []
                    for tb in range(NTB):
                        ts = slice(tb * 512, (tb + 1) * 512)

                        def post_q(b, r, rb, tb=tb, ts=ts):
                            T.op("dve", [self.bankb[b], rb, self.cb], [qtb[tb]],
                                 lambda e: e.scalar_tensor_tensor(qt[:, ts], self.bank[b][:], HG[:, HG_D_Q:HG_D_Q + 1], r[:],
                                                                  ALU.mult, ALU.mult))
                        jobs.append({"mm": (lambda b, ts=ts, tb=tb: self.mm_acc(self.bank[b][:],
                                                                             [(wq[:, pr, c, :], self.h[:, c, ts]) for c in range(8)],
                                                                             [wqb, self.hb[tb]], self.bankb[b])),
                                     "sq_rows": 128, "rrows": 128, "nd": 64, "sums": (lambda sq: [(self.bd64[:], sq[:])]), "post": post_q})
                    self.norm_pipeline(jobs, PB, SUMB, hn)
                    for h2 in range(2):
                        self._mix3_head(2 * pair + h2, qt, qtb, KT, KTb, VA, VAb, G3, G3b, Sb_, Sbb, PT, PTb, R, Rb, es_t, esb,
                                        AO, AOb, STB, ACB, cnt)
                self.W.done()
            self.out_proj("d_o%d", AO, AOb, 8)
            self.barrier()

    def _mix3_head(self, hd, qt, qtb, KT, KTb, VA, VAb, G3, G3b, Sb_, Sbb, PT, PTb, R, Rb, es_t, esb, AO, AOb, STB, ACB, cnt):
        T = self.T
        kvh = hd // 8
        par = hd % 2
        kt, ktb = KT[kvh][par], KTb[kvh][par]
        vo = par * 64
        urow = slice(0, 64) if par == 0 else slice(64, 128)
        drow = slice(64, 128) if par == 0 else slice(0, 64)
        g3, g3b = G3[hd % 2], G3b[hd % 2]
        T.dma("sp", g3[:], self.d_g3[hd], [], [g3b], self.g3k[hd % 2])
        base = cnt[0]
        cnt[0] += 16

        def A(kc):
            k0 = kc * 128
            N = 256 if kc < 15 else 128
            bs = STB[(base + kc) % 4]
            qbufs = [qtb[k0 // 512]] + ([qtb[(k0 + 128) // 512]] if kc < 15 else [])
            self.mm_acc(self.bank[bs][:, 0:N], [(kt[:, k0:k0 + 128], qt[:, k0:k0 + N])], [ktb[k0 // 512]] + qbufs, self.bankb[bs])

        def B(kc):
            N = 256 if kc < 15 else 128
            bs = STB[(base + kc) % 4]
            s_i = (base + kc) % 3
            T.op("dve", [self.bankb[bs], g3b], [Sbb[s_i]],
                 lambda e: e.scalar_tensor_tensor(Sb_[s_i][:, 0:N], self.bank[bs][:, 0:N], 0.125, g3[:, 0:N], ALU.mult, ALU.add))
            T.op("act", [Sbb[s_i]], [PTb[s_i]], lambda e: e.activation(PT[s_i][:, 0:N], Sb_[s_i][:, 0:N], AF.Exp))
            ab0 = ACB[(kc // 4) % 2]
            c0 = (kc % 4) * 128
            T.op("pe", [VAb[kvh][kc // 4], PTb[s_i]], [self.bankb[ab0]],
                 lambda e: e.matmul(self.bank[ab0][:, c0:c0 + 128], VA[kvh][:, kc, vo:vo + 128], PT[s_i][:, 0:128],
                                    start=(kc == 0), stop=True))
            if kc < 15:
                ab1 = ACB[((kc + 1) // 4) % 2]
                c1 = ((kc + 1) % 4) * 128
                T.op("pe", [VAb[kvh][kc // 4], PTb[s_i]], [self.bankb[ab1]],
                     lambda e: e.matmul(self.bank[ab1][:, c1:c1 + 128], VA[kvh][:, kc, vo:vo + 128], PT[s_i][:, 128:256],
                                        start=True, stop=False))
            if kc % 4 == 3:
                def fin():
                    m = kc // 4
                    ts = slice(m * 512, (m + 1) * 512)
                    r, rb = R[m % 2], Rb[m % 2]
                    self.recip(r[drow, :], self.bank[ab0][drow, :], [self.bankb[ab0], esb], rb, bias=es_t[drow, hd:hd + 1])
                    T.op("dve", [self.bankb[ab0], rb], [AOb[hd // 2][m]],
                         lambda e: e.tensor_tensor(AO[urow, hd // 2, ts], self.bank[ab0][urow, :], r[drow, :], ALU.mult))
                return fin
        self.pipe(16, A, B, 3)


def host_weights(inp):
    f32 = np.float32
    out = {}
    wup = inp["f_w_up"].reshape(4, 8, 128, 2, NJ, 128)
    out["wup"] = np.ascontiguousarray(wup.transpose(0, 4, 2, 1, 3, 5)).reshape(4, NJ, 128, 2048)
    wdn = inp["f_w_down"].reshape(4, 2, 11, 128, 8, 128)
    out["wdn"] = np.ascontiguousarray(wdn.transpose(0, 1, 4, 3, 2, 5)).reshape(4, 2, 8, 128, 11 * 128)
    gains = np.zeros((128, 64), f32)
    gains[:, 0:32] = inp["norm_mix"].reshape(4, 8, 128).transpose(2, 0, 1).reshape(128, 32)
    gains[:, 32:64] = inp["norm_ffn"].reshape(4, 8, 128).transpose(2, 0, 1).reshape(128, 32)
    out["gains"] = gains
    cw = inp["f_conv_w"].reshape(4, 3, NJ, 128)
    cbv = inp["f_conv_b"].reshape(4, 1, NJ, 128)
    convp = np.concatenate([cw, cbv], axis=1)
    out["convp"] = np.ascontiguousarray(convp.transpose(3, 0, 1, 2)).reshape(128, 4 * 4 * NJ)
    hg = np.zeros((128, NHG), f32)
    tab = inp["rel_bias_table"]
    dw = inp["d_w_in"][0]
    rep2 = lambda v: np.concatenate([v, v])
    hg[:, HG_D_Q] = rep2(inp["d_q_norm"][0])
    hg[:, HG_D_K] = rep2(inp["d_k_norm"][0])
    hg[:, HG_D_SINK:HG_D_SINK + 16] = inp["d_sinks"][0][None, :]
    wk = dw[:, 1024:1152].reshape(8, 128, 2, 64).transpose(1, 2, 0, 3)
    out["d_k"] = np.ascontiguousarray(np.concatenate([wk, wk], axis=3)).reshape(128, 2048)
    out["d_v"] = np.ascontiguousarray(dw[:, 1152:1280].reshape(8, 128, 128).transpose(1, 0, 2)).reshape(128, 1024)
    wq = dw[:, 0:1024].reshape(8, 128, 4, 2, 128).transpose(2, 1, 3, 0, 4)
    out["d_q"] = np.ascontiguousarray(wq).reshape(4, 128, 2048)
    out["d_o"] = tile_fm(inp["d_w_out"][0])
    jj = np.arange(256)[None, :] - np.arange(128)[:, None]
    valid = (jj >= 0) & (jj <= 127)
    bk = t5_bucket_np(np.maximum(jj, 0))
    g3 = np.where(valid[None], tab[bk].transpose(2, 0, 1), f32(NEG))
    out["d_g3"] = np.ascontiguousarray(g3.astype(f32))
    aw = inp["a_w_in"][0].reshape(8, 128, 3, 3, 8, 64)
    for g in range(3):
        hg[:, HG_A_Q + g] = rep2(inp["a_q_norm"][0][g])
        hg[:, HG_A_K + g] = rep2(inp["a_k_norm"][0][g])
    av = aw[:, :, :, 2].reshape(8, 128, 3, 2, 4, 64)
    out["a_v"] = np.ascontiguousarray(av.transpose(3, 2, 1, 0, 4, 5)).reshape(2, 3, 128, 2048)
    aqk = aw[:, :, :, 0:2]
    out["a_qk"] = np.ascontiguousarray(aqk.transpose(4, 2, 1, 3, 0, 5)).reshape(8, 3, 128, 1024)
    out["a_o"] = tile_fm(inp["a_w_out"][0])
    valid0 = (jj >= 0) & (jj <= 128)
    g0 = np.zeros((3, 8, 128, 256), f32)
    for g, dil in enumerate((1, 4, 16)):
        bk0 = t5_bucket_np(np.maximum(jj, 0) * dil)
        g0[g] = np.where(valid0[None], tab[:, 0:8][bk0].transpose(2, 0, 1), f32(NEG))
    out["a_g0"] = g0
    bw = inp["b_w_in"][0]
    part = np.concatenate([np.arange(16, 32), np.arange(0, 16)])
    hg[:, HG_B_QA:HG_B_QA + 3] = inp["b_q_a_norm"][0].reshape(3, 128).T
    hg[:, HG_B_KVA:HG_B_KVA + 2] = inp["b_kv_a_norm"][0].reshape(2, 128).T
    qn, kn = inp["b_q_norm"][0], inp["b_k_norm"][0]
    hg[:, HG_B_Q] = np.concatenate([qn, qn[64 + part]])
    hg[0:64, HG_B_K] = kn[0:64]
    hg[0:64, HG_B_KPE] = np.concatenate([kn[64:96], kn[64 + part]])
    tl = lambda w: w.reshape(8, 128, -1).transpose(1, 0, 2).reshape(128, -1)
    tiles = [tl(bw[:, i * 128:(i + 1) * 128]) for i in range(5)]
    pe_cols = np.concatenate([640 + np.arange(32), 640 + part])
    b_in = np.zeros((3, 128, 2048), f32)
    b_in[0, :, 0:1024], b_in[0, :, 1024:2048] = tiles[0], tiles[1]
    b_in[1, :, 0:1024], b_in[1, :, 1024:2048] = tiles[2], tiles[3]
    b_in[2, :, 0:1024], b_in[2, :, 1024:1536] = tiles[4], tl(bw[:, pe_cols])
    out["b_in"] = b_in
    kvu = inp["b_w_kv_up"][0].reshape(2, 128, 16, 2, 64)
    out["b_kup"] = np.ascontiguousarray(kvu[:, :, :, 0].transpose(1, 2, 0, 3)).reshape(128, 2048)
    vv = kvu[:, :, :, 1].reshape(2, 128, 8, 128)
    out["b_vup"] = np.ascontiguousarray(vv.transpose(1, 2, 0, 3)).reshape(128, 2048)
    qu = inp["b_w_q_up"][0].reshape(3, 128, 16, 96)
    qu = np.concatenate([qu, qu[:, :, :, 64 + part]], axis=3)
    out["b_qup"] = np.ascontiguousarray(qu.reshape(3, 128, 4, 4, 128).transpose(2, 1, 3, 0, 4)).reshape(4, 128, 1536)
    out["b_o"] = tile_fm(inp["b_w_out"][0])
    inv_freq = (np.float32(10000.0) ** (-np.arange(0, 32, 2, dtype=f32) / np.float32(32))).astype(f32)
    ang = (np.arange(SEQ, dtype=f32)[:, None] * inv_freq[None, :]).astype(f32)
    cos = np.cos(ang).astype(f32).T
    sin = np.sin(ang).astype(f32).T
    cos32 = np.concatenate([cos, cos], axis=0)
    sins32 = np.concatenate([-sin, sin], axis=0)
    out["b_rope"] = np.ascontiguousarray(np.concatenate([cos32, sins32, cos32, sins32], axis=0))
    pp = np.arange(128)
    out["b_tri"] = np.where(pp[None, :] >= pp[:, None], f32(0), f32(NEG)).astype(f32)
    cw = inp["c_w_in"][0]
    hg[:, HG_C_Q] = rep2(inp["c_q_norm"][0])
    hg[:, HG_C_K] = rep2(inp["c_k_norm"][0])
    hg[:, HG_C_SUB] = inp["c_subln"][0]
    cq = cw[:, 0:1024].reshape(8, 128, 8, 2, 64)
    ck = cw[:, 1024:2048].reshape(8, 128, 8, 2, 64)
    cqk = np.stack([cq.reshape(8, 128, 8, 128), ck.reshape(8, 128, 8, 128)], axis=3)
    out["c_qk"] = np.ascontiguousarray(cqk.transpose(2, 1, 3, 0, 4)).reshape(8, 128, 2048)
    cv = cw[:, 2048:3072].reshape(8, 128, 8, 128)
    out["c_v"] = np.ascontiguousarray(cv.transpose(2, 1, 0, 3)).reshape(8, 128, 1024)
    out["c_o"] = tile_fm(inp["c_w_out"][0])
    dd = np.arange(SEQ)[None, :] - np.arange(128)[:, None]
    bkd = t5_bucket_np(np.maximum(dd, 0))
    out["c_g"] = np.ascontiguousarray(np.where((dd >= 0)[None], tab[bkd].transpose(2, 0, 1), f32(NEG)).astype(f32))
    lam4 = np.concatenate([inp["c_lambda_q1"][0], inp["c_lambda_k1"][0], inp["c_lambda_q2"][0], inp["c_lambda_k2"][0]])
    out["c_lam"] = np.ascontiguousarray(np.broadcast_to(lam4[None, :], (128, 256))).astype(f32)
    out["hgains"] = hg
    return out


def tile_fm(w):
    K, N = w.shape
    return np.ascontiguousarray(w.reshape(K // 128, 128, N // 128, 128).transpose(2, 1, 0, 3)).reshape(N // 128, 128, K)


def t5_bucket_np(dist):
    d = np.maximum(dist, 1).astype(np.float32)
    large = 16 + (np.log(d / np.float32(16)) / np.float32(np.log(2048 / 16)) * np.float32(16)).astype(np.int32)
    return np.where(dist < 16, dist, np.minimum(large, 31)).astype(np.int64)


def host_x(x):
    b = x.shape[0]
    return np.ascontiguousarray(x.reshape(b, SEQ, 8, 128).transpose(0, 2, 3, 1))


def host_y(yT):
    b = yT.shape[0]
    return np.ascontiguousarray(yT.transpose(0, 3, 1, 2)).reshape(b, SEQ, DM)


def kernel(**inputs):
    x = np.asarray(inputs["x"], np.float32)
    nb = x.shape[0]
    per = nb // NCORES
    prog = Prog(per)
    nc = prog.build()
    wts = host_weights({k: np.asarray(v) for k, v in inputs.items()})
    xT = host_x(x)
    in_maps = []
    for c in range(NCORES):
        m = dict(wts)
        m["xT"] = xT[c * per:(c + 1) * per]
        in_maps.append(m)
    res = run_bass_kernel_spmd(nc, in_maps, core_ids=list(range(NCORES)))
    yT = np.concatenate([r["yT"] for r in res.results], axis=0)
    return host_y(yT)
```
